# Optimizing a Trainium2 kernel written in Bass

```python
import math
import jax, jax.numpy as jnp
from jax import lax
import numpy as np

D_MODEL = 1024
BATCH = 8
SEQ = 4096
DEPTH = 4
DEC_BATCH = 16
DEC_SEQ = 32
PAST_LEN = 2048

CHUNK = 64
GDN_HEAD_DIM = 128
GDN_WIDTH = D_MODEL // 2
GDN_HEADS = GDN_WIDTH // GDN_HEAD_DIM
GDN_CONV = 4
QKV_COLS = 3 * GDN_WIDTH
S5_WIDTH = D_MODEL // 4
S5_GROUP = 16
S5_GROUPS = S5_WIDTH // S5_GROUP
S5_STATE = 64
CC_WIDTH = D_MODEL - GDN_WIDTH - S5_WIDTH
CC_KERNEL = 31
FFN_HIDDEN = -(-8 * D_MODEL // (3 * 256)) * 256
PLE_DIM = 256
OFF_Z = QKV_COLS
OFF_BA = OFF_Z + GDN_WIDTH
OFF_S5 = OFF_BA + 2 * GDN_HEADS
OFF_CC = OFF_S5 + S5_WIDTH
IN_COLS = OFF_CC + 2 * CC_WIDTH

kernel_name = 'hybrid_streaming_gdn_s5_conformer_step'

F32 = jnp.float32


def rmsnorm(x, g, eps=1e-6):
    xf = x.astype(F32)
    y = xf * lax.rsqrt(jnp.mean(xf * xf, axis=-1, keepdims=True) + eps)
    return (y * g.astype(F32)).astype(x.dtype)


def layernorm(x, g, b, eps=1e-5):
    xf = x.astype(F32)
    mu = jnp.mean(xf, axis=-1, keepdims=True)
    var = jnp.mean(jnp.square(xf - mu), axis=-1, keepdims=True)
    return ((xf - mu) * lax.rsqrt(var + eps) * g.astype(F32) + b.astype(F32)).astype(x.dtype)


def l2norm(x, eps=1e-6):
    return x * lax.rsqrt(jnp.sum(x * x, axis=-1, keepdims=True) + eps)


def causal_dwconv(x, buf, w):
    width, ch = w.shape
    xp = jnp.concatenate([buf.astype(x.dtype), x], axis=1)
    y = lax.conv_general_dilated(xp, w[:, None, :].astype(x.dtype), window_strides=(1,), padding='VALID',
                                 dimension_numbers=('NWC', 'WIO', 'NWC'), feature_group_count=ch)
    return y, xp[:, xp.shape[1] - (width - 1):]


def gated_delta_rule(q, k, v, g, beta, s0):
    bsz, t, h, dk = q.shape
    dv = v.shape[-1]
    c = CHUNK if t % CHUNK == 0 else t
    n = t // c

    def chunks(a):
        return jnp.swapaxes(a.reshape((bsz, n, c) + a.shape[2:]), 2, 3)

    q, k, v, g, beta = chunks(q), chunks(k), chunks(v), chunks(g), chunks(beta)
    gc = jnp.cumsum(g, axis=-1)
    idx = jnp.arange(c)
    causal = idx[:, None] >= idx[None, :]
    strict = idx[:, None] > idx[None, :]
    decay = jnp.exp(jnp.where(causal, gc[..., :, None] - gc[..., None, :], -jnp.inf))
    kb = k * beta[..., None]
    kk = jnp.einsum('bnhck,bnhsk->bnhcs', kb, k) * decay
    a_mat = jnp.eye(c, dtype=F32) + jnp.where(strict, kk, 0.0)
    rhs = jnp.concatenate([v * beta[..., None], kb * jnp.exp(gc)[..., None]], axis=-1)
    sol = lax.linalg.triangular_solve(a_mat, rhs, left_side=True, lower=True, unit_diagonal=True)
    u, w = sol[..., :dv], sol[..., dv:]
    aqk = jnp.einsum('bnhck,bnhsk->bnhcs', q, k) * decay
    g_last = gc[..., -1]
    qg = q * jnp.exp(gc)[..., None]
    kd = k * jnp.exp(g_last[..., None] - gc)[..., None]

    def step(s, xs):
        qg_c, kd_c, u_c, w_c, aqk_c, dl_c = xs
        v_new = u_c - jnp.einsum('bhck,bhkv->bhcv', w_c, s)
        o = jnp.einsum('bhck,bhkv->bhcv', qg_c, s) + jnp.einsum('bhcs,bhsv->bhcv', aqk_c, v_new)
        s = s * dl_c[..., None, None] + jnp.einsum('bhck,bhcv->bhkv', kd_c, v_new)
        return s, o

    xs = tuple(jnp.moveaxis(a, 1, 0) for a in (qg, kd, u, w, aqk, jnp.exp(g_last)))
    s_final, o = lax.scan(step, s0, xs)
    o = jnp.swapaxes(jnp.moveaxis(o, 0, 1), 2, 3).reshape(bsz, t, h, dv)
    return o, s_final


def gdn_mixer(qkv_pre, z, ba, buf, s0, conv_w, a_log, dt_bias, norm_w):
    bsz, t, _ = qkv_pre.shape
    qkv, new_buf = causal_dwconv(qkv_pre, buf, conv_w)
    qkv = jax.nn.silu(qkv.astype(F32))
    q, k, v = jnp.split(qkv, 3, axis=-1)
    q = l2norm(q.reshape(bsz, t, GDN_HEADS, GDN_HEAD_DIM)) * (GDN_HEAD_DIM ** -0.5)
    k = l2norm(k.reshape(bsz, t, GDN_HEADS, GDN_HEAD_DIM))
    v = v.reshape(bsz, t, GDN_HEADS, GDN_HEAD_DIM)
    b_raw, a_raw = jnp.split(ba.astype(F32), 2, axis=-1)
    beta = jax.nn.sigmoid(b_raw)
    g = -jnp.exp(a_log.astype(F32)) * jax.nn.softplus(a_raw + dt_bias.astype(F32))
    o, s_new = gated_delta_rule(q, k, v, g, beta, s0.astype(F32))
    o = o * lax.rsqrt(jnp.mean(o * o, axis=-1, keepdims=True) + 1e-6) * norm_w.astype(F32)
    o = o * jax.nn.silu(z.astype(F32).reshape(bsz, t, GDN_HEADS, GDN_HEAD_DIM))
    return o.reshape(bsz, t, GDN_WIDTH).astype(qkv_pre.dtype), s_new, new_buf


def s5_mixer(u, h0, lam_re, lam_im, log_dt, b_re, b_im, c_re, c_im, d, glu_w, glu_b):
    bsz, t, _ = u.shape
    uf = u.astype(F32).reshape(bsz, t, S5_GROUPS, S5_GROUP)
    lam = lax.complex(lam_re.astype(F32), lam_im.astype(F32))
    dt = jnp.exp(log_dt.astype(F32))[:, None]
    lam_bar = jnp.exp(lam * dt)
    b_bar = ((lam_bar - 1.0) / lam)[..., None] * lax.complex(b_re.astype(F32), b_im.astype(F32))
    cmat = lax.complex(c_re.astype(F32), c_im.astype(F32))
    bu = jnp.einsum('gpc,btgc->btgp', b_bar, uf.astype(jnp.complex64))
    h0c = lax.complex(h0[..., 0].astype(F32), h0[..., 1].astype(F32))
    bu = bu.at[:, 0].add(lam_bar * h0c)
    a = jnp.broadcast_to(lam_bar, bu.shape)

    def combine(e1, e2):
        a1, b1 = e1
        a2, b2 = e2
        return a2 * a1, a2 * b1 + b2

    _, xs = lax.associative_scan(combine, (a, bu), axis=1)
    y = jnp.einsum('gcp,btgp->btgc', cmat, xs).real + d.astype(F32).reshape(S5_GROUPS, S5_GROUP) * uf
    h_last = xs[:, -1]
    h_new = jnp.stack([h_last.real, h_last.imag], axis=-1)
    y = jax.nn.gelu(y.reshape(bsz, t, S5_WIDTH)).astype(u.dtype)
    y = y * jax.nn.sigmoid(y @ glu_w + glu_b)
    return y, h_new


def conformer_conv(c_in, buf, dw_w, dw_b, ln_g, ln_b):
    a, gate = jnp.split(c_in, 2, axis=-1)
    x = a * jax.nn.sigmoid(gate)
    xc, new_buf = causal_dwconv(x, buf, dw_w)
    xc = xc + dw_b
    return jax.nn.silu(layernorm(xc, ln_g, ln_b)), new_buf


def trunk(x, p, st_gdn, st_gconv, st_s5, st_conv, W):
    h = x
    n_gdn, n_gconv, n_s5, n_conv = [], [], [], []
    for i in range(DEPTH):
        hn = rmsnorm(h, W['norm_mix'][i])
        proj = hn @ W['w_in'][i]
        qkv_pre, z, ba, u5, cc = jnp.split(proj, [OFF_Z, OFF_BA, OFF_S5, OFF_CC], axis=-1)
        ya, s_gdn, b_gconv = gdn_mixer(qkv_pre, z, ba, st_gconv[i], st_gdn[i], W['gdn_conv_w'][i],
                                       W['gdn_a_log'][i], W['gdn_dt_bias'][i], W['gdn_norm'][i])
        yb, s_s5 = s5_mixer(u5, st_s5[i], W['s5_lam_re'][i], W['s5_lam_im'][i], W['s5_log_dt'][i],
                            W['s5_b_re'][i], W['s5_b_im'][i], W['s5_c_re'][i], W['s5_c_im'][i],
                            W['s5_d'][i], W['s5_glu_w'][i], W['s5_glu_b'][i])
        yc, b_conv = conformer_conv(cc, st_conv[i], W['cc_dw_w'][i], W['cc_dw_b'][i],
                                    W['cc_ln_g'][i], W['cc_ln_b'][i])
        mix = jnp.concatenate([ya.astype(h.dtype), yb.astype(h.dtype), yc.astype(h.dtype)], axis=-1)
        h = h + mix @ W['w_out'][i]
        hf = rmsnorm(h, W['norm_ffn'][i])
        h = h + (jax.nn.silu(hf @ W['ffn_w1'][i]) * (hf @ W['ffn_w3'][i])) @ W['ffn_w2'][i]
        h = h + (p[i] @ W['pe_w'][i]) * jax.nn.sigmoid(h @ W['pe_gate_w'][i])
        n_gdn.append(s_gdn)
        n_gconv.append(b_gconv)
        n_s5.append(s_s5)
        n_conv.append(b_conv)
    y = rmsnorm(h, W['norm_final'])
    return y, jnp.stack(n_gdn), jnp.stack(n_gconv), jnp.stack(n_s5), jnp.stack(n_conv)


def setup_inputs(seed: int = 0) -> dict:
    key = jax.random.key(seed)
    ks = iter(jax.random.split(key, 48))

    def nrm(shape, s=1.0):
        return s * jax.random.normal(next(ks), shape, F32)

    def unif(shape, lo, hi):
        return jax.random.uniform(next(ks), shape, F32, lo, hi)

    L = DEPTH
    x_prompt = nrm((BATCH, SEQ, D_MODEL))
    x_sample = nrm((DEC_BATCH, DEC_SEQ, D_MODEL))
    p_prompt = nrm((L, BATCH, SEQ, PLE_DIM))
    p_sample = nrm((L, DEC_BATCH, DEC_SEQ, PLE_DIM))
    state_gdn = nrm((L, DEC_BATCH, GDN_HEADS, GDN_HEAD_DIM, GDN_HEAD_DIM), 0.1)
    state_gdn_conv = nrm((L, DEC_BATCH, GDN_CONV - 1, QKV_COLS))
    state_s5 = nrm((L, DEC_BATCH, S5_GROUPS, S5_STATE, 2), 0.5)
    state_conv = nrm((L, DEC_BATCH, CC_KERNEL - 1, CC_WIDTH))
    norm_mix = 1.0 + nrm((L, D_MODEL), 0.02)
    w_in = nrm((L, D_MODEL, IN_COLS), D_MODEL ** -0.5)
    gdn_conv_w = nrm((L, GDN_CONV, QKV_COLS), GDN_CONV ** -0.5)
    gdn_a_log = jnp.log(unif((L, GDN_HEADS), 1.0, 16.0))
    dt = jnp.exp(unif((L, GDN_HEADS), math.log(1e-3), math.log(1e-1)))
    gdn_dt_bias = dt + jnp.log(-jnp.expm1(-dt))
    gdn_norm = 1.0 + nrm((L, GDN_HEAD_DIM), 0.02)
    n_idx = jnp.arange(S5_STATE, dtype=F32)
    s5_lam_re = -0.5 + nrm((L, S5_GROUPS, S5_STATE), 0.01)
    s5_lam_im = jnp.pi * n_idx + nrm((L, S5_GROUPS, S5_STATE), 0.01)
    s5_log_dt = unif((L, S5_GROUPS), math.log(1e-3), math.log(1e-1))
    s5_b_re = nrm((L, S5_GROUPS, S5_STATE, S5_GROUP), (2 * S5_GROUP) ** -0.5)
    s5_b_im = nrm((L, S5_GROUPS, S5_STATE, S5_GROUP), (2 * S5_GROUP) ** -0.5)
    s5_c_re = nrm((L, S5_GROUPS, S5_GROUP, S5_STATE), S5_STATE ** -0.5)
    s5_c_im = nrm((L, S5_GROUPS, S5_GROUP, S5_STATE), S5_STATE ** -0.5)
    s5_d = nrm((L, S5_WIDTH))
    s5_glu_w = nrm((L, S5_WIDTH, S5_WIDTH), S5_WIDTH ** -0.5)
    s5_glu_b = nrm((L, S5_WIDTH), 0.01)
    cc_dw_w = nrm((L, CC_KERNEL, CC_WIDTH), CC_KERNEL ** -0.5)
    cc_dw_b = nrm((L, CC_WIDTH), 0.01)
    cc_ln_g = 1.0 + nrm((L, CC_WIDTH), 0.02)
    cc_ln_b = nrm((L, CC_WIDTH), 0.01)
    w_out = nrm((L, D_MODEL, D_MODEL), D_MODEL ** -0.5)
    norm_ffn = 1.0 + nrm((L, D_MODEL), 0.02)
    ffn_w1 = nrm((L, D_MODEL, FFN_HIDDEN), D_MODEL ** -0.5)
    ffn_w3 = nrm((L, D_MODEL, FFN_HIDDEN), D_MODEL ** -0.5)
    ffn_w2 = nrm((L, FFN_HIDDEN, D_MODEL), FFN_HIDDEN ** -0.5)
    pe_w = nrm((L, PLE_DIM, D_MODEL), PLE_DIM ** -0.5)
    pe_gate_w = nrm((L, D_MODEL, D_MODEL), D_MODEL ** -0.5)
    norm_final = 1.0 + nrm((D_MODEL,), 0.02)
    return {'x_prompt': x_prompt, 'x_sample': x_sample, 'p_prompt': p_prompt, 'p_sample': p_sample,
            'state_gdn': state_gdn, 'state_gdn_conv': state_gdn_conv, 'state_s5': state_s5, 'state_conv': state_conv,
            'norm_mix': norm_mix, 'w_in': w_in, 'gdn_conv_w': gdn_conv_w, 'gdn_a_log': gdn_a_log,
            'gdn_dt_bias': gdn_dt_bias, 'gdn_norm': gdn_norm, 's5_lam_re': s5_lam_re, 's5_lam_im': s5_lam_im,
            's5_log_dt': s5_log_dt, 's5_b_re': s5_b_re, 's5_b_im': s5_b_im, 's5_c_re': s5_c_re, 's5_c_im': s5_c_im,
            's5_d': s5_d, 's5_glu_w': s5_glu_w, 's5_glu_b': s5_glu_b, 'cc_dw_w': cc_dw_w, 'cc_dw_b': cc_dw_b,
            'cc_ln_g': cc_ln_g, 'cc_ln_b': cc_ln_b, 'w_out': w_out, 'norm_ffn': norm_ffn, 'ffn_w1': ffn_w1,
            'ffn_w3': ffn_w3, 'ffn_w2': ffn_w2, 'pe_w': pe_w, 'pe_gate_w': pe_gate_w, 'norm_final': norm_final}


def reference(x_prompt, x_sample, p_prompt, p_sample, state_gdn, state_gdn_conv, state_s5, state_conv,
              norm_mix, w_in, gdn_conv_w, gdn_a_log, gdn_dt_bias, gdn_norm, s5_lam_re, s5_lam_im, s5_log_dt,
              s5_b_re, s5_b_im, s5_c_re, s5_c_im, s5_d, s5_glu_w, s5_glu_b, cc_dw_w, cc_dw_b, cc_ln_g, cc_ln_b,
              w_out, norm_ffn, ffn_w1, ffn_w3, ffn_w2, pe_w, pe_gate_w, norm_final):
    W = dict(norm_mix=norm_mix, w_in=w_in, gdn_conv_w=gdn_conv_w, gdn_a_log=gdn_a_log, gdn_dt_bias=gdn_dt_bias,
             gdn_norm=gdn_norm, s5_lam_re=s5_lam_re, s5_lam_im=s5_lam_im, s5_log_dt=s5_log_dt, s5_b_re=s5_b_re,
             s5_b_im=s5_b_im, s5_c_re=s5_c_re, s5_c_im=s5_c_im, s5_d=s5_d, s5_glu_w=s5_glu_w, s5_glu_b=s5_glu_b,
             cc_dw_w=cc_dw_w, cc_dw_b=cc_dw_b, cc_ln_g=cc_ln_g, cc_ln_b=cc_ln_b, w_out=w_out, norm_ffn=norm_ffn,
             ffn_w1=ffn_w1, ffn_w3=ffn_w3, ffn_w2=ffn_w2, pe_w=pe_w, pe_gate_w=pe_gate_w, norm_final=norm_final)
    bp = x_prompt.shape[0]
    z_gdn = jnp.zeros((DEPTH, bp, GDN_HEADS, GDN_HEAD_DIM, GDN_HEAD_DIM), F32)
    z_gconv = jnp.zeros((DEPTH, bp, GDN_CONV - 1, QKV_COLS), x_prompt.dtype)
    z_s5 = jnp.zeros((DEPTH, bp, S5_GROUPS, S5_STATE, 2), F32)
    z_conv = jnp.zeros((DEPTH, bp, CC_KERNEL - 1, CC_WIDTH), x_prompt.dtype)
    y_prompt, gdn_p, gconv_p, s5_p, conv_p = trunk(x_prompt, p_prompt, z_gdn, z_gconv, z_s5, z_conv, W)
    y_sample, gdn_s, gconv_s, s5_s, conv_s = trunk(x_sample, p_sample, state_gdn, state_gdn_conv,
                                                   state_s5, state_conv, W)
    return (y_prompt, y_sample, gdn_p, gconv_p, s5_p, conv_p, gdn_s, gconv_s, s5_s, conv_s)
```

```python
import math
import numpy as np
import ml_dtypes
import concourse.bass as bass
import concourse.mybir as mybir
from concourse.bass_utils import run_bass_kernel_spmd

F32 = mybir.dt.float32
BF16 = mybir.dt.bfloat16
I32 = mybir.dt.int32
AF = mybir.ActivationFunctionType
ALU = mybir.AluOpType

D = 1024
GDN_F32 = True
L_FULL = 4
SEQ = 4096
DSEQ = 32
HID = 2816
NCHUNK = 30
CHW = 4096
TWO_PI = 2.0 * math.pi

C_ID, C_MU, C_SU, C_IOTA, C_ONES, C_SEL = 0, 128, 256, 384, 896, 1024
NCONST = 1024 + 1024


def _esize(dt):
    return 2 if dt == BF16 else 4


class Sched:
    ENG = ("pe", "act", "dve", "pool", "sp")

    def __init__(self, nc, esems, dsems):
        self.nc = nc
        self.sem = {}
        self.cur = {}
        for n, s in zip(self.ENG, esems):
            self.sem[n] = s
            self.cur[n] = 0
        self.dq = {"sp": [], "pool": []}
        half = len(dsems) // 2
        for i, s in enumerate(dsems):
            k = "d%d" % i
            self.sem[k] = s
            self.cur[k] = 0
            self.dq["sp"].append(k)
        self.dnext = {"sp": 0, "pool": 0}
        self.clock = {n: {} for n in self.ENG}
        self.prog = {n: [] for n in self.ENG}
        self.blocks = {}
        self.tokvc = {}
        self.nops = 0

    def _blocks(self, ap):
        sp = str(ap.space)
        if "DRAM" in sp:
            return []
        a = ap.ap
        pstep = a[0][0]
        es = _esize(ap.dtype)
        off = int(ap.offset)
        col = off % pstep if pstep > 0 else off
        ext = 1
        for st, cnt in a[1:]:
            ext += (cnt - 1) * abs(st)
        lo = col * es
        hi = lo + ext * es
        key = "P" if "PSUM" in sp else "S"
        g = 2048 if key == "P" else 256
        return [(key, b) for b in range(lo // g, (hi - 1) // g + 1)]

    def _deps(self, eng, reads, writes):
        need = {}
        clk = self.clock[eng]

        def add(tok):
            if tok is None:
                return
            s, v = tok
            if eng == "pe" and s == "pe":
                return
            if clk.get(s, 0) >= v:
                return
            if need.get(s, 0) < v:
                need[s] = v

        rb = set()
        wb = set()
        for ap in reads:
            rb.update(self._blocks(ap))
        for ap in writes:
            wb.update(self._blocks(ap))
        for b in rb:
            st = self.blocks.get(b)
            if st is not None:
                add(st[0])
                if b[0] == "P":
                    for s, v in st[1].items():
                        if s != eng:
                            add((s, v))
        for b in wb:
            st = self.blocks.get(b)
            if st is not None:
                add(st[0])
                for s, v in st[1].items():
                    add((s, v))
        return need, rb, wb

    def _emit_waits(self, eng, need):
        clk = self.clock[eng]
        for s, v in need.items():
            if clk.get(s, 0) >= v:
                continue
            sem = self.sem[s]
            self.prog[eng].append(("w", sem, v, s))
            vc = self.tokvc.get((s, v))
            if vc is not None:
                for k2, v2 in vc.items():
                    if clk.get(k2, 0) < v2:
                        clk[k2] = v2
            if clk.get(s, 0) < v:
                clk[s] = v

    def _commit(self, tok, eng, rb, wb):
        vc = dict(self.clock[eng])
        vc[tok[0]] = tok[1]
        self.tokvc[tok] = vc
        for b in rb:
            st = self.blocks.get(b)
            if st is None:
                st = [None, {}]
                self.blocks[b] = st
            st[1][tok[0]] = tok[1]
        for b in wb:
            self.blocks[b] = [tok, {}]
        self.nops += 1
        if len(self.tokvc) > 60000:
            keys = list(self.tokvc.keys())
            for k in keys[:30000]:
                del self.tokvc[k]

    def op(self, eng, fn, reads, writes):
        need, rb, wb = self._deps(eng, reads, writes)
        self._emit_waits(eng, need)
        self.cur[eng] += 1
        tok = (eng, self.cur[eng])
        self.prog[eng].append(("i", fn, self.sem[eng], 1))
        self._commit(tok, eng, rb, wb)
        return tok

    def dma(self, q, out, in_, extra_tokens=()):
        ring = self.dq[q]
        k = ring[self.dnext[q] % len(ring)]
        self.dnext[q] += 1
        need, rb, wb = self._deps(q, [in_], [out])
        clk = self.clock[q]
        prev = self.cur[k]
        if prev > 0 and clk.get(k, 0) < prev:
            need[k] = max(need.get(k, 0), prev)
        for s, v in extra_tokens:
            if clk.get(s, 0) < v:
                need[s] = max(need.get(s, 0), v)
        self._emit_waits(q, need)
        self.cur[k] += 16
        tok = (k, self.cur[k])
        self.prog[q].append(("i", lambda e, o=out, i=in_: e.dma_start(out=o, in_=i), self.sem[k], 16))
        self._commit(tok, q, rb, wb)
        return tok

    def barrier(self):
        for eng in self.ENG:
            need = {}
            for s, v in self.cur.items():
                if s == eng and eng == "pe":
                    continue
                if v > 0 and self.clock[eng].get(s, 0) < v:
                    need[s] = v
            self._emit_waits(eng, need)

    def finish(self):
        for eng in self.ENG:
            need = {}
            for s, v in self.cur.items():
                if s == eng:
                    continue
                if v > 0 and self.clock[eng].get(s, 0) < v:
                    need[s] = v
            self._emit_waits(eng, need)

    def finalize(self):
        waited = {n: set() for n in self.ENG}
        for eng in self.ENG:
            for it in self.prog[eng]:
                if it[0] == "w" and it[3] in waited:
                    waited[it[3]].add(it[2])
        self.tickmap = {}
        for n in self.ENG:
            self.tickmap[n] = {v: i + 1 for i, v in enumerate(sorted(waited[n]))}
        self.nsig = {n: len(waited[n]) for n in self.ENG}

    def replay(self, eng, e):
        seq = 0
        tm = self.tickmap
        for it in self.prog[eng]:
            if it[0] == "w":
                k = it[3]
                if k in tm:
                    e.wait_ge(it[1], tm[k][it[2]])
                else:
                    e.wait_ge(it[1], it[2])
            else:
                ins = it[1](e)
                if it[3] == 16:
                    ins.then_inc(it[2], 16)
                else:
                    seq += 1
                    if seq in tm[eng]:
                        ins.then_inc(it[2], 1)


class _Stop(Exception):
    pass


import os
KSTOP = int(os.environ.get("KSTOP", "0"))


def ck(n):
    if KSTOP == n:
        raise _Stop()


class Alloc:
    def __init__(self, big, words):
        self.big = big
        self.words = words
        self.off = 0

    def get(self, dtype, shape):
        n = 1
        for s in shape[1:]:
            n *= s
        w = n if dtype != BF16 else (n + 1) // 2
        w = (w + 63) // 64 * 64
        o = self.off
        self.off += w
        assert self.off <= self.words, "SBUF overflow %d > %d" % (self.off, self.words)
        ap = self.big[:, o:o + w]
        if dtype == BF16:
            ap = ap.bitcast(BF16)
        elif dtype == I32:
            ap = ap.bitcast(I32)
        ap = ap[:, 0:n]
        if len(shape) == 3:
            ap = ap.rearrange("p (a b) -> p a b", a=shape[1])
        elif len(shape) == 4:
            ap = ap.rearrange("p (a b c) -> p a b c", a=shape[1], b=shape[2])
        if shape[0] < 128:
            ap = ap[0:shape[0]]
        return ap


def bc(ap, shape):
    return ap.to_broadcast(list(shape))


def build_program(NPT, L, with_sample=True):
    nc = bass.Bass("TRN2", target_bir_lowering=False)
    NTOK = NPT * 512 + 64

    def din(name, shape, dt=F32):
        return nc.dram_tensor(name, list(shape), dt, kind="ExternalInput").ap()

    def dout(name, shape, dt=F32):
        return nc.dram_tensor(name, list(shape), dt, kind="ExternalOutput").ap()

    xT = din("xT", [128, 8, NTOK])
    pT = din("pT", [L, 128, 2, NTOK])
    wch = din("wch", [L, NCHUNK, 128, CHW])
    consts = din("consts", [128, NCONST])
    vecs = din("vecs", [128, L, 40])
    gfin = din("gfin", [128, 8])
    cwg = din("cwg", [128, L, 12, 4])
    cwc = din("cwc", [128, L, 2, 31])
    ba8 = din("ba8", [8, L, 2])
    s5col = din("s5col", [128, L, 8, 3])
    s5row = din("s5row", [128, L, 3, 8, 128])
    s5bt = din("s5bt", [L, 2, 128, 8, 128])
    s5ct = din("s5ct", [L, 2, 128, 8, 128])
    st_gdn = din("st_gdn", [L, 2, 128, 4, 128])
    st_gc = din("st_gc", [L, 2, 128, 12, 3])
    st_s5 = din("st_s5", [L, 2, 128, 8, 2])
    st_cc = din("st_cc", [L, 2, 128, 2, 30])

    yT = dout("yT", [128, 8, NTOK])
    o_gdn = dout("o_gdn", [L, 3, 128, 4, 128])
    o_gc = dout("o_gc", [L, 3, 128, 12, 3])
    o_s5 = dout("o_s5", [L, 3, 128, 8, 2])
    o_cc = dout("o_cc", [L, 3, 128, 2, 30])

    wscr = nc.dram_tensor("wscr", [L, NCHUNK, 128, CHW], BF16, kind="Internal").ap()
    s5tab = nc.dram_tensor("s5tab", [L, 8, 128, 2, 512], F32, kind="Internal").ap()
    s5mscr = nc.dram_tensor("s5mscr", [L, 128, 4, 8, 128], BF16, kind="Internal").ap()

    SBW = 53000
    NDS = 6
    import contextlib
    with contextlib.ExitStack() as es:
        big = es.enter_context(nc.sbuf_tensor("big", [128, SBW], F32))
        psum = es.enter_context(nc.psum_tensor("psum", [128, 8, 512], F32))
        esems = [es.enter_context(nc.semaphore("e_%s" % n)) for n in Sched.ENG]
        dsems = [es.enter_context(nc.semaphore("dm_%d" % i)) for i in range(NDS)]
        block = es.enter_context(nc.Block())
        S = Sched(nc, esems, dsems)
        A = Alloc(big, SBW)

        def mm(out, lhsT, rhs, start=True, stop=True):
            S.op("pe", lambda e: e.matmul(out, lhsT=lhsT, rhs=rhs, start=start, stop=stop),
                 [lhsT, rhs], [out])

        def tr(out, in_, ident):
            S.op("pe", lambda e: e.transpose(out=out, in_=in_, identity=ident), [in_, ident], [out])

        def act(out, in_, func, bias=None, scale=None):
            kw = {}
            r = [in_]
            if bias is not None:
                kw["bias"] = bias
                if not isinstance(bias, float):
                    r.append(bias)
            if scale is not None:
                kw["scale"] = scale
            S.op("act", lambda e: e.activation(out=out, in_=in_, func=func, **kw), r, [out])

        def tt(eng, out, in0, in1, op):
            S.op(eng, lambda e: e.tensor_tensor(out=out, in0=in0, in1=in1, op=op), [in0, in1], [out])

        def ts(eng, out, in0, s1, op0, s2=None, op1=None):
            r = [in0]
            if not isinstance(s1, float):
                r.append(s1)
            if s2 is not None and not isinstance(s2, float):
                r.append(s2)
            if op1 is None:
                S.op(eng, lambda e: e.tensor_scalar(out=out, in0=in0, scalar1=s1, scalar2=None, op0=op0), r, [out])
            else:
                S.op(eng, lambda e: e.tensor_scalar(out=out, in0=in0, scalar1=s1, scalar2=s2, op0=op0, op1=op1), r, [out])

        def stt(eng, out, in0, scalar, in1, op0, op1):
            r = [in0, in1]
            if not isinstance(scalar, float):
                r.append(scalar)
            S.op(eng, lambda e: e.scalar_tensor_tensor(out=out, in0=in0, scalar=scalar, in1=in1, op0=op0, op1=op1), r, [out])

        def cp(eng, out, in_):
            if eng == "act":
                act(out, in_, AF.Copy)
            else:
                S.op(eng, lambda e: e.tensor_copy(out=out, in_=in_), [in_], [out])

        def memset(eng, out, val):
            S.op(eng, lambda e: e.memset(out, val), [], [out])

        def recip(out, in_):
            S.op("dve", lambda e: e.reciprocal(out=out, in_=in_), [in_], [out])

        def scan(out, d0, d1, init):
            r = [d0, d1]
            if not isinstance(init, float):
                r.append(init)
            S.op("dve", lambda e: e.tensor_tensor_scan(out=out, data0=d0, data1=d1, initial=init,
                                                       op0=ALU.mult, op1=ALU.add), r, [out])

        pcnt = [0]
        pinned = set()

        def pbank(pin=False):
            while True:
                b = pcnt[0] % 8
                pcnt[0] += 1
                if b not in pinned:
                    break
            if pin:
                pinned.add(b)
            return b

        def PS(b, shape=None, dt=F32):
            ap = psum[:, b, :]
            if dt == BF16:
                ap = ap.bitcast(BF16)
            if shape is None:
                return ap
            n = 1
            for s in shape[1:]:
                n *= s
            ap = ap[:, 0:n]
            if len(shape) == 3:
                ap = ap.rearrange("p (a b) -> p a b", a=shape[1])
            if shape[0] < 128:
                ap = ap[0:shape[0]]
            return ap

        cst = A.get(F32, [128, NCONST])
        ident = cst[:, C_ID:C_ID + 128]
        masku = cst[:, C_MU:C_MU + 128]
        su = cst[:, C_SU:C_SU + 128]
        iota1 = cst[:, C_IOTA:C_IOTA + 512]
        ones = cst[:, C_ONES:C_ONES + 128]
        sel = cst[0:8, C_SEL:C_SEL + 1024]
        identb = A.get(BF16, [128, 128])
        onesb = A.get(BF16, [128, 128])
        vec = A.get(F32, [128, L, 40])
        gf = A.get(F32, [128, 8])
        cwg_s = A.get(F32, [128, L, 12, 4])
        cwc_s = A.get(F32, [128, L, 2, 31])
        ba8_s = A.get(F32, [8, L, 2])
        na8 = A.get(F32, [8, L])
        s5c = A.get(F32, [128, L, 8, 3])
        s5r = A.get(F32, [128, L, 8])
        hT = A.get(F32, [128, 8, 512])
        hn = A.get(BF16, [128, 8, 512])
        mix = A.get(BF16, [128, 8, 512])
        Sst = [A.get(F32, [128, 4, 128]) for _ in range(L)]
        ghist = [A.get(F32, [128, 12, 3]) for _ in range(L)]
        chist = [A.get(F32, [128, 2, 30]) for _ in range(L)]
        xst = [A.get(F32, [128, 8, 2]) for _ in range(L)]
        Sst_s = [A.get(F32, [128, 4, 128]) for _ in range(2)]
        ghist_s = [A.get(F32, [128, 12, 3]) for _ in range(2)]
        chist_s = [A.get(F32, [128, 2, 30]) for _ in range(2)]
        xst_s = [A.get(F32, [128, 8, 2]) for _ in range(2)]
        NSLOT = 5
        wslot = [A.get(BF16, [128, CHW]) for _ in range(NSLOT)]
        s5m = A.get(BF16, [128, 4, 8, 128])
        persist_end = A.off

        try:
            S.dma("sp", cst, consts)
            S.dma("sp", vec, vecs)
            S.dma("sp", gf, gfin)
            S.dma("sp", cwg_s, cwg)
            S.dma("sp", cwc_s, cwc)
            S.dma("sp", ba8_s, ba8)
            S.dma("sp", s5c, s5col)
            cp("dve", identb, ident)
            cp("dve", onesb, ones)
            act(na8, ba8_s[:, :, 1], AF.Exp)
            ts("dve", na8, na8, -1.0, ALU.mult)
            ck(1)
            for l in range(L):
                memset("dve", Sst[l], 0.0)
                memset("dve", ghist[l], 0.0)
                memset("dve", chist[l], 0.0)
                memset("dve", xst[l], 0.0)

            def range_reduce(dst, src, tmpi, tmpf):
                ts("dve", tmpi, src, 1.0 / TWO_PI, ALU.mult)
                cp("dve", tmpf, tmpi)
                stt("dve", dst, tmpf, -TWO_PI, src, ALU.mult, ALU.add)
                ts("dve", dst, dst, -3.1415925, ALU.max, 3.1415925, ALU.min)

            A.off = persist_end
            th = A.get(F32, [128, L, 8])
            dtc = A.get(F32, [128, L, 8])
            t_i = A.get(I32, [128, 512])
            t_f = A.get(F32, [128, 512])
            t_a = A.get(F32, [128, 512])
            t_b = A.get(F32, [128, 512])
            tabs = [A.get(F32, [128, 2, 512]) for _ in range(2)]
            stg = [A.get(F32, [128, CHW]) for _ in range(2)]
            stb = [A.get(BF16, [128, CHW]) for _ in range(2)]
            act(dtc, s5c[:, :, :, 2], AF.Exp)
            tt("dve", th, s5c[:, :, :, 1], dtc, ALU.mult)
            tt("dve", s5r, s5c[:, :, :, 0], dtc, ALU.mult)
            act(s5r, s5r, AF.Exp)
            thf = th.rearrange("p a b -> p (a b)")
            range_reduce(thf, thf, t_i[:, 0:L * 8], t_f[:, 0:L * 8])
            k = 0
            for l in range(L):
                for q in range(8):
                    tb = tabs[k % 2]
                    k += 1
                    ts("dve", t_a, iota1, th[:, l, q:q + 1], ALU.mult)
                    range_reduce(t_b, t_a, t_i, t_f)
                    act(tb[:, 1, :], t_b, AF.Sin)
                    ts("dve", t_a, t_b, math.pi / 2, ALU.add)
                    range_reduce(t_b, t_a, t_i, t_f)
                    act(tb[:, 0, :], t_b, AF.Sin)
                    S.dma("sp", s5tab[l, q], tb)
            ck(2)
            s5stage = A.get(F32, [128, 3, 8, 128])
            s5t = [A.get(F32, [128, 512]) for _ in range(6)]
            for l in range(L):
                S.dma("sp", s5stage[:, 0], s5row[:, l, 0])
                S.dma("sp", s5stage[:, 1], s5row[:, l, 1])
                S.dma("sp", s5stage[:, 2], s5row[:, l, 2])
                lre, lim, ldt = s5stage[:, 0], s5stage[:, 1], s5stage[:, 2]
                act(ldt, ldt, AF.Exp)
                for hq in range(2):
                    sl = slice(hq * 4, hq * 4 + 4)
                    a_re = lre[:, sl, :].rearrange("p a b -> p (a b)")
                    a_im = lim[:, sl, :].rearrange("p a b -> p (a b)")
                    a_dt = ldt[:, sl, :].rearrange("p a b -> p (a b)")
                    w0, w1, w2, w3, w4, w5 = s5t
                    tt("dve", w0, a_im, a_dt, ALU.mult)
                    range_reduce(w1, w0, t_i, t_f)
                    act(w2, w1, AF.Sin)
                    ts("dve", w0, w1, math.pi / 2, ALU.add)
                    range_reduce(w1, w0, t_i, t_f)
                    act(w3, w1, AF.Sin)
                    tt("dve", w0, a_re, a_dt, ALU.mult)
                    act(w0, w0, AF.Exp)
                    tt("dve", w3, w3, w0, ALU.mult)
                    ts("dve", w3, w3, -1.0, ALU.add)
                    tt("dve", w2, w2, w0, ALU.mult)
                    tt("dve", w0, a_re, a_re, ALU.mult)
                    tt("dve", w1, a_im, a_im, ALU.mult)
                    tt("dve", w0, w0, w1, ALU.add)
                    recip(w0, w0)
                    tt("dve", w1, w3, a_re, ALU.mult)
                    tt("dve", w4, w2, a_im, ALU.mult)
                    tt("dve", w1, w1, w4, ALU.add)
                    tt("dve", w1, w1, w0, ALU.mult)
                    tt("dve", w4, w2, a_re, ALU.mult)
                    tt("dve", w5, w3, a_im, ALU.mult)
                    tt("dve", w4, w4, w5, ALU.subtract)
                    tt("dve", w4, w4, w0, ALU.mult)
                    S.dma("sp", w2.rearrange("p (a b) -> p a b", a=4), s5bt[l, 0][:, sl, :])
                    S.dma("sp", w3.rearrange("p (a b) -> p a b", a=4), s5bt[l, 1][:, sl, :])
                    tt("dve", w0, w1, w2, ALU.mult)
                    tt("dve", w5, w4, w3, ALU.mult)
                    tt("dve", s5m[:, 0, sl, :].rearrange("p a b -> p (a b)"), w0, w5, ALU.subtract)
                    tt("dve", w0, w1, w3, ALU.mult)
                    tt("dve", w5, w4, w2, ALU.mult)
                    tt("dve", s5m[:, 1, sl, :].rearrange("p a b -> p (a b)"), w0, w5, ALU.add)
                    S.dma("sp", w2.rearrange("p (a b) -> p a b", a=4), s5ct[l, 0][:, sl, :])
                    S.dma("sp", w3.rearrange("p (a b) -> p a b", a=4), s5ct[l, 1][:, sl, :])
                    cp("dve", s5m[:, 2, sl, :].rearrange("p a b -> p (a b)"), w2)
                    ts("dve", s5m[:, 3, sl, :].rearrange("p a b -> p (a b)"), w3, -1.0, ALU.mult)

                S.dma("sp", s5mscr[l], s5m)
            ck(3)
            k = 0
            for l in range(L):
                for j in range(NCHUNK):
                    a = stg[k % 2]
                    b = stb[k % 2]
                    S.dma("sp", a, wch[l, j])
                    cp("dve" if k % 2 == 0 else "act", b, a)
                    S.dma("sp", wscr[l, j], b)
                    k += 1
            S.barrier()
            ck(4)
            A.off = persist_end

            wseq = []
            tiles = []
            for i in range(NPT):
                tiles.append(dict(tok0=i * 512, T=512, segs=[(0, 512)], C=128, last=(i == NPT - 1)))
            if with_sample:
                tiles.append(dict(tok0=NPT * 512, T=64, segs=[(1, 32), (2, 32)], C=32, last=True))
            for ti in range(len(tiles)):
                for l in range(L):
                    for j in range(NCHUNK):
                        wseq.append((l, j))
            wstate = dict(issued=0, used=0, released=0)

            def wpump():
                while wstate["issued"] < len(wseq) and wstate["issued"] - NSLOT < wstate["released"]:
                    m = wstate["issued"]
                    l_, j_ = wseq[m]
                    S.dma("sp", wslot[m % NSLOT], wscr[l_, j_])
                    wstate["issued"] += 1

            def wget():
                n = wstate["used"]
                wstate["used"] += 1
                wpump()
                assert wstate["issued"] > n
                return wslot[n % NSLOT]

            def wdone(k=1):
                wstate["released"] += k
                wpump()

            gluw = A.get(BF16, [128, 2, 256])
            sqb = A.get(BF16, [128, 512])
            sdt = A.get(F32, [128, 512])
            rst = A.get(F32, [128, 512])
            t512 = A.get(F32, [128, 512])
            sgp = A.get(F32, [128, 512])
            u5f = A.get(F32, [128, 2, 512])
            u5b = A.get(BF16, [128, 2, 512])
            ov0 = A.off
            xpb = A.get(BF16, [128, 3 * 2 * 520])
            dgc = A.get(BF16, [128, 3, 4, 128])
            qkv = A.get(F32, [128, 3, 512])
            zs = A.get(F32, [128, 512])
            kTb = A.get(BF16, [128, 512])
            kbT = A.get(BF16, [128, 512])
            qTb = A.get(BF16, [128, 512])
            qgT = A.get(BF16, [128, 512])
            betaB = A.get(F32, [128, 512])
            gcB = A.get(F32, [128, 512])
            egB = A.get(F32, [128, 512])
            oT = A.get(F32, [128, 512])
            sig8 = A.get(F32, [8, 512])
            g8 = A.get(F32, [8, 512])
            e8 = A.get(F32, [8, 512])
            cols = A.get(F32, [128, 4, 2])
            negc = A.get(F32, [128, 4])
            ecol = A.get(F32, [128, 4])
            bexp = A.get(F32, [128, 4])
            kdcol = A.get(F32, [128, 4])
            GD = F32 if GDN_F32 else BF16
            kbg = A.get(GD, [128, 4, 128])
            kd = A.get(BF16, [128, 4, 128])
            vbt = A.get(GD, [128, 4, 128])
            Du = A.get(F32, [128, 4, 128])
            EU = A.get(F32, [128, 4, 128])
            EUs = A.get(F32, [128, 4, 128])
            Ub = [A.get(GD, [128, 4, 128]) for _ in range(2)]
            Lb = [A.get(GD, [128, 4, 128]) for _ in range(2)]
            Rb = [A.get(GD, [128, 4, 128]) for _ in range(2)]
            Aqk = A.get(BF16, [128, 4, 128])
            nwT = A.get(GD, [128, 4, 128])
            vnew = A.get(BF16, [128, 128])
            Sbf = A.get(BF16, [128, 128])
            ov_end = A.off
            A.off = ov0
            tabq = [A.get(F32, [128, 2, 512]) for _ in range(2)]
            s5t = [A.get(F32, [128, 512]) for _ in range(6)]
            xbre = A.get(BF16, [128, 512])
            xbim = A.get(BF16, [128, 512])
            ygf = A.get(F32, [128, 2, 512])
            ygb = A.get(BF16, [128, 2, 512])
            ov_end = max(ov_end, A.off)
            A.off = ov0
            xcb = A.get(BF16, [128, 2 * 2 * 544])
            dgcc = A.get(BF16, [128, 2, 31, 128])
            xg = A.get(F32, [128, 2, 512])
            xc16 = A.get(BF16, [128, 2, 512])
            ov_end = max(ov_end, A.off)
            A.off = ov0
            act_base = A.get(F32, [128, 22 * 256])
            actb = act_base.bitcast(BF16).rearrange("p (a b) -> p a b", a=22)
            yout = act_base[:, 0:8 * 512].rearrange("p (a b) -> p a b", a=8)
            pTf = A.get(F32, [128, 2, 512])
            pTb = A.get(BF16, [128, 2, 512])
            ov_end = max(ov_end, A.off)
            A.off = ov_end
            print("SBUF words used", A.off, "of", SBW)

            def rmsnorm_to(dst_bf, gcol_fn, T, final_out=None):
                pb = pbank()
                for kc in range(8):
                    act(sqb[:, 0:T], hT[:, kc, 0:T], AF.Square)
                    mm(PS(pb)[:, 0:T], onesb, sqb[:, 0:T], start=(kc == 0), stop=(kc == 7))
                act(sdt[:, 0:T], PS(pb)[:, 0:T], AF.Sqrt, bias=1e-6, scale=1.0 / D)
                recip(rst[:, 0:T], sdt[:, 0:T])
                for kc in range(8):
                    o = dst_bf[:, kc, 0:T] if final_out is None else final_out[:, kc, 0:T]
                    stt("dve", o, hT[:, kc, 0:T], gcol_fn(kc), rst[:, 0:T], ALU.mult, ALU.mult)

            VEC_GMIX, VEC_GFFN, VEC_GN, VEC_S5D, VEC_GLUB, VEC_DWB, VEC_LNG, VEC_LNB = 0, 8, 16, 17, 19, 21, 23, 25

            for tile in tiles:
                T = tile["T"]
                tok0 = tile["tok0"]
                segs = tile["segs"]
                nseg = len(segs)
                Ls = segs[0][1]
                C = tile["C"]
                nb = T // C
                m_lev = int(math.log2(C))
                is_sample = segs[0][0] != 0
                S.dma("sp", hT[:, :, 0:T], xT[:, :, tok0:tok0 + T])

                def seqstate(lst_p, lst_s, l, si):
                    seq = segs[si][0]
                    return lst_p[l] if seq == 0 else lst_s[seq - 1]

                for l in range(L):
                    if is_sample:
                        for si in range(nseg):
                            S.dma("sp", Sst_s[si], st_gdn[l, si])
                            S.dma("sp", ghist_s[si], st_gc[l, si])
                            S.dma("sp", chist_s[si], st_cc[l, si])
                            S.dma("sp", xst_s[si], st_s5[l, si])
                    S.dma("sp", s5m, s5mscr[l])

                    rmsnorm_to(hn, lambda kc: vec[:, l, VEC_GMIX + kc:VEC_GMIX + kc + 1], T)
                    ck(5)

                    w4 = wget()
                    w4a = w4[:, 0:8 * 264].rearrange("p (a b) -> p a b", a=8)
                    cp("dve", gluw.rearrange("p a b -> p (a b)"), w4[:, 8 * 264:8 * 264 + 512])
                    pb_ba = pbank()
                    for kc in range(8):
                        mm(PS(pb_ba)[0:8, 0:T], w4a[:, kc, 256:264], hn[:, kc, 0:T], start=(kc == 0), stop=(kc == 7))
                    pu = [pbank(), pbank()]
                    for mc in range(2):
                        for kc in range(8):
                            mm(PS(pu[mc])[:, 0:T], w4a[:, kc, mc * 128:(mc + 1) * 128], hn[:, kc, 0:T],
                               start=(kc == 0), stop=(kc == 7))
                    wdone(1)
                    ck(51)
                    act(sig8[:, 0:T], PS(pb_ba)[0:8, 0:T], AF.Sigmoid)
                    ck(52)
                    act(e8[:, 0:T], PS(pb_ba)[0:8, 0:T], AF.Exp, bias=ba8_s[:, l, 0:1])
                    ck(53)
                    act(e8[:, 0:T], e8[:, 0:T], AF.Ln, bias=1.0)
                    ck(54)
                    ts("dve", g8[:, 0:T], e8[:, 0:T], na8[:, l:l + 1], ALU.mult)
                    ck(55)
                    for mc in range(2):
                        cp("act", u5f[:, mc, 0:T], PS(pu[mc])[:, 0:T])
                        ck(56 + mc * 2)
                        cp("dve", u5b[:, mc, 0:T], PS(pu[mc])[:, 0:T])
                        ck(57 + mc * 2)
                    ck(6)

                    for h in range(4):
                        wh = wget().rearrange("p (a b) -> p a b", a=8)
                        xv = xpb[:, 0:3 * nseg * (4 + Ls)].rearrange("p (a b c) -> p a b c", a=3, b=nseg)
                        for si in range(nseg):
                            gh = seqstate(ghist, ghist_s, l, si)
                            for c3 in range(3):
                                cp("dve", xv[:, c3, si, 1:4], gh[:, c3 * 4 + h, :])
                        for c3 in range(3):
                            for j in range(4):
                                ts("dve", dgc[:, c3, j, :], identb, cwg_s[:, l, c3 * 4 + h, j:j + 1], ALU.mult)
                        ck(61)
                        pz = None
                        for c4 in range(4):
                            pb = pbank()
                            for kc in range(8):
                                mm(PS(pb)[:, 0:T], wh[:, kc, c4 * 128:(c4 + 1) * 128], hn[:, kc, 0:T],
                                   start=(kc == 0), stop=(kc == 7))
                            if c4 < 3:
                                for si in range(nseg):
                                    gh = seqstate(ghist, ghist_s, l, si)
                                    cp("act", xv[:, c4, si, 4:4 + Ls], PS(pb)[:, si * Ls:(si + 1) * Ls])
                                    cp("dve", gh[:, c4 * 4 + h, :], PS(pb)[:, (si + 1) * Ls - 3:(si + 1) * Ls])
                            else:
                                act(zs[:, 0:T], PS(pb)[:, 0:T], AF.Silu)
                        wdone(1)
                        ck(62)
                        for c3 in range(3):
                            pb = pbank()
                            for si in range(nseg):
                                for j in range(4):
                                    mm(PS(pb)[:, si * Ls:(si + 1) * Ls], dgc[:, c3, j, :], xv[:, c3, si, 1 + j:1 + j + Ls],
                                       start=(j == 0), stop=(j == 3))
                            act(qkv[:, c3, 0:T], PS(pb)[:, 0:T], AF.Silu)
                        ck(63)
                        for c3, dst, sc in ((0, qTb, 128.0 ** -0.5), (1, kTb, 1.0)):
                            act(sqb[:, 0:T], qkv[:, c3, 0:T], AF.Square)
                            pb = pbank()
                            mm(PS(pb)[:, 0:T], onesb, sqb[:, 0:T])
                            act(sdt[:, 0:T], PS(pb)[:, 0:T], AF.Sqrt, bias=1e-6)
                            recip(rst[:, 0:T], sdt[:, 0:T])
                            stt("dve", dst[:, 0:T], qkv[:, c3, 0:T], sc, rst[:, 0:T], ALU.mult, ALU.mult)
                        ck(64)
                        pbb = pbank()
                        mm(PS(pbb)[:, 0:T], sel[:, h * 128:(h + 1) * 128], sig8[:, 0:T])
                        pbg = pbank()
                        mm(PS(pbg)[:, 0:T], sel[:, (4 + h) * 128:(5 + h) * 128], g8[:, 0:T])
                        cp("act", betaB[:, 0:T], PS(pbb)[:, 0:T])
                        for b in range(nb):
                            scan(gcB[:, b * C:(b + 1) * C], ones[:, 0:C], PS(pbg)[:, b * C:(b + 1) * C], 0.0)
                        act(egB[:, 0:T], gcB[:, 0:T], AF.Exp)
                        tt("dve", kbT[:, 0:T], kTb[:, 0:T], betaB[:, 0:T], ALU.mult)
                        tt("dve", qgT[:, 0:T], qTb[:, 0:T], egB[:, 0:T], ALU.mult)
                        ck(65)
                        pbc = pbank()
                        pcv = PS(pbc)[:, 0:nb * 2].rearrange("p (a b) -> p a b", a=nb)[0:C]
                        for b in range(nb):
                            mm(pcv[:, b, 0:1], gcB[:, b * C:(b + 1) * C], ident[:, 0:1])
                            mm(pcv[:, b, 1:2], betaB[:, b * C:(b + 1) * C], ident[:, 0:1])
                        cv = cols[0:C, 0:nb, :]
                        cp("dve", cv, pcv)
                        ts("dve", negc[0:C, 0:nb], cv[:, :, 0], -1.0, ALU.mult)
                        act(ecol[0:C, 0:nb], cv[:, :, 0], AF.Exp)
                        tt("dve", bexp[0:C, 0:nb], cv[:, :, 1], ecol[0:C, 0:nb], ALU.mult)
                        for b in range(nb):
                            act(kdcol[0:C, b:b + 1], cv[:, b, 0:1], AF.Exp, bias=gcB[0:C, (b + 1) * C - 1:(b + 1) * C], scale=-1.0)
                        ck(66)
                        pkt = pbank()
                        pk = PS(pkt, dt=BF16)[:, 0:nb * 128].rearrange("p (a b) -> p a b", a=nb)[0:C]
                        for b in range(nb):
                            tr(pk[:, b, :], kTb[:, b * C:(b + 1) * C], identb)
                        pvt = pbank()
                        pv = PS(pvt)[:, 0:nb * 128].rearrange("p (a b) -> p a b", a=nb)[0:C]
                        for b in range(nb):
                            tr(pv[:, b, :], qkv[:, 2, b * C:(b + 1) * C], ident)
                        for b in range(nb):
                            ts("dve", kbg[0:C, b, :], pk[:, b, :], bexp[0:C, b:b + 1], ALU.mult)
                            ts("dve", kd[0:C, b, :], pk[:, b, :], kdcol[0:C, b:b + 1], ALU.mult)
                            ts("dve", vbt[0:C, b, :], pv[:, b, :], cv[:, b, 1:2], ALU.mult)
                        ck(67)
                        pg = pbank()
                        pgv = PS(pg)[:, 0:nb * C].rearrange("p (a b) -> p a b", a=nb)[0:C]
                        pa = pbank()
                        pav = PS(pa)[:, 0:nb * C].rearrange("p (a b) -> p a b", a=nb)[0:C]
                        for b in range(nb):
                            mm(pgv[:, b, :], kTb[:, b * C:(b + 1) * C], kbT[:, b * C:(b + 1) * C])
                            mm(pav[:, b, :], kTb[:, b * C:(b + 1) * C], qTb[:, b * C:(b + 1) * C])
                        Duv = Du[0:C, 0:nb, 0:C]
                        EUv = EU[0:C, 0:nb, 0:C]
                        EUsv = EUs[0:C, 0:nb, 0:C]
                        for b in range(nb):
                            stt("dve", Duv[:, b, :], gcB[0:C, b * C:(b + 1) * C], negc[0:C, b:b + 1], masku[0:C, 0:C],
                                ALU.add, ALU.add)
                        act(EUv, Duv, AF.Exp)
                        for b in range(nb):
                            tt("dve", EUsv[:, b, :], EUv[:, b, :], su[0:C, 0:C], ALU.mult)
                        U0 = Ub[0][0:C, 0:nb, 0:C]
                        tt("dve", U0, pgv, EUsv, ALU.mult)
                        Aqv = Aqk[0:C, 0:nb, 0:C]
                        tt("dve", Aqv, pav, EUv, ALU.mult)
                        ck(68)
                        plt = pbank()
                        idg = ident if GDN_F32 else identb
                        pl = PS(plt, dt=GD)[:, 0:nb * C].rearrange("p (a b) -> p a b", a=nb)[0:C]
                        for b in range(nb):
                            tr(pl[:, b, :], U0[:, b, :], idg[0:C, 0:C])
                        L0 = Lb[0][0:C, 0:nb, 0:C]
                        cp("act", L0, pl)
                        R0 = Rb[0][0:C, 0:nb, 0:C]
                        for b in range(nb):
                            stt("dve", R0[:, b, :], U0[:, b, :], -1.0, idg[0:C, 0:C], ALU.mult, ALU.add)
                        ck(69)
                        cur = 0
                        for lev in range(1, m_lev):
                            Up = Ub[cur][0:C, 0:nb, 0:C]
                            Lp = Lb[cur][0:C, 0:nb, 0:C]
                            Rp = Rb[cur][0:C, 0:nb, 0:C]
                            Un = Ub[1 - cur][0:C, 0:nb, 0:C]
                            Ln = Lb[1 - cur][0:C, 0:nb, 0:C]
                            Rn = Rb[1 - cur][0:C, 0:nb, 0:C]
                            p1 = pbank()
                            p1v = PS(p1)[:, 0:nb * C].rearrange("p (a b) -> p a b", a=nb)[0:C]
                            for b in range(nb):
                                mm(p1v[:, b, :], Up[:, b, :], Lp[:, b, :])
                            cp("act", Ln, p1v)
                            if lev < m_lev - 1:
                                p2 = pbank()
                                p2v = PS(p2)[:, 0:nb * C].rearrange("p (a b) -> p a b", a=nb)[0:C]
                                for b in range(nb):
                                    mm(p2v[:, b, :], Lp[:, b, :], Up[:, b, :])
                                cp("dve", Un, p2v)
                            p3 = pbank()
                            p3v = PS(p3)[:, 0:nb * C].rearrange("p (a b) -> p a b", a=nb)[0:C]
                            for b in range(nb):
                                mm(p3v[:, b, :], idg[0:C, 0:C], Rp[:, b, :], start=True, stop=False)
                                mm(p3v[:, b, :], Ln[:, b, :], Rp[:, b, :], start=False, stop=True)
                            cp("dve", Rn, p3v)
                            cur = 1 - cur
                        R = Rb[cur][0:C, 0:nb, 0:C]
                        ck(70)
                        pw = pbank()
                        pwv = PS(pw)[:, 0:nb * C].rearrange("p (a b) -> p a b", a=nb)
                        for b in range(nb):
                            mm(pwv[:, b, :], kbg[0:C, b, :], R[:, b, :])
                        nwv = nwT[:, 0:nb, 0:C]
                        ts("dve", nwv, pwv, -1.0, ALU.mult)
                        ck(71)
                        for b in range(nb):
                            si = (b * C) // Ls
                            Sm = seqstate(Sst, Sst_s, l, si)[:, h, :]
                            if b == 0 or (b * C) % Ls == 0:
                                cp("act", Sbf, Sm)
                            p_v = pbank()
                            pvn = PS(p_v)[0:C, 0:128]
                            mm(pvn, R[:, b, :], vbt[0:C, b, :], start=True, stop=False)
                            mm(pvn, nwv[:, b, :], Sm if GDN_F32 else Sbf, start=False, stop=True)
                            cp("act", vnew[0:C, :], pvn)
                            p_o = pbank()
                            po = PS(p_o)[:, 0:C]
                            mm(po, Sbf, qgT[:, b * C:(b + 1) * C], start=True, stop=False)
                            mm(po, vnew[0:C, :], Aqv[:, b, :], start=False, stop=True)
                            cp("dve", oT[:, b * C:(b + 1) * C], po)
                            p_s = pbank()
                            psn = PS(p_s)[:, 0:128]
                            mm(psn, kd[0:C, b, :], vnew[0:C, :])
                            stt("dve", Sm, Sm, egB[:, (b + 1) * C - 1:(b + 1) * C], psn, ALU.mult, ALU.add)
                            if b + 1 < nb and ((b + 1) * C) % Ls != 0:
                                cp("act", Sbf, Sm)
                        ck(72)
                        act(sqb[:, 0:T], oT[:, 0:T], AF.Square)
                        pb = pbank()
                        mm(PS(pb)[:, 0:T], onesb, sqb[:, 0:T])
                        act(sdt[:, 0:T], PS(pb)[:, 0:T], AF.Sqrt, bias=1e-6, scale=1.0 / 128)
                        recip(rst[:, 0:T], sdt[:, 0:T])
                        stt("dve", t512[:, 0:T], oT[:, 0:T], vec[:, l, VEC_GN:VEC_GN + 1], rst[:, 0:T], ALU.mult, ALU.mult)
                        tt("dve", mix[:, h, 0:T], t512[:, 0:T], zs[:, 0:T], ALU.mult)
                        ck(7)

                    for si in range(nseg):
                        seq = segs[si][0]
                        if tile["last"]:
                            S.dma("sp", o_gdn[l, seq], seqstate(Sst, Sst_s, l, si))
                            S.dma("sp", o_gc[l, seq], seqstate(ghist, ghist_s, l, si))

                    ck(8)
                    py = [pbank(pin=True), pbank(pin=True)]
                    for q in range(8):
                        kc = q // 4
                        tb = tabq[q % 2]
                        S.dma("sp", tb[:, :, 0:Ls], s5tab[l, q][:, :, 0:Ls])
                        p_re = pbank()
                        p_im = pbank()
                        mm(PS(p_re)[:, 0:T], s5m[:, 0, q, :], u5b[:, kc, 0:T])
                        mm(PS(p_im)[:, 0:T], s5m[:, 1, q, :], u5b[:, kc, 0:T])
                        w0, w1, w2, w3, w4, w5 = s5t
                        for si in range(nseg):
                            cs = slice(si * Ls, (si + 1) * Ls)
                            nC = tb[:, 0, 0:Ls]
                            nS = tb[:, 1, 0:Ls]
                            xs_ = seqstate(xst, xst_s, l, si)
                            tt("dve", w0[:, cs], nC, PS(p_re)[:, cs], ALU.mult)
                            tt("dve", w1[:, cs], nS, PS(p_im)[:, cs], ALU.mult)
                            tt("dve", w0[:, cs], w0[:, cs], w1[:, cs], ALU.add)
                            tt("dve", w2[:, cs], nC, PS(p_im)[:, cs], ALU.mult)
                            tt("dve", w3[:, cs], nS, PS(p_re)[:, cs], ALU.mult)
                            tt("dve", w2[:, cs], w2[:, cs], w3[:, cs], ALU.subtract)
                            scan(w1[:, cs], bc(s5r[:, l, q:q + 1], [128, Ls]), w0[:, cs], xs_[:, q, 0:1])
                            scan(w3[:, cs], bc(s5r[:, l, q:q + 1], [128, Ls]), w2[:, cs], xs_[:, q, 1:2])
                            tt("dve", w4[:, cs], nC, w1[:, cs], ALU.mult)
                            tt("dve", w5[:, cs], nS, w3[:, cs], ALU.mult)
                            tt("dve", w0[:, cs], w4[:, cs], w5[:, cs], ALU.subtract)
                            cp("act", xbre[:, cs], w0[:, cs])
                            tt("dve", w4[:, cs], nS, w1[:, cs], ALU.mult)
                            tt("dve", w5[:, cs], nC, w3[:, cs], ALU.mult)
                            tt("dve", w2[:, cs], w4[:, cs], w5[:, cs], ALU.add)
                            cp("act", xbim[:, cs], w2[:, cs])
                            e_ = (si + 1) * Ls - 1
                            cp("dve", xs_[:, q, 0:1], w0[:, e_:e_ + 1])
                            cp("dve", xs_[:, q, 1:2], w2[:, e_:e_ + 1])
                        mm(PS(py[kc])[:, 0:T], s5m[:, 2, q, :], xbre[:, 0:T], start=(q % 4 == 0), stop=False)
                        mm(PS(py[kc])[:, 0:T], s5m[:, 3, q, :], xbim[:, 0:T], start=False, stop=(q % 4 == 3))
                    pinned.clear()
                    for kc in range(2):
                        stt("dve", ygf[:, kc, 0:T], u5f[:, kc, 0:T], vec[:, l, VEC_S5D + kc:VEC_S5D + kc + 1],
                            PS(py[kc])[:, 0:T], ALU.mult, ALU.add)
                        act(ygf[:, kc, 0:T], ygf[:, kc, 0:T], AF.Gelu_apprx_tanh)
                        cp("dve", ygb[:, kc, 0:T], ygf[:, kc, 0:T])
                    for mc in range(2):
                        pb = pbank()
                        for kc in range(2):
                            mm(PS(pb)[:, 0:T], gluw[:, kc, mc * 128:(mc + 1) * 128], ygb[:, kc, 0:T],
                               start=(kc == 0), stop=(kc == 1))
                        act(sgp[:, 0:T], PS(pb)[:, 0:T], AF.Sigmoid, bias=vec[:, l, VEC_GLUB + mc:VEC_GLUB + mc + 1])
                        tt("dve", mix[:, 4 + mc, 0:T], ygf[:, mc, 0:T], sgp[:, 0:T], ALU.mult)
                    if tile["last"]:
                        for si in range(nseg):
                            S.dma("sp", o_s5[l, segs[si][0]], seqstate(xst, xst_s, l, si))

                    ck(9)
                    wc = wget().rearrange("p (a b) -> p a b", a=8)
                    xcv = xcb[:, 0:2 * nseg * (30 + Ls)].rearrange("p (a b c) -> p a b c", a=2, b=nseg)
                    for kc in range(2):
                        for j in range(31):
                            ts("dve", dgcc[:, kc, j, :], identb, cwc_s[:, l, kc, j:j + 1], ALU.mult)
                    for si in range(nseg):
                        ch = seqstate(chist, chist_s, l, si)
                        cp("dve", xcv[:, :, si, 0:30], ch)
                    pa_ = [pbank(), pbank()]
                    pg_ = [pbank(), pbank()]
                    for c4 in range(4):
                        pb = pa_[c4] if c4 < 2 else pg_[c4 - 2]
                        for kc in range(8):
                            mm(PS(pb)[:, 0:T], wc[:, kc, c4 * 128:(c4 + 1) * 128], hn[:, kc, 0:T],
                               start=(kc == 0), stop=(kc == 7))
                    wdone(1)
                    for kc in range(2):
                        act(sgp[:, 0:T], PS(pg_[kc])[:, 0:T], AF.Sigmoid)
                        tt("dve", xg[:, kc, 0:T], PS(pa_[kc])[:, 0:T], sgp[:, 0:T], ALU.mult)
                        for si in range(nseg):
                            ch = seqstate(chist, chist_s, l, si)
                            cp("act", xcv[:, kc, si, 30:30 + Ls], xg[:, kc, si * Ls:(si + 1) * Ls])
                            cp("dve", ch[:, kc, :], xg[:, kc, (si + 1) * Ls - 30:(si + 1) * Ls])
                    pcv_ = [pbank(), pbank()]
                    for kc in range(2):
                        for si in range(nseg):
                            for j in range(31):
                                mm(PS(pcv_[kc])[:, si * Ls:(si + 1) * Ls], dgcc[:, kc, j, :], xcv[:, kc, si, j:j + Ls],
                                   start=(j == 0), stop=(j == 30))
                    xc = xg
                    for kc in range(2):
                        act(xc[:, kc, 0:T], PS(pcv_[kc])[:, 0:T], AF.Identity, bias=vec[:, l, VEC_DWB + kc:VEC_DWB + kc + 1])
                        cp("dve", xc16[:, kc, 0:T], xc[:, kc, 0:T])
                    pm = pbank()
                    for kc in range(2):
                        mm(PS(pm)[:, 0:T], onesb, xc16[:, kc, 0:T], start=(kc == 0), stop=(kc == 1))
                    for kc in range(2):
                        stt("dve", xc[:, kc, 0:T], PS(pm)[:, 0:T], -1.0 / 256, xc[:, kc, 0:T], ALU.mult, ALU.add)
                        act(xc16[:, kc, 0:T], xc[:, kc, 0:T], AF.Square)
                    pv_ = pbank()
                    for kc in range(2):
                        mm(PS(pv_)[:, 0:T], onesb, xc16[:, kc, 0:T], start=(kc == 0), stop=(kc == 1))
                    act(sdt[:, 0:T], PS(pv_)[:, 0:T], AF.Sqrt, bias=1e-5, scale=1.0 / 256)
                    recip(rst[:, 0:T], sdt[:, 0:T])
                    for kc in range(2):
                        stt("dve", t512[:, 0:T], xc[:, kc, 0:T], vec[:, l, VEC_LNG + kc:VEC_LNG + kc + 1], rst[:, 0:T],
                            ALU.mult, ALU.mult)
                        act(mix[:, 6 + kc, 0:T], t512[:, 0:T], AF.Silu, bias=vec[:, l, VEC_LNB + kc:VEC_LNB + kc + 1])
                    if tile["last"]:
                        for si in range(nseg):
                            S.dma("sp", o_cc[l, segs[si][0]], seqstate(chist, chist_s, l, si))

                    ck(10)
                    for half in range(2):
                        wo = wget().rearrange("p (a b) -> p a b", a=8)
                        for m4 in range(4):
                            mc = half * 4 + m4
                            pb = pbank()
                            for kc in range(8):
                                mm(PS(pb)[:, 0:T], wo[:, kc, m4 * 128:(m4 + 1) * 128], mix[:, kc, 0:T],
                                   start=(kc == 0), stop=(kc == 7))
                            tt("dve", hT[:, mc, 0:T], hT[:, mc, 0:T], PS(pb)[:, 0:T], ALU.add)
                        wdone(1)

                    ck(11)
                    rmsnorm_to(hn, lambda kc: vec[:, l, VEC_GFFN + kc:VEC_GFFN + kc + 1], T)
                    for j in range(11):
                        wf = wget().rearrange("p (a b) -> p a b", a=8)
                        for c2 in range(2):
                            hc = j * 2 + c2
                            p1 = pbank()
                            p3 = pbank()
                            for kc in range(8):
                                mm(PS(p1)[:, 0:T], wf[:, kc, c2 * 128:(c2 + 1) * 128], hn[:, kc, 0:T],
                                   start=(kc == 0), stop=(kc == 7))
                            for kc in range(8):
                                mm(PS(p3)[:, 0:T], wf[:, kc, 256 + c2 * 128:256 + (c2 + 1) * 128], hn[:, kc, 0:T],
                                   start=(kc == 0), stop=(kc == 7))
                            act(sgp[:, 0:T], PS(p1)[:, 0:T], AF.Silu)
                            tt("dve", actb[:, hc, 0:T], sgp[:, 0:T], PS(p3)[:, 0:T], ALU.mult)
                        wdone(1)
                    for mc in range(8):
                        w2c = wget()[:, 0:22 * 128].rearrange("p (a b) -> p a b", a=22)
                        pb = pbank()
                        for kc in range(22):
                            mm(PS(pb)[:, 0:T], w2c[:, kc, :], actb[:, kc, 0:T], start=(kc == 0), stop=(kc == 21))
                        tt("dve", hT[:, mc, 0:T], hT[:, mc, 0:T], PS(pb)[:, 0:T], ALU.add)
                        wdone(1)

                    ck(12)
                    S.dma("sp", pTf[:, :, 0:T], pT[l][:, :, tok0:tok0 + T])
                    for kc in range(8):
                        cp("act", hn[:, kc, 0:T], hT[:, kc, 0:T])
                    cp("dve", pTb[:, :, 0:T], pTf[:, :, 0:T])
                    wg0 = wget().rearrange("p (a b) -> p a b", a=8)
                    wpe = wget()[:, 0:2048].rearrange("p (a b) -> p a b", a=2)
                    wg1 = None
                    for mc in range(8):
                        if mc == 4:
                            wdone(1)
                            wg1 = wget().rearrange("p (a b) -> p a b", a=8)
                        wg = wg0 if mc < 4 else wg1
                        m4 = mc % 4
                        pb = pbank()
                        for kc in range(8):
                            mm(PS(pb)[:, 0:T], wg[:, kc, m4 * 128:(m4 + 1) * 128], hn[:, kc, 0:T],
                               start=(kc == 0), stop=(kc == 7))
                        act(sgp[:, 0:T], PS(pb)[:, 0:T], AF.Sigmoid)
                        pb2 = pbank()
                        for kc in range(2):
                            mm(PS(pb2)[:, 0:T], wpe[:, kc, mc * 128:(mc + 1) * 128], pTb[:, kc, 0:T],
                               start=(kc == 0), stop=(kc == 1))
                        tt("dve", t512[:, 0:T], PS(pb2)[:, 0:T], sgp[:, 0:T], ALU.mult)
                        tt("dve", hT[:, mc, 0:T], hT[:, mc, 0:T], t512[:, 0:T], ALU.add)
                    wdone(2)
                    ck(13)

                rmsnorm_to(None, lambda kc: gf[:, kc:kc + 1], T, final_out=yout)
                S.dma("sp", yT[:, :, tok0:tok0 + T], yout[:, :, 0:T])

        except _Stop:
            print('STOPPED at', KSTOP)
            KT = int(os.environ.get("KTAIL", "0"))
            if KT == 1:
                memset("dve", t512, 0.0)
            elif KT == 2:
                cp("act", t512, sdt)
            elif KT == 3:
                mm(PS(7), onesb, sqb)
                cp("dve", t512, PS(7))
            elif KT == 4:
                mm(PS(7), onesb, sqb)
                cp("act", t512, PS(7))
        S.finish()
        S.finalize()
        print("ops recorded:", S.nops, {k: len(v) for k, v in S.prog.items()}, "signals", S.nsig)

        @block.tensor
        def _(e):
            S.replay("pe", e)

        @block.scalar
        def _(e):
            S.replay("act", e)

        @block.vector
        def _(e):
            S.replay("dve", e)

        @block.gpsimd
        def _(e):
            S.replay("pool", e)

        @block.sync
        def _(e):
            S.replay("sp", e)
    return nc


def _consts():
    c = np.zeros((128, NCONST), np.float32)
    c[:, C_ID:C_ID + 128] = np.eye(128)
    s = np.arange(128)[:, None]
    cc = np.arange(128)[None, :]
    c[:, C_MU:C_MU + 128] = np.where(s <= cc, 0.0, -30000.0)
    c[:, C_SU:C_SU + 128] = (s < cc).astype(np.float32)
    c[:, C_IOTA:C_IOTA + 512] = np.arange(1, 513)[None, :]
    c[:, C_ONES:C_ONES + 128] = 1.0
    for j in range(8):
        c[j, C_SEL + j * 128:C_SEL + (j + 1) * 128] = 1.0
    return c


def _fm(v):
    C = v.shape[-1]
    r = v.reshape(v.shape[:-1] + (C // 128, 128))
    return np.moveaxis(r, -1, 0)


def _pack_weights(inp, L):
    w = np.zeros((L, NCHUNK, 128, CHW), np.float32)

    def kc_layout(mat):
        K, N = mat.shape
        return mat.reshape(K // 128, 128, N).transpose(1, 0, 2)

    for l in range(L):
        win = inp["w_in"][l]
        c0 = np.concatenate([win[:, 2056:2312], win[:, 2048:2056]], axis=1)
        a = kc_layout(c0).reshape(128, -1)
        g = kc_layout(inp["s5_glu_w"][l]).reshape(128, -1)
        w[l, 0, :, 0:a.shape[1]] = a
        w[l, 0, :, a.shape[1]:a.shape[1] + g.shape[1]] = g
        for h in range(4):
            cols = np.concatenate([win[:, 128 * h:128 * h + 128], win[:, 512 + 128 * h:512 + 128 * h + 128],
                                   win[:, 1024 + 128 * h:1024 + 128 * h + 128],
                                   win[:, 1536 + 128 * h:1536 + 128 * h + 128]], axis=1)
            w[l, 1 + h] = kc_layout(cols).reshape(128, -1)
        w[l, 5] = kc_layout(win[:, 2312:2824]).reshape(128, -1)
        wo = inp["w_out"][l]
        w[l, 6] = kc_layout(wo[:, 0:512]).reshape(128, -1)
        w[l, 7] = kc_layout(wo[:, 512:1024]).reshape(128, -1)
        for j in range(11):
            cols = np.concatenate([inp["ffn_w1"][l][:, 256 * j:256 * j + 256], inp["ffn_w3"][l][:, 256 * j:256 * j + 256]], axis=1)
            w[l, 8 + j] = kc_layout(cols).reshape(128, -1)
        for mc in range(8):
            a = kc_layout(inp["ffn_w2"][l][:, 128 * mc:128 * mc + 128]).reshape(128, -1)
            w[l, 19 + mc, :, 0:a.shape[1]] = a
        wg = inp["pe_gate_w"][l]
        w[l, 27] = kc_layout(wg[:, 0:512]).reshape(128, -1)
        w[l, 29] = kc_layout(wg[:, 512:1024]).reshape(128, -1)
        a = kc_layout(inp["pe_w"][l]).reshape(128, -1)
        w[l, 28, :, 0:a.shape[1]] = a
    return w


def _prep_shared(inp, L):
    sh = {}
    sh["wch"] = _pack_weights(inp, L)
    sh["consts"] = _consts()
    vec = np.zeros((128, L, 40), np.float32)
    for l in range(L):
        vec[:, l, 0:8] = _fm(inp["norm_mix"][l])
        vec[:, l, 8:16] = _fm(inp["norm_ffn"][l])
        vec[:, l, 16] = inp["gdn_norm"][l]
        vec[:, l, 17:19] = _fm(inp["s5_d"][l])
        vec[:, l, 19:21] = _fm(inp["s5_glu_b"][l])
        vec[:, l, 21:23] = _fm(inp["cc_dw_b"][l])
        vec[:, l, 23:25] = _fm(inp["cc_ln_g"][l])
        vec[:, l, 25:27] = _fm(inp["cc_ln_b"][l])
    sh["vecs"] = vec
    sh["gfin"] = np.ascontiguousarray(_fm(inp["norm_final"]))
    cw = inp["gdn_conv_w"][:L]
    sh["cwg"] = np.ascontiguousarray(cw.reshape(L, 4, 12, 128).transpose(3, 0, 2, 1))
    cc = inp["cc_dw_w"][:L]
    sh["cwc"] = np.ascontiguousarray(cc.reshape(L, 31, 2, 128).transpose(3, 0, 2, 1))
    ba8 = np.zeros((8, L, 2), np.float32)
    for l in range(L):
        ba8[4:8, l, 0] = inp["gdn_dt_bias"][l]
        ba8[4:8, l, 1] = inp["gdn_a_log"][l]
    sh["ba8"] = ba8
    s5col = np.zeros((128, L, 8, 3), np.float32)
    s5row = np.zeros((128, L, 3, 8, 128), np.float32)
    s5bt = np.zeros((L, 2, 128, 8, 128), np.float32)
    s5ct = np.zeros((L, 2, 128, 8, 128), np.float32)
    for l in range(L):
        for g in range(16):
            q, g2 = g // 2, g % 2
            ps = slice(g2 * 64, g2 * 64 + 64)
            s5col[ps, l, q, 0] = inp["s5_lam_re"][l, g]
            s5col[ps, l, q, 1] = inp["s5_lam_im"][l, g]
            s5col[ps, l, q, 2] = inp["s5_log_dt"][l, g]
            s5row[:, l, 0, q, ps] = inp["s5_lam_re"][l, g][None, :]
            s5row[:, l, 1, q, ps] = inp["s5_lam_im"][l, g][None, :]
            s5row[:, l, 2, q, ps] = inp["s5_log_dt"][l, g]
            r0 = (g % 8) * 16
            s5bt[l, 0, r0:r0 + 16, q, ps] = inp["s5_b_re"][l, g].T
            s5bt[l, 1, r0:r0 + 16, q, ps] = inp["s5_b_im"][l, g].T
            s5ct[l, 0, ps, q, r0:r0 + 16] = inp["s5_c_re"][l, g].T
            s5ct[l, 1, ps, q, r0:r0 + 16] = inp["s5_c_im"][l, g].T
    sh["s5col"] = s5col
    sh["s5row"] = s5row
    sh["s5bt"] = s5bt
    sh["s5ct"] = s5ct
    return sh


def _prep_core(inp, b, NPT, L):
    m = {}
    xp = inp["x_prompt"][b, :NPT * 512]
    xs = inp["x_sample"][2 * b:2 * b + 2].reshape(64, D)
    x = np.concatenate([xp, xs], axis=0)
    m["xT"] = np.ascontiguousarray(x.reshape(-1, 8, 128).transpose(2, 1, 0))
    pp = inp["p_prompt"][:L, b, :NPT * 512]
    ps = inp["p_sample"][:L, 2 * b:2 * b + 2].reshape(L, 64, 256)
    p = np.concatenate([pp, ps], axis=1)
    m["pT"] = np.ascontiguousarray(p.reshape(L, -1, 2, 128).transpose(0, 3, 2, 1))
    sg = inp["state_gdn"][:L, 2 * b:2 * b + 2]
    m["st_gdn"] = np.ascontiguousarray(sg.transpose(0, 1, 3, 2, 4))
    gc = inp["state_gdn_conv"][:L, 2 * b:2 * b + 2]
    m["st_gc"] = np.ascontiguousarray(gc.reshape(L, 2, 3, 12, 128).transpose(0, 1, 4, 3, 2))
    s5 = inp["state_s5"][:L, 2 * b:2 * b + 2]
    m["st_s5"] = np.ascontiguousarray(s5.reshape(L, 2, 8, 2, 64, 2).transpose(0, 1, 3, 4, 2, 5).reshape(L, 2, 128, 8, 2))
    cc = inp["state_conv"][:L, 2 * b:2 * b + 2]
    m["st_cc"] = np.ascontiguousarray(cc.reshape(L, 2, 30, 2, 128).transpose(0, 1, 4, 3, 2))
    return m


_PROG_CACHE = {}


def run_cores(inp, NPT, L, cores):
    key = (NPT, L)
    if key not in _PROG_CACHE:
        _PROG_CACHE[key] = build_program(NPT, L)
    nc = _PROG_CACHE[key]
    sh = _prep_shared(inp, L)
    in_maps = []
    for b in cores:
        m = dict(sh)
        m.update(_prep_core(inp, b, NPT, L))
        in_maps.append(m)
    res = run_bass_kernel_spmd(nc, in_maps, core_ids=list(range(len(cores))))
    return res


def assemble(res, NPT, L, ncores):
    B = ncores
    y_p = np.zeros((B, NPT * 512, D), np.float32)
    y_s = np.zeros((2 * B, DSEQ, D), np.float32)
    gdn = np.zeros((L, 3 * B, 4, 128, 128), np.float32)
    gcv = np.zeros((L, 3 * B, 3, 1536), np.float32)
    s5 = np.zeros((L, 3 * B, 16, 64, 2), np.float32)
    ccv = np.zeros((L, 3 * B, 30, 256), np.float32)
    for b in range(B):
        r = res.results[b]
        yT = r["yT"]
        y = yT.transpose(2, 1, 0).reshape(-1, D)
        y_p[b] = y[:NPT * 512]
        y_s[2 * b:2 * b + 2] = y[NPT * 512:].reshape(2, DSEQ, D)
        og = r["o_gdn"]
        gdn[:, 3 * b:3 * b + 3] = og.transpose(0, 1, 3, 2, 4)
        oc = r["o_gc"]
        gcv[:, 3 * b:3 * b + 3] = oc.transpose(0, 1, 4, 3, 2).reshape(L, 3, 3, 1536)
        o5 = r["o_s5"]
        s5[:, 3 * b:3 * b + 3] = o5.reshape(L, 3, 2, 64, 8, 2).transpose(0, 1, 4, 2, 3, 5).reshape(L, 3, 16, 64, 2)
        o3 = r["o_cc"]
        ccv[:, 3 * b:3 * b + 3] = o3.transpose(0, 1, 4, 3, 2).reshape(L, 3, 30, 256)
    pi = [3 * b for b in range(B)]
    si = [3 * b + 1 + k for b in range(B) for k in range(2)]
    return (y_p, y_s, gdn[:, pi], gcv[:, pi], s5[:, pi], ccv[:, pi],
            gdn[:, si], gcv[:, si], s5[:, si], ccv[:, si])


def kernel(**inputs):
    inp = {k: np.asarray(v) for k, v in inputs.items()}
    res = run_cores(inp, SEQ // 512, L_FULL, list(range(8)))
    return assemble(res, SEQ // 512, L_FULL, 8)
```

```python
import math
import os
import numpy as np
import ml_dtypes
import concourse.bass as bass
import concourse.mybir as mybir
from concourse.bass_utils import run_bass_kernel_spmd

F32 = mybir.dt.float32
BF16 = mybir.dt.bfloat16
I32 = mybir.dt.int32
AF = mybir.ActivationFunctionType
ALU = mybir.AluOpType

D = 1024
S5POOL = os.environ.get("S5POOL", "pool")
PUMP_LEVEL = 6
PUMP_SEQ = 3
GDN_F32 = True
L_FULL = 4
SEQ = 4096
DSEQ = 32
HID = 2816
NCHUNK = 30
CHW = 4096
TWO_PI = 2.0 * math.pi

C_ID, C_MU, C_SU, C_IOTA, C_ONES = 0, 128, 256, 384, 896
NCONST = 1024


def _esize(dt):
    return 2 if dt == BF16 else 4


class Sched:
    ENG = ("pe", "act", "dve", "pool", "sp")

    def __init__(self, nc, esems, dsems):
        self.nc = nc
        self.sem = {}
        self.cur = {}
        for n, s in zip(self.ENG, esems):
            self.sem[n] = s
            self.cur[n] = 0
        self.dq = {"sp": [], "pool": []}
        half = len(dsems) // 2
        for i, s in enumerate(dsems):
            k = "d%d" % i
            self.sem[k] = s
            self.cur[k] = 0
            self.dq["sp"].append(k)
        self.dnext = {"sp": 0, "pool": 0}
        self.clock = {n: {} for n in self.ENG}
        self.prog = {n: [] for n in self.ENG}
        self.blocks = {}
        self.tokvc = {}
        self.nops = 0

    def _blocks(self, ap):
        sp = str(ap.space)
        if "DRAM" in sp:
            return []
        a = ap.ap
        pstep = a[0][0]
        es = _esize(ap.dtype)
        off = int(ap.offset)
        col = off % pstep if pstep > 0 else off
        ext = 1
        for st, cnt in a[1:]:
            ext += (cnt - 1) * abs(st)
        lo = col * es
        hi = lo + ext * es
        key = "P" if "PSUM" in sp else "S"
        g = 2048 if key == "P" else 256
        return [(key, b) for b in range(lo // g, (hi - 1) // g + 1)]

    def _deps(self, eng, reads, writes):
        need = {}
        clk = self.clock[eng]

        def add(tok):
            if tok is None:
                return
            s, v = tok
            if eng == "pe" and s == "pe":
                return
            if clk.get(s, 0) >= v:
                return
            if need.get(s, 0) < v:
                need[s] = v

        rb = set()
        wb = set()
        for ap in reads:
            rb.update(self._blocks(ap))
        for ap in writes:
            wb.update(self._blocks(ap))
        for b in rb:
            st = self.blocks.get(b)
            if st is not None:
                add(st[0])
                if b[0] == "P":
                    for s, v in st[1].items():
                        if s != eng:
                            add((s, v))
        for b in wb:
            st = self.blocks.get(b)
            if st is not None:
                add(st[0])
                for s, v in st[1].items():
                    add((s, v))
        return need, rb, wb

    def _emit_waits(self, eng, need):
        clk = self.clock[eng]
        for s, v in need.items():
            if clk.get(s, 0) >= v:
                continue
            sem = self.sem[s]
            self.prog[eng].append(("w", sem, v, s))
            vc = self.tokvc.get((s, v))
            if vc is not None:
                for k2, v2 in vc.items():
                    if clk.get(k2, 0) < v2:
                        clk[k2] = v2
            if clk.get(s, 0) < v:
                clk[s] = v

    def _commit(self, tok, eng, rb, wb):
        vc = dict(self.clock[eng])
        vc[tok[0]] = tok[1]
        self.tokvc[tok] = vc
        for b in rb:
            st = self.blocks.get(b)
            if st is None:
                st = [None, {}]
                self.blocks[b] = st
            st[1][tok[0]] = tok[1]
        for b in wb:
            self.blocks[b] = [tok, {}]
        self.nops += 1
        if len(self.tokvc) > 60000:
            keys = list(self.tokvc.keys())
            for k in keys[:30000]:
                del self.tokvc[k]

    def op(self, eng, fn, reads, writes):
        need, rb, wb = self._deps(eng, reads, writes)
        self._emit_waits(eng, need)
        self.cur[eng] += 1
        tok = (eng, self.cur[eng])
        self.prog[eng].append(("i", fn, self.sem[eng], 1))
        self._commit(tok, eng, rb, wb)
        return tok

    def dma(self, q, out, in_, extra_tokens=()):
        ring = self.dq[q]
        k = ring[self.dnext[q] % len(ring)]
        self.dnext[q] += 1
        need, rb, wb = self._deps(q, [in_], [out])
        clk = self.clock[q]
        prev = self.cur[k]
        if prev > 0 and clk.get(k, 0) < prev:
            need[k] = max(need.get(k, 0), prev)
        for s, v in extra_tokens:
            if clk.get(s, 0) < v:
                need[s] = max(need.get(s, 0), v)
        self._emit_waits(q, need)
        self.cur[k] += 16
        tok = (k, self.cur[k])
        self.prog[q].append(("i", lambda e, o=out, i=in_: e.dma_start(out=o, in_=i), self.sem[k], 16))
        self._commit(tok, q, rb, wb)
        return tok

    def barrier(self):
        for eng in self.ENG:
            need = {}
            for s, v in self.cur.items():
                if s == eng and eng == "pe":
                    continue
                if v > 0 and self.clock[eng].get(s, 0) < v:
                    need[s] = v
            self._emit_waits(eng, need)

    def finish(self):
        for eng in self.ENG:
            need = {}
            for s, v in self.cur.items():
                if s == eng:
                    continue
                if v > 0 and self.clock[eng].get(s, 0) < v:
                    need[s] = v
            self._emit_waits(eng, need)

    def finalize(self):
        waited = {n: set() for n in self.ENG}
        for eng in self.ENG:
            for it in self.prog[eng]:
                if it[0] == "w" and it[3] in waited:
                    waited[it[3]].add(it[2])
        self.tickmap = {}
        for n in self.ENG:
            self.tickmap[n] = {v: i + 1 for i, v in enumerate(sorted(waited[n]))}
        self.nsig = {n: len(waited[n]) for n in self.ENG}

    def replay(self, eng, e):
        seq = 0
        tm = self.tickmap
        for it in self.prog[eng]:
            if it[0] == "w":
                k = it[3]
                if k in tm:
                    e.wait_ge(it[1], tm[k][it[2]])
                else:
                    e.wait_ge(it[1], it[2])
            else:
                ins = it[1](e)
                if it[3] == 16:
                    ins.then_inc(it[2], 16)
                else:
                    seq += 1
                    if seq in tm[eng]:
                        ins.then_inc(it[2], 1)


class _Stop(Exception):
    pass


import os
KSTOP = int(os.environ.get("KSTOP", "0"))


def ck(n):
    if KSTOP == n:
        raise _Stop()


class Alloc:
    def __init__(self, big, words):
        self.big = big
        self.words = words
        self.off = 0

    def get(self, dtype, shape):
        n = 1
        for s in shape[1:]:
            n *= s
        w = n if dtype != BF16 else (n + 1) // 2
        w = (w + 63) // 64 * 64
        o = self.off
        self.off += w
        assert self.off <= self.words, "SBUF overflow %d > %d" % (self.off, self.words)
        ap = self.big[:, o:o + w]
        if dtype == BF16:
            ap = ap.bitcast(BF16)
        elif dtype == I32:
            ap = ap.bitcast(I32)
        ap = ap[:, 0:n]
        if len(shape) == 3:
            ap = ap.rearrange("p (a b) -> p a b", a=shape[1])
        elif len(shape) == 4:
            ap = ap.rearrange("p (a b c) -> p a b c", a=shape[1], b=shape[2])
        if shape[0] < 128:
            ap = ap[0:shape[0]]
        return ap


def bc(ap, shape):
    return ap.to_broadcast(list(shape))


def build_program(NPT, L, with_sample=True):
    nc = bass.Bass("TRN2", target_bir_lowering=False)
    NTOK = NPT * 512 + 64

    def din(name, shape, dt=F32):
        return nc.dram_tensor(name, list(shape), dt, kind="ExternalInput").ap()

    def dout(name, shape, dt=F32):
        return nc.dram_tensor(name, list(shape), dt, kind="ExternalOutput").ap()

    xT = din("xT", [128, 8, NTOK])
    pT = din("pT", [L, 128, 2, NTOK])
    wch = din("wch", [L, NCHUNK, 128, CHW])
    consts = din("consts", [128, NCONST])
    vecs = din("vecs", [128, L, 40])
    gfin = din("gfin", [128, 8])
    cwg = din("cwg", [128, L, 12, 4])
    cwc = din("cwc", [128, L, 2, 31])
    ba8 = din("ba8", [8, L, 2])
    s5col = din("s5col", [128, L, 8, 3])
    s5row = din("s5row", [128, L, 3, 8, 128])
    s5bt = din("s5bt", [L, 2, 128, 8, 128])
    s5ct = din("s5ct", [L, 2, 128, 8, 128])
    st_gdn = din("st_gdn", [L, 2, 128, 4, 128])
    st_gc = din("st_gc", [L, 2, 128, 12, 3])
    st_s5 = din("st_s5", [L, 2, 128, 8, 2])
    st_cc = din("st_cc", [L, 2, 128, 2, 30])

    yT = dout("yT", [128, 8, NTOK])
    o_gdn = dout("o_gdn", [L, 3, 128, 4, 128])
    o_gc = dout("o_gc", [L, 3, 128, 12, 3])
    o_s5 = dout("o_s5", [L, 3, 128, 8, 2])
    o_cc = dout("o_cc", [L, 3, 128, 2, 30])

    wscr = nc.dram_tensor("wscr", [L, NCHUNK, 128, CHW], BF16, kind="Internal").ap()
    s5tab = nc.dram_tensor("s5tab", [L, 8, 128, 2, 512], F32, kind="Internal").ap()
    s5mscr = nc.dram_tensor("s5mscr", [L, 128, 4, 8, 128], BF16, kind="Internal").ap()

    SBW = 53000
    NDS = 6
    import contextlib
    with contextlib.ExitStack() as es:
        big = es.enter_context(nc.sbuf_tensor("big", [128, SBW], F32))
        psum = es.enter_context(nc.psum_tensor("psum", [128, 8, 512], F32))
        esems = [es.enter_context(nc.semaphore("e_%s" % n)) for n in Sched.ENG]
        dsems = [es.enter_context(nc.semaphore("dm_%d" % i)) for i in range(NDS)]
        block = es.enter_context(nc.Block())
        S = Sched(nc, esems, dsems)
        A = Alloc(big, SBW)

        def mm(out, lhsT, rhs, start=True, stop=True):
            S.op("pe", lambda e: e.matmul(out, lhsT=lhsT, rhs=rhs, start=start, stop=stop),
                 [lhsT, rhs], [out])

        def tr(out, in_, ident):
            S.op("pe", lambda e: e.transpose(out=out, in_=in_, identity=ident), [in_, ident], [out])

        def act(out, in_, func, bias=None, scale=None):
            kw = {}
            r = [in_]
            if bias is not None:
                kw["bias"] = bias
                if not isinstance(bias, float):
                    r.append(bias)
            if scale is not None:
                kw["scale"] = scale
                if not isinstance(scale, float):
                    r.append(scale)
            S.op("act", lambda e: e.activation(out=out, in_=in_, func=func, **kw), r, [out])

        def tt(eng, out, in0, in1, op):
            S.op(eng, lambda e: e.tensor_tensor(out=out, in0=in0, in1=in1, op=op), [in0, in1], [out])

        def ts(eng, out, in0, s1, op0, s2=None, op1=None):
            r = [in0]
            if not isinstance(s1, float):
                r.append(s1)
            if s2 is not None and not isinstance(s2, float):
                r.append(s2)
            if op1 is None:
                S.op(eng, lambda e: e.tensor_scalar(out=out, in0=in0, scalar1=s1, scalar2=None, op0=op0), r, [out])
            else:
                S.op(eng, lambda e: e.tensor_scalar(out=out, in0=in0, scalar1=s1, scalar2=s2, op0=op0, op1=op1), r, [out])

        def stt(eng, out, in0, scalar, in1, op0, op1):
            r = [in0, in1]
            if not isinstance(scalar, float):
                r.append(scalar)
            S.op(eng, lambda e: e.scalar_tensor_tensor(out=out, in0=in0, scalar=scalar, in1=in1, op0=op0, op1=op1), r, [out])

        def cp(eng, out, in_):
            if eng == "act":
                act(out, in_, AF.Copy)
            else:
                S.op(eng, lambda e: e.tensor_copy(out=out, in_=in_), [in_], [out])

        def memset(eng, out, val):
            S.op(eng, lambda e: e.memset(out, val), [], [out])

        def recip(out, in_):
            S.op("dve", lambda e: e.reciprocal(out=out, in_=in_), [in_], [out])

        def scan(out, d0, d1, init):
            r = [d0, d1]
            if not isinstance(init, float):
                r.append(init)
            S.op("dve", lambda e: e.tensor_tensor_scan(out=out, data0=d0, data1=d1, initial=init,
                                                       op0=ALU.mult, op1=ALU.add), r, [out])

        pcnt = [0]
        pinned = set()

        def pbank(pin=False):
            while True:
                b = pcnt[0] % 8
                pcnt[0] += 1
                if b not in pinned:
                    break
            if pin:
                pinned.add(b)
            return b

        def PS(b, shape=None, dt=F32):
            ap = psum[:, b, :]
            if dt == BF16:
                ap = ap.bitcast(BF16)
            if shape is None:
                return ap
            n = 1
            for s in shape[1:]:
                n *= s
            ap = ap[:, 0:n]
            if len(shape) == 3:
                ap = ap.rearrange("p (a b) -> p a b", a=shape[1])
            if shape[0] < 128:
                ap = ap[0:shape[0]]
            return ap

        cst = A.get(F32, [128, NCONST])
        ident = cst[:, C_ID:C_ID + 128]
        masku = cst[:, C_MU:C_MU + 128]
        su = cst[:, C_SU:C_SU + 128]
        iota1 = cst[:, C_IOTA:C_IOTA + 512]
        ones = cst[:, C_ONES:C_ONES + 128]
        identb = A.get(BF16, [128, 128])
        onesb = A.get(BF16, [128, 128])
        vec = A.get(F32, [128, L, 40])
        gf = A.get(F32, [128, 8])
        cwg_s = A.get(F32, [128, L, 12, 4])
        cwc_s = A.get(F32, [128, L, 2, 31])
        ba8_s = A.get(F32, [8, L, 2])
        na8 = A.get(F32, [8, L])
        s5c = A.get(F32, [128, L, 8, 3])
        s5r = A.get(F32, [128, L, 8])
        hT = A.get(F32, [128, 8, 512])
        hn = A.get(BF16, [128, 8, 512])
        mix = A.get(BF16, [128, 8, 512])
        Sst = [A.get(F32, [128, 4, 128]) for _ in range(L)]
        ghist = [A.get(F32, [128, 12, 3]) for _ in range(L)]
        chist = [A.get(F32, [128, 2, 30]) for _ in range(L)]
        xst = [A.get(F32, [128, 8, 2]) for _ in range(L)]
        Sst_s = [A.get(F32, [128, 4, 128]) for _ in range(2)]
        ghist_s = [A.get(F32, [128, 12, 3]) for _ in range(2)]
        chist_s = [A.get(F32, [128, 2, 30]) for _ in range(2)]
        xst_s = [A.get(F32, [128, 8, 2]) for _ in range(2)]
        NSLOT = 5
        wslot = [A.get(BF16, [128, CHW]) for _ in range(NSLOT)]
        s5m = A.get(BF16, [128, 4, 8, 128])
        persist_end = A.off

        try:
            S.dma("sp", cst, consts)
            S.dma("sp", vec, vecs)
            S.dma("sp", gf, gfin)
            S.dma("sp", cwg_s, cwg)
            S.dma("sp", cwc_s, cwc)
            S.dma("sp", ba8_s, ba8)
            S.dma("sp", s5c, s5col)
            cp("dve", identb, ident)
            cp("dve", onesb, ones)
            act(na8, ba8_s[:, :, 1], AF.Exp)
            ts("dve", na8, na8, -1.0, ALU.mult)
            ck(1)
            for l in range(L):
                memset("dve", Sst[l], 0.0)
                memset("dve", ghist[l], 0.0)
                memset("dve", chist[l], 0.0)
                memset("dve", xst[l], 0.0)

            def range_reduce(dst, src, tmpi, tmpf):
                ts("dve", tmpi, src, 1.0 / TWO_PI, ALU.mult)
                cp("dve", tmpf, tmpi)
                stt("dve", dst, tmpf, -TWO_PI, src, ALU.mult, ALU.add)
                ts("dve", dst, dst, -3.1415925, ALU.max, 3.1415925, ALU.min)

            A.off = persist_end
            th = A.get(F32, [128, L, 8])
            dtc = A.get(F32, [128, L, 8])
            t_i = A.get(I32, [128, 512])
            t_f = A.get(F32, [128, 512])
            t_a = A.get(F32, [128, 512])
            t_b = A.get(F32, [128, 512])
            tabs = [A.get(F32, [128, 2, 512]) for _ in range(2)]
            stg = [A.get(F32, [128, CHW]) for _ in range(2)]
            stb = [A.get(BF16, [128, CHW]) for _ in range(2)]
            act(dtc, s5c[:, :, :, 2], AF.Exp)
            tt("dve", th, s5c[:, :, :, 1], dtc, ALU.mult)
            tt("dve", s5r, s5c[:, :, :, 0], dtc, ALU.mult)
            act(s5r, s5r, AF.Exp)
            thf = th.rearrange("p a b -> p (a b)")
            range_reduce(thf, thf, t_i[:, 0:L * 8], t_f[:, 0:L * 8])
            k = 0
            for l in range(L):
                for q in range(8):
                    tb = tabs[k % 2]
                    k += 1
                    ts("dve", t_a, iota1, th[:, l, q:q + 1], ALU.mult)
                    range_reduce(t_b, t_a, t_i, t_f)
                    act(tb[:, 1, :], t_b, AF.Sin)
                    ts("dve", t_a, t_b, math.pi / 2, ALU.add)
                    range_reduce(t_b, t_a, t_i, t_f)
                    act(tb[:, 0, :], t_b, AF.Sin)
                    S.dma("sp", s5tab[l, q], tb)
            ck(2)
            s5stage = A.get(F32, [128, 3, 8, 128])
            s5t = [A.get(F32, [128, 512]) for _ in range(6)]
            for l in range(L):
                S.dma("sp", s5stage[:, 0], s5row[:, l, 0])
                S.dma("sp", s5stage[:, 1], s5row[:, l, 1])
                S.dma("sp", s5stage[:, 2], s5row[:, l, 2])
                lre, lim, ldt = s5stage[:, 0], s5stage[:, 1], s5stage[:, 2]
                act(ldt, ldt, AF.Exp)
                for hq in range(2):
                    sl = slice(hq * 4, hq * 4 + 4)
                    a_re = lre[:, sl, :].rearrange("p a b -> p (a b)")
                    a_im = lim[:, sl, :].rearrange("p a b -> p (a b)")
                    a_dt = ldt[:, sl, :].rearrange("p a b -> p (a b)")
                    w0, w1, w2, w3, w4, w5 = s5t
                    tt("dve", w0, a_im, a_dt, ALU.mult)
                    range_reduce(w1, w0, t_i, t_f)
                    act(w2, w1, AF.Sin)
                    ts("dve", w0, w1, math.pi / 2, ALU.add)
                    range_reduce(w1, w0, t_i, t_f)
                    act(w3, w1, AF.Sin)
                    tt("dve", w0, a_re, a_dt, ALU.mult)
                    act(w0, w0, AF.Exp)
                    tt("dve", w3, w3, w0, ALU.mult)
                    ts("dve", w3, w3, -1.0, ALU.add)
                    tt("dve", w2, w2, w0, ALU.mult)
                    tt("dve", w0, a_re, a_re, ALU.mult)
                    tt("dve", w1, a_im, a_im, ALU.mult)
                    tt("dve", w0, w0, w1, ALU.add)
                    recip(w0, w0)
                    tt("dve", w1, w3, a_re, ALU.mult)
                    tt("dve", w4, w2, a_im, ALU.mult)
                    tt("dve", w1, w1, w4, ALU.add)
                    tt("dve", w1, w1, w0, ALU.mult)
                    tt("dve", w4, w2, a_re, ALU.mult)
                    tt("dve", w5, w3, a_im, ALU.mult)
                    tt("dve", w4, w4, w5, ALU.subtract)
                    tt("dve", w4, w4, w0, ALU.mult)
                    S.dma("sp", w2.rearrange("p (a b) -> p a b", a=4), s5bt[l, 0][:, sl, :])
                    S.dma("sp", w3.rearrange("p (a b) -> p a b", a=4), s5bt[l, 1][:, sl, :])
                    tt("dve", w0, w1, w2, ALU.mult)
                    tt("dve", w5, w4, w3, ALU.mult)
                    tt("dve", s5m[:, 0, sl, :].rearrange("p a b -> p (a b)"), w0, w5, ALU.subtract)
                    tt("dve", w0, w1, w3, ALU.mult)
                    tt("dve", w5, w4, w2, ALU.mult)
                    tt("dve", s5m[:, 1, sl, :].rearrange("p a b -> p (a b)"), w0, w5, ALU.add)
                    S.dma("sp", w2.rearrange("p (a b) -> p a b", a=4), s5ct[l, 0][:, sl, :])
                    S.dma("sp", w3.rearrange("p (a b) -> p a b", a=4), s5ct[l, 1][:, sl, :])
                    cp("dve", s5m[:, 2, sl, :].rearrange("p a b -> p (a b)"), w2)
                    ts("dve", s5m[:, 3, sl, :].rearrange("p a b -> p (a b)"), w3, -1.0, ALU.mult)

                S.dma("sp", s5mscr[l], s5m)
            ck(3)
            k = 0
            for l in range(L):
                for j in range(NCHUNK):
                    a = stg[k % 2]
                    b = stb[k % 2]
                    S.dma("sp", a, wch[l, j])
                    cp("dve" if k % 2 == 0 else "act", b, a)
                    S.dma("sp", wscr[l, j], b)
                    k += 1
            S.barrier()
            ck(4)
            A.off = persist_end

            wseq = []
            tiles = []
            for i in range(NPT):
                tiles.append(dict(tok0=i * 512, T=512, segs=[(0, 512)], C=128, last=(i == NPT - 1)))
            if with_sample:
                tiles.append(dict(tok0=NPT * 512, T=64, segs=[(1, 32), (2, 32)], C=32, last=True))
            for ti in range(len(tiles)):
                for l in range(L):
                    for j in range(NCHUNK):
                        wseq.append((l, j))
            wstate = dict(issued=0, used=0, released=0)

            def wpump():
                while wstate["issued"] < len(wseq) and wstate["issued"] - NSLOT < wstate["released"]:
                    m = wstate["issued"]
                    l_, j_ = wseq[m]
                    S.dma("sp", wslot[m % NSLOT], wscr[l_, j_])
                    wstate["issued"] += 1

            def wget():
                n = wstate["used"]
                wstate["used"] += 1
                wpump()
                assert wstate["issued"] > n
                return wslot[n % NSLOT]

            def wdone(k=1):
                wstate["released"] += k
                wpump()

            gluw = A.get(BF16, [128, 2, 256])
            sqb = A.get(BF16, [128, 512])
            sdt = A.get(F32, [128, 512])
            rst = A.get(F32, [128, 512])
            t512 = A.get(F32, [128, 512])
            sgp = A.get(F32, [128, 512])
            u5f = A.get(F32, [128, 2, 512])
            u5b = A.get(BF16, [128, 2, 512])
            selh = A.get(F32, [8, 2, 128])
            tabq = [A.get(F32, [128, 2, 512]) for _ in range(2)]
            s5t = [A.get(F32, [128, 512]) for _ in range(6)]
            xbre = A.get(BF16, [128, 512])
            xbim = A.get(BF16, [128, 512])
            ygf = u5f
            ygb = u5b
            ov0 = A.off
            xpb = A.get(BF16, [128, 3 * 2 * 520])
            dgc = A.get(BF16, [128, 3, 4, 128])
            qkv = A.get(F32, [128, 3, 512])
            zs = A.get(F32, [128, 512])
            kTb = A.get(BF16, [128, 512])
            kbT = A.get(BF16, [128, 512])
            qTb = A.get(BF16, [128, 512])
            qgT = A.get(BF16, [128, 512])
            betaB = A.get(F32, [128, 512])
            gcB = A.get(F32, [128, 512])
            egB = A.get(F32, [128, 512])
            oT = A.get(F32, [128, 512])
            sig8 = A.get(F32, [8, 512])
            g8 = A.get(F32, [8, 512])
            e8 = A.get(F32, [8, 512])
            cols = A.get(F32, [128, 4, 2])
            negc = A.get(F32, [128, 4])
            ecol = A.get(F32, [128, 4])
            bexp = A.get(F32, [128, 4])
            kdcol = A.get(F32, [128, 4])
            GD = F32 if GDN_F32 else BF16
            kbg = A.get(GD, [128, 4, 128])
            kd = A.get(BF16, [128, 4, 128])
            vbt = A.get(GD, [128, 4, 128])
            Du = A.get(F32, [128, 4, 128])
            EU = A.get(F32, [128, 4, 128])
            EUs = A.get(F32, [128, 4, 128])
            Ub = [A.get(GD, [128, 4, 128]) for _ in range(2)]
            Lb = [A.get(GD, [128, 4, 128]) for _ in range(2)]
            Rb = [A.get(GD, [128, 4, 128]) for _ in range(2)]
            Aqk = A.get(BF16, [128, 4, 128])
            nwT = A.get(GD, [128, 4, 128])
            vnew = A.get(BF16, [128, 128])
            Sbf = A.get(BF16, [128, 128])
            ov_end = A.off
            A.off = ov0
            xcb = A.get(BF16, [128, 2 * 2 * 544])
            dgcc = A.get(BF16, [128, 2, 31, 128])
            xg = A.get(F32, [128, 2, 512])
            xc16 = A.get(BF16, [128, 2, 512])
            ov_end = max(ov_end, A.off)
            A.off = ov0
            act_base = A.get(F32, [128, 22 * 256])
            actb = act_base.bitcast(BF16).rearrange("p (a b) -> p a b", a=22)
            yout = act_base[:, 0:8 * 512].rearrange("p (a b) -> p a b", a=8)
            pTf = A.get(F32, [128, 2, 512])
            pTb = A.get(BF16, [128, 2, 512])
            ov_end = max(ov_end, A.off)
            A.off = ov_end
            print("SBUF words used", A.off, "of", SBW)

            def rmsnorm_to(dst_bf, gcol_fn, T, final_out=None):
                pb = pbank()
                for kc in range(8):
                    act(sqb[:, 0:T], hT[:, kc, 0:T], AF.Square)
                    mm(PS(pb)[:, 0:T], onesb, sqb[:, 0:T], start=(kc == 0), stop=(kc == 7))
                act(sdt[:, 0:T], PS(pb)[:, 0:T], AF.Ln, bias=1e-6, scale=1.0 / D)
                act(rst[:, 0:T], sdt[:, 0:T], AF.Exp, scale=-0.5)
                for kc in range(8):
                    o = dst_bf[:, kc, 0:T] if final_out is None else final_out[:, kc, 0:T]
                    stt("dve", o, hT[:, kc, 0:T], gcol_fn(kc), rst[:, 0:T], ALU.mult, ALU.mult)

            VEC_GMIX, VEC_GFFN, VEC_GN, VEC_S5D, VEC_GLUB, VEC_DWB, VEC_LNG, VEC_LNB = 0, 8, 16, 17, 19, 21, 23, 25

            for tile in tiles:
                T = tile["T"]
                tok0 = tile["tok0"]
                segs = tile["segs"]
                nseg = len(segs)
                Ls = segs[0][1]
                C = tile["C"]
                nb = T // C
                m_lev = int(math.log2(C))
                is_sample = segs[0][0] != 0
                S.dma("sp", hT[:, :, 0:T], xT[:, :, tok0:tok0 + T])

                def seqstate(lst_p, lst_s, l, si):
                    seq = segs[si][0]
                    return lst_p[l] if seq == 0 else lst_s[seq - 1]

                for l in range(L):
                    if is_sample:
                        for si in range(nseg):
                            S.dma("sp", Sst_s[si], st_gdn[l, si])
                            S.dma("sp", ghist_s[si], st_gc[l, si])
                            S.dma("sp", chist_s[si], st_cc[l, si])
                            S.dma("sp", xst_s[si], st_s5[l, si])
                    S.dma("sp", s5m, s5mscr[l])

                    rmsnorm_to(hn, lambda kc: vec[:, l, VEC_GMIX + kc:VEC_GMIX + kc + 1], T)
                    ck(5)

                    w4 = wget()
                    w4a = w4[:, 0:8 * 264].rearrange("p (a b) -> p a b", a=8)
                    cp("dve", gluw.rearrange("p a b -> p (a b)"), w4[:, 8 * 264:8 * 264 + 512])
                    pb_ba = pbank()
                    for kc in range(8):
                        mm(PS(pb_ba)[0:8, 0:T], w4a[:, kc, 256:264], hn[:, kc, 0:T], start=(kc == 0), stop=(kc == 7))
                    pu = [pbank(), pbank()]
                    for mc in range(2):
                        for kc in range(8):
                            mm(PS(pu[mc])[:, 0:T], w4a[:, kc, mc * 128:(mc + 1) * 128], hn[:, kc, 0:T],
                               start=(kc == 0), stop=(kc == 7))
                    wdone(1)
                    ck(51)
                    act(sig8[:, 0:T], PS(pb_ba)[0:8, 0:T], AF.Sigmoid)
                    ck(52)
                    act(e8[:, 0:T], PS(pb_ba)[0:8, 0:T], AF.Exp, bias=ba8_s[:, l, 0:1])
                    ck(53)
                    act(e8[:, 0:T], e8[:, 0:T], AF.Ln, bias=1.0)
                    ck(54)
                    ts("dve", g8[:, 0:T], e8[:, 0:T], na8[:, l:l + 1], ALU.mult)
                    ck(55)
                    for mc in range(2):
                        cp("act", u5f[:, mc, 0:T], PS(pu[mc])[:, 0:T])
                        ck(56 + mc * 2)
                        cp("dve", u5b[:, mc, 0:T], PS(pu[mc])[:, 0:T])
                        ck(57 + mc * 2)
                    ck(6)

                    def s5_gen(l=l, T=T, Ls=Ls, nseg=nseg):
                        py = [pbank(pin=True), pbank(pin=True)]
                        for q in range(8):
                            kc = q // 4
                            tb = tabq[q % 2]
                            S.dma("sp", tb[:, :, 0:Ls], s5tab[l, q][:, :, 0:Ls])
                            p_re = pbank(pin=True)
                            p_im = pbank(pin=True)
                            mm(PS(p_re)[:, 0:T], s5m[:, 0, q, :], u5b[:, kc, 0:T])
                            mm(PS(p_im)[:, 0:T], s5m[:, 1, q, :], u5b[:, kc, 0:T])
                            yield
                            w0, w1, w2, w3, w4, w5 = s5t
                            for si in range(nseg):
                                cs = slice(si * Ls, (si + 1) * Ls)
                                nC = tb[:, 0, 0:Ls]
                                nS = tb[:, 1, 0:Ls]
                                xs_ = seqstate(xst, xst_s, l, si)
                                tt("dve", w0[:, cs], nC, PS(p_re)[:, cs], ALU.mult)
                                yield
                                tt("dve", w1[:, cs], nS, PS(p_im)[:, cs], ALU.mult)
                                yield
                                tt(S5POOL, w0[:, cs], w0[:, cs], w1[:, cs], ALU.add)
                                yield
                                tt("dve", w2[:, cs], nC, PS(p_im)[:, cs], ALU.mult)
                                yield
                                tt("dve", w3[:, cs], nS, PS(p_re)[:, cs], ALU.mult)
                                yield
                                tt(S5POOL, w2[:, cs], w2[:, cs], w3[:, cs], ALU.subtract)
                                yield
                                scan(w1[:, cs], bc(s5r[:, l, q:q + 1], [128, Ls]), w0[:, cs], xs_[:, q, 0:1])
                                yield
                                scan(w3[:, cs], bc(s5r[:, l, q:q + 1], [128, Ls]), w2[:, cs], xs_[:, q, 1:2])
                                yield
                                tt(S5POOL, w4[:, cs], nC, w1[:, cs], ALU.mult)
                                yield
                                tt(S5POOL, w5[:, cs], nS, w3[:, cs], ALU.mult)
                                yield
                                tt("dve", w0[:, cs], w4[:, cs], w5[:, cs], ALU.subtract)
                                cp("act", xbre[:, cs], w0[:, cs])
                                yield
                                tt(S5POOL, w4[:, cs], nS, w1[:, cs], ALU.mult)
                                yield
                                tt(S5POOL, w5[:, cs], nC, w3[:, cs], ALU.mult)
                                yield
                                tt("dve", w2[:, cs], w4[:, cs], w5[:, cs], ALU.add)
                                cp("act", xbim[:, cs], w2[:, cs])
                                yield
                                e_ = (si + 1) * Ls - 1
                                cp("dve", xs_[:, q, 0:1], w0[:, e_:e_ + 1])
                                cp("dve", xs_[:, q, 1:2], w2[:, e_:e_ + 1])
                                yield
                            pinned.discard(p_re)
                            pinned.discard(p_im)
                            mm(PS(py[kc])[:, 0:T], s5m[:, 2, q, :], xbre[:, 0:T], start=(q % 4 == 0), stop=False)
                            mm(PS(py[kc])[:, 0:T], s5m[:, 3, q, :], xbim[:, 0:T], start=False, stop=(q % 4 == 3))
                            yield
                        for kc in range(2):
                            stt("dve", ygf[:, kc, 0:T], u5f[:, kc, 0:T], vec[:, l, VEC_S5D + kc:VEC_S5D + kc + 1],
                                PS(py[kc])[:, 0:T], ALU.mult, ALU.add)
                            act(ygf[:, kc, 0:T], ygf[:, kc, 0:T], AF.Gelu_apprx_tanh)
                            cp("dve", ygb[:, kc, 0:T], ygf[:, kc, 0:T])
                            yield
                        pinned.discard(py[0])
                        pinned.discard(py[1])
                        for mc in range(2):
                            pb = pbank()
                            for kc in range(2):
                                mm(PS(pb)[:, 0:T], gluw[:, kc, mc * 128:(mc + 1) * 128], ygb[:, kc, 0:T],
                                   start=(kc == 0), stop=(kc == 1))
                            act(s5t[4][:, 0:T], PS(pb)[:, 0:T], AF.Sigmoid, bias=vec[:, l, VEC_GLUB + mc:VEC_GLUB + mc + 1])
                            tt("dve", mix[:, 4 + mc, 0:T], ygf[:, mc, 0:T], s5t[4][:, 0:T], ALU.mult)
                            yield

                    s5g = s5_gen()

                    def pump(n):
                        for _ in range(n):
                            try:
                                next(s5g)
                            except StopIteration:
                                return

                    for h in range(4):
                        wh = wget().rearrange("p (a b) -> p a b", a=8)
                        xv = xpb[:, 0:3 * nseg * (4 + Ls)].rearrange("p (a b c) -> p a b c", a=3, b=nseg)
                        for si in range(nseg):
                            gh = seqstate(ghist, ghist_s, l, si)
                            for c3 in range(3):
                                cp("dve", xv[:, c3, si, 1:4], gh[:, c3 * 4 + h, :])
                        for c3 in range(3):
                            for j in range(4):
                                act(dgc[:, c3, j, :], identb, AF.Copy, scale=cwg_s[:, l, c3 * 4 + h, j:j + 1])
                        ck(61)
                        pz = None
                        for c4 in range(4):
                            pb = pbank()
                            for kc in range(8):
                                mm(PS(pb)[:, 0:T], wh[:, kc, c4 * 128:(c4 + 1) * 128], hn[:, kc, 0:T],
                                   start=(kc == 0), stop=(kc == 7))
                            if c4 < 3:
                                for si in range(nseg):
                                    gh = seqstate(ghist, ghist_s, l, si)
                                    cp("act", xv[:, c4, si, 4:4 + Ls], PS(pb)[:, si * Ls:(si + 1) * Ls])
                                    cp("dve", gh[:, c4 * 4 + h, :], PS(pb)[:, (si + 1) * Ls - 3:(si + 1) * Ls])
                            else:
                                act(zs[:, 0:T], PS(pb)[:, 0:T], AF.Silu)
                        wdone(1)
                        ck(62)
                        for c3 in range(3):
                            pb = pbank()
                            for si in range(nseg):
                                for j in range(4):
                                    mm(PS(pb)[:, si * Ls:(si + 1) * Ls], dgc[:, c3, j, :], xv[:, c3, si, 1 + j:1 + j + Ls],
                                       start=(j == 0), stop=(j == 3))
                            act(qkv[:, c3, 0:T], PS(pb)[:, 0:T], AF.Silu)
                        ck(63)
                        for c3, dst, sc in ((0, qTb, 128.0 ** -0.5), (1, kTb, 1.0)):
                            act(sqb[:, 0:T], qkv[:, c3, 0:T], AF.Square)
                            pb = pbank()
                            mm(PS(pb)[:, 0:T], onesb, sqb[:, 0:T])
                            act(sdt[:, 0:T], PS(pb)[:, 0:T], AF.Ln, bias=1e-6)
                            act(rst[:, 0:T], sdt[:, 0:T], AF.Exp, scale=-0.5)
                            stt("dve", dst[:, 0:T], qkv[:, c3, 0:T], sc, rst[:, 0:T], ALU.mult, ALU.mult)
                        ck(64)
                        ts("dve", selh[:, 0, :], ones[0:8, :], ident[0:8, h:h + 1], ALU.mult)
                        ts("dve", selh[:, 1, :], ones[0:8, :], ident[0:8, 4 + h:5 + h], ALU.mult)
                        pbb = pbank()
                        mm(PS(pbb)[:, 0:T], selh[:, 0, :], sig8[:, 0:T])
                        pbg = pbank()
                        mm(PS(pbg)[:, 0:T], selh[:, 1, :], g8[:, 0:T])
                        cp("act", betaB[:, 0:T], PS(pbb)[:, 0:T])
                        for b in range(nb):
                            scan(gcB[:, b * C:(b + 1) * C], ones[:, 0:C], PS(pbg)[:, b * C:(b + 1) * C], 0.0)
                        act(egB[:, 0:T], gcB[:, 0:T], AF.Exp)
                        tt("dve", kbT[:, 0:T], kTb[:, 0:T], betaB[:, 0:T], ALU.mult)
                        tt("dve", qgT[:, 0:T], qTb[:, 0:T], egB[:, 0:T], ALU.mult)
                        ck(65)
                        pbc = pbank()
                        pcv = PS(pbc)[:, 0:nb * 2].rearrange("p (a b) -> p a b", a=nb)[0:C]
                        for b in range(nb):
                            mm(pcv[:, b, 0:1], gcB[:, b * C:(b + 1) * C], ident[:, 0:1])
                            mm(pcv[:, b, 1:2], betaB[:, b * C:(b + 1) * C], ident[:, 0:1])
                        cv = cols[0:C, 0:nb, :]
                        cp("dve", cv, pcv)
                        ts("dve", negc[0:C, 0:nb], cv[:, :, 0], -1.0, ALU.mult)
                        act(ecol[0:C, 0:nb], cv[:, :, 0], AF.Exp)
                        tt("dve", bexp[0:C, 0:nb], cv[:, :, 1], ecol[0:C, 0:nb], ALU.mult)
                        for b in range(nb):
                            act(kdcol[0:C, b:b + 1], cv[:, b, 0:1], AF.Exp, bias=gcB[0:C, (b + 1) * C - 1:(b + 1) * C], scale=-1.0)
                        ck(66)
                        pkt = pbank()
                        pk = PS(pkt, dt=BF16)[:, 0:nb * 128].rearrange("p (a b) -> p a b", a=nb)[0:C]
                        for b in range(nb):
                            tr(pk[:, b, :], kTb[:, b * C:(b + 1) * C], identb)
                        pvt = pbank()
                        pv = PS(pvt)[:, 0:nb * 128].rearrange("p (a b) -> p a b", a=nb)[0:C]
                        for b in range(nb):
                            tr(pv[:, b, :], qkv[:, 2, b * C:(b + 1) * C], ident)
                        for b in range(nb):
                            act(kbg[0:C, b, :], pk[:, b, :], AF.Copy, scale=bexp[0:C, b:b + 1])
                            act(kd[0:C, b, :], pk[:, b, :], AF.Copy, scale=kdcol[0:C, b:b + 1])
                            act(vbt[0:C, b, :], pv[:, b, :], AF.Copy, scale=cv[:, b, 1:2])
                        ck(67)
                        pg = pbank()
                        pgv = PS(pg)[:, 0:nb * C].rearrange("p (a b) -> p a b", a=nb)[0:C]
                        pa = pbank()
                        pav = PS(pa)[:, 0:nb * C].rearrange("p (a b) -> p a b", a=nb)[0:C]
                        for b in range(nb):
                            mm(pgv[:, b, :], kTb[:, b * C:(b + 1) * C], kbT[:, b * C:(b + 1) * C])
                            mm(pav[:, b, :], kTb[:, b * C:(b + 1) * C], qTb[:, b * C:(b + 1) * C])
                        Duv = Du[0:C, 0:nb, 0:C]
                        EUv = EU[0:C, 0:nb, 0:C]
                        EUsv = EUs[0:C, 0:nb, 0:C]
                        for b in range(nb):
                            stt("dve", Duv[:, b, :], gcB[0:C, b * C:(b + 1) * C], negc[0:C, b:b + 1], masku[0:C, 0:C],
                                ALU.add, ALU.add)
                        act(EUv, Duv, AF.Exp)
                        for b in range(nb):
                            tt("dve", EUsv[:, b, :], EUv[:, b, :], su[0:C, 0:C], ALU.mult)
                        U0 = Ub[0][0:C, 0:nb, 0:C]
                        tt("dve", U0, pgv, EUsv, ALU.mult)
                        Aqv = Aqk[0:C, 0:nb, 0:C]
                        tt("dve", Aqv, pav, EUv, ALU.mult)
                        ck(68)
                        plt = pbank()
                        idg = ident if GDN_F32 else identb
                        pl = PS(plt, dt=GD)[:, 0:nb * C].rearrange("p (a b) -> p a b", a=nb)[0:C]
                        for b in range(nb):
                            tr(pl[:, b, :], U0[:, b, :], idg[0:C, 0:C])
                        L0 = Lb[0][0:C, 0:nb, 0:C]
                        cp("act", L0, pl)
                        R0 = Rb[0][0:C, 0:nb, 0:C]
                        for b in range(nb):
                            stt("dve", R0[:, b, :], U0[:, b, :], -1.0, idg[0:C, 0:C], ALU.mult, ALU.add)
                        ck(69)
                        cur = 0
                        for lev in range(1, m_lev):
                            Up = Ub[cur][0:C, 0:nb, 0:C]
                            Lp = Lb[cur][0:C, 0:nb, 0:C]
                            Rp = Rb[cur][0:C, 0:nb, 0:C]
                            Un = Ub[1 - cur][0:C, 0:nb, 0:C]
                            Ln = Lb[1 - cur][0:C, 0:nb, 0:C]
                            Rn = Rb[1 - cur][0:C, 0:nb, 0:C]
                            p1 = pbank()
                            p1v = PS(p1)[:, 0:nb * C].rearrange("p (a b) -> p a b", a=nb)[0:C]
                            for b in range(nb):
                                mm(p1v[:, b, :], Up[:, b, :], Lp[:, b, :])
                            cp("act", Ln, p1v)
                            if lev < m_lev - 1:
                                p2 = pbank()
                                p2v = PS(p2)[:, 0:nb * C].rearrange("p (a b) -> p a b", a=nb)[0:C]
                                for b in range(nb):
                                    mm(p2v[:, b, :], Lp[:, b, :], Up[:, b, :])
                                cp("dve", Un, p2v)
                            p3 = pbank()
                            p3v = PS(p3)[:, 0:nb * C].rearrange("p (a b) -> p a b", a=nb)[0:C]
                            for b in range(nb):
                                mm(p3v[:, b, :], idg[0:C, 0:C], Rp[:, b, :], start=True, stop=False)
                                mm(p3v[:, b, :], Ln[:, b, :], Rp[:, b, :], start=False, stop=True)
                            cp("dve", Rn, p3v)
                            cur = 1 - cur
                            pump(PUMP_LEVEL)
                        R = Rb[cur][0:C, 0:nb, 0:C]
                        ck(70)
                        pw = pbank()
                        pwv = PS(pw)[:, 0:nb * C].rearrange("p (a b) -> p a b", a=nb)
                        for b in range(nb):
                            mm(pwv[:, b, :], kbg[0:C, b, :], R[:, b, :])
                        nwv = nwT[:, 0:nb, 0:C]
                        ts("dve", nwv, pwv, -1.0, ALU.mult)
                        ck(71)
                        for b in range(nb):
                            si = (b * C) // Ls
                            Sm = seqstate(Sst, Sst_s, l, si)[:, h, :]
                            if b == 0 or (b * C) % Ls == 0:
                                cp("act", Sbf, Sm)
                            p_v = pbank()
                            pvn = PS(p_v)[0:C, 0:128]
                            mm(pvn, R[:, b, :], vbt[0:C, b, :], start=True, stop=False)
                            mm(pvn, nwv[:, b, :], Sm if GDN_F32 else Sbf, start=False, stop=True)
                            cp("act", vnew[0:C, :], pvn)
                            p_o = pbank()
                            po = PS(p_o)[:, 0:C]
                            mm(po, Sbf, qgT[:, b * C:(b + 1) * C], start=True, stop=False)
                            mm(po, vnew[0:C, :], Aqv[:, b, :], start=False, stop=True)
                            cp("dve", oT[:, b * C:(b + 1) * C], po)
                            p_s = pbank()
                            psn = PS(p_s)[:, 0:128]
                            mm(psn, kd[0:C, b, :], vnew[0:C, :])
                            stt("dve", Sm, Sm, egB[:, (b + 1) * C - 1:(b + 1) * C], psn, ALU.mult, ALU.add)
                            if b + 1 < nb and ((b + 1) * C) % Ls != 0:
                                cp("act", Sbf, Sm)
                            pump(PUMP_SEQ)
                        ck(72)
                        act(sqb[:, 0:T], oT[:, 0:T], AF.Square)
                        pb = pbank()
                        mm(PS(pb)[:, 0:T], onesb, sqb[:, 0:T])
                        act(sdt[:, 0:T], PS(pb)[:, 0:T], AF.Ln, bias=1e-6, scale=1.0 / 128)
                        act(rst[:, 0:T], sdt[:, 0:T], AF.Exp, scale=-0.5)
                        stt("dve", t512[:, 0:T], oT[:, 0:T], vec[:, l, VEC_GN:VEC_GN + 1], rst[:, 0:T], ALU.mult, ALU.mult)
                        tt("dve", mix[:, h, 0:T], t512[:, 0:T], zs[:, 0:T], ALU.mult)
                        ck(7)

                    for si in range(nseg):
                        seq = segs[si][0]
                        if tile["last"]:
                            S.dma("sp", o_gdn[l, seq], seqstate(Sst, Sst_s, l, si))
                            S.dma("sp", o_gc[l, seq], seqstate(ghist, ghist_s, l, si))

                    ck(8)
                    for _ in s5g:
                        pass
                    if tile["last"]:
                        for si in range(nseg):
                            S.dma("sp", o_s5[l, segs[si][0]], seqstate(xst, xst_s, l, si))

                    ck(9)
                    wc = wget().rearrange("p (a b) -> p a b", a=8)
                    xcv = xcb[:, 0:2 * nseg * (30 + Ls)].rearrange("p (a b c) -> p a b c", a=2, b=nseg)
                    for kc in range(2):
                        for j in range(31):
                            act(dgcc[:, kc, j, :], identb, AF.Copy, scale=cwc_s[:, l, kc, j:j + 1])
                    for si in range(nseg):
                        ch = seqstate(chist, chist_s, l, si)
                        cp("dve", xcv[:, :, si, 0:30], ch)
                    pa_ = [pbank(), pbank()]
                    pg_ = [pbank(), pbank()]
                    for c4 in range(4):
                        pb = pa_[c4] if c4 < 2 else pg_[c4 - 2]
                        for kc in range(8):
                            mm(PS(pb)[:, 0:T], wc[:, kc, c4 * 128:(c4 + 1) * 128], hn[:, kc, 0:T],
                               start=(kc == 0), stop=(kc == 7))
                    wdone(1)
                    for kc in range(2):
                        act(sgp[:, 0:T], PS(pg_[kc])[:, 0:T], AF.Sigmoid)
                        tt("dve", xg[:, kc, 0:T], PS(pa_[kc])[:, 0:T], sgp[:, 0:T], ALU.mult)
                        for si in range(nseg):
                            ch = seqstate(chist, chist_s, l, si)
                            cp("act", xcv[:, kc, si, 30:30 + Ls], xg[:, kc, si * Ls:(si + 1) * Ls])
                            cp("dve", ch[:, kc, :], xg[:, kc, (si + 1) * Ls - 30:(si + 1) * Ls])
                    pcv_ = [pbank(), pbank()]
                    for kc in range(2):
                        for si in range(nseg):
                            for j in range(31):
                                mm(PS(pcv_[kc])[:, si * Ls:(si + 1) * Ls], dgcc[:, kc, j, :], xcv[:, kc, si, j:j + Ls],
                                   start=(j == 0), stop=(j == 30))
                    xc = xg
                    for kc in range(2):
                        act(xc[:, kc, 0:T], PS(pcv_[kc])[:, 0:T], AF.Identity, bias=vec[:, l, VEC_DWB + kc:VEC_DWB + kc + 1])
                        cp("dve", xc16[:, kc, 0:T], xc[:, kc, 0:T])
                    pm = pbank()
                    for kc in range(2):
                        mm(PS(pm)[:, 0:T], onesb, xc16[:, kc, 0:T], start=(kc == 0), stop=(kc == 1))
                    for kc in range(2):
                        stt("dve", xc[:, kc, 0:T], PS(pm)[:, 0:T], -1.0 / 256, xc[:, kc, 0:T], ALU.mult, ALU.add)
                        act(xc16[:, kc, 0:T], xc[:, kc, 0:T], AF.Square)
                    pv_ = pbank()
                    for kc in range(2):
                        mm(PS(pv_)[:, 0:T], onesb, xc16[:, kc, 0:T], start=(kc == 0), stop=(kc == 1))
                    act(sdt[:, 0:T], PS(pv_)[:, 0:T], AF.Ln, bias=1e-5, scale=1.0 / 256)
                    act(rst[:, 0:T], sdt[:, 0:T], AF.Exp, scale=-0.5)
                    for kc in range(2):
                        stt("dve", t512[:, 0:T], xc[:, kc, 0:T], vec[:, l, VEC_LNG + kc:VEC_LNG + kc + 1], rst[:, 0:T],
                            ALU.mult, ALU.mult)
                        act(mix[:, 6 + kc, 0:T], t512[:, 0:T], AF.Silu, bias=vec[:, l, VEC_LNB + kc:VEC_LNB + kc + 1])
                    if tile["last"]:
                        for si in range(nseg):
                            S.dma("sp", o_cc[l, segs[si][0]], seqstate(chist, chist_s, l, si))

                    ck(10)
                    for half in range(2):
                        wo = wget().rearrange("p (a b) -> p a b", a=8)
                        for m4 in range(4):
                            mc = half * 4 + m4
                            pb = pbank()
                            for kc in range(8):
                                mm(PS(pb)[:, 0:T], wo[:, kc, m4 * 128:(m4 + 1) * 128], mix[:, kc, 0:T],
                                   start=(kc == 0), stop=(kc == 7))
                            tt("dve", hT[:, mc, 0:T], hT[:, mc, 0:T], PS(pb)[:, 0:T], ALU.add)
                        wdone(1)

                    ck(11)
                    rmsnorm_to(hn, lambda kc: vec[:, l, VEC_GFFN + kc:VEC_GFFN + kc + 1], T)
                    for j in range(11):
                        wf = wget().rearrange("p (a b) -> p a b", a=8)
                        for c2 in range(2):
                            hc = j * 2 + c2
                            p1 = pbank()
                            p3 = pbank()
                            for kc in range(8):
                                mm(PS(p1)[:, 0:T], wf[:, kc, c2 * 128:(c2 + 1) * 128], hn[:, kc, 0:T],
                                   start=(kc == 0), stop=(kc == 7))
                            for kc in range(8):
                                mm(PS(p3)[:, 0:T], wf[:, kc, 256 + c2 * 128:256 + (c2 + 1) * 128], hn[:, kc, 0:T],
                                   start=(kc == 0), stop=(kc == 7))
                            act(sgp[:, 0:T], PS(p1)[:, 0:T], AF.Silu)
                            tt("dve", actb[:, hc, 0:T], sgp[:, 0:T], PS(p3)[:, 0:T], ALU.mult)
                        wdone(1)
                    for mc in range(8):
                        w2c = wget()[:, 0:22 * 128].rearrange("p (a b) -> p a b", a=22)
                        pb = pbank()
                        for kc in range(22):
                            mm(PS(pb)[:, 0:T], w2c[:, kc, :], actb[:, kc, 0:T], start=(kc == 0), stop=(kc == 21))
                        tt("dve", hT[:, mc, 0:T], hT[:, mc, 0:T], PS(pb)[:, 0:T], ALU.add)
                        wdone(1)

                    ck(12)
                    S.dma("sp", pTf[:, :, 0:T], pT[l][:, :, tok0:tok0 + T])
                    for kc in range(8):
                        cp("act", hn[:, kc, 0:T], hT[:, kc, 0:T])
                    cp("dve", pTb[:, :, 0:T], pTf[:, :, 0:T])
                    wg0 = wget().rearrange("p (a b) -> p a b", a=8)
                    wpe = wget()[:, 0:2048].rearrange("p (a b) -> p a b", a=2)
                    wg1 = None
                    for mc in range(8):
                        if mc == 4:
                            wdone(1)
                            wg1 = wget().rearrange("p (a b) -> p a b", a=8)
                        wg = wg0 if mc < 4 else wg1
                        m4 = mc % 4
                        pb = pbank()
                        for kc in range(8):
                            mm(PS(pb)[:, 0:T], wg[:, kc, m4 * 128:(m4 + 1) * 128], hn[:, kc, 0:T],
                               start=(kc == 0), stop=(kc == 7))
                        act(sgp[:, 0:T], PS(pb)[:, 0:T], AF.Sigmoid)
                        pb2 = pbank()
                        for kc in range(2):
                            mm(PS(pb2)[:, 0:T], wpe[:, kc, mc * 128:(mc + 1) * 128], pTb[:, kc, 0:T],
                               start=(kc == 0), stop=(kc == 1))
                        tt("dve", t512[:, 0:T], PS(pb2)[:, 0:T], sgp[:, 0:T], ALU.mult)
                        tt("dve", hT[:, mc, 0:T], hT[:, mc, 0:T], t512[:, 0:T], ALU.add)
                    wdone(2)
                    ck(13)

                rmsnorm_to(None, lambda kc: gf[:, kc:kc + 1], T, final_out=yout)
                S.dma("sp", yT[:, :, tok0:tok0 + T], yout[:, :, 0:T])

        except _Stop:
            print('STOPPED at', KSTOP)
            KT = int(os.environ.get("KTAIL", "0"))
            if KT == 1:
                memset("dve", t512, 0.0)
            elif KT == 2:
                cp("act", t512, sdt)
            elif KT == 3:
                mm(PS(7), onesb, sqb)
                cp("dve", t512, PS(7))
            elif KT == 4:
                mm(PS(7), onesb, sqb)
                cp("act", t512, PS(7))
        S.finish()
        S.finalize()
        print("ops recorded:", S.nops, {k: len(v) for k, v in S.prog.items()}, "signals", S.nsig)

        @block.tensor
        def _(e):
            S.replay("pe", e)

        @block.scalar
        def _(e):
            S.replay("act", e)

        @block.vector
        def _(e):
            S.replay("dve", e)

        @block.gpsimd
        def _(e):
            S.replay("pool", e)

        @block.sync
        def _(e):
            S.replay("sp", e)
    return nc


def _consts():
    c = np.zeros((128, NCONST), np.float32)
    c[:, C_ID:C_ID + 128] = np.eye(128)
    s = np.arange(128)[:, None]
    cc = np.arange(128)[None, :]
    c[:, C_MU:C_MU + 128] = np.where(s <= cc, 0.0, -30000.0)
    c[:, C_SU:C_SU + 128] = (s < cc).astype(np.float32)
    c[:, C_IOTA:C_IOTA + 512] = np.arange(1, 513)[None, :]
    c[:, C_ONES:C_ONES + 128] = 1.0
    return c


def _fm(v):
    C = v.shape[-1]
    r = v.reshape(v.shape[:-1] + (C // 128, 128))
    return np.moveaxis(r, -1, 0)


def _pack_weights(inp, L):
    w = np.zeros((L, NCHUNK, 128, CHW), np.float32)

    def kc_layout(mat):
        K, N = mat.shape
        return mat.reshape(K // 128, 128, N).transpose(1, 0, 2)

    for l in range(L):
        win = inp["w_in"][l]
        c0 = np.concatenate([win[:, 2056:2312], win[:, 2048:2056]], axis=1)
        a = kc_layout(c0).reshape(128, -1)
        g = kc_layout(inp["s5_glu_w"][l]).reshape(128, -1)
        w[l, 0, :, 0:a.shape[1]] = a
        w[l, 0, :, a.shape[1]:a.shape[1] + g.shape[1]] = g
        for h in range(4):
            cols = np.concatenate([win[:, 128 * h:128 * h + 128], win[:, 512 + 128 * h:512 + 128 * h + 128],
                                   win[:, 1024 + 128 * h:1024 + 128 * h + 128],
                                   win[:, 1536 + 128 * h:1536 + 128 * h + 128]], axis=1)
            w[l, 1 + h] = kc_layout(cols).reshape(128, -1)
        w[l, 5] = kc_layout(win[:, 2312:2824]).reshape(128, -1)
        wo = inp["w_out"][l]
        w[l, 6] = kc_layout(wo[:, 0:512]).reshape(128, -1)
        w[l, 7] = kc_layout(wo[:, 512:1024]).reshape(128, -1)
        for j in range(11):
            cols = np.concatenate([inp["ffn_w1"][l][:, 256 * j:256 * j + 256], inp["ffn_w3"][l][:, 256 * j:256 * j + 256]], axis=1)
            w[l, 8 + j] = kc_layout(cols).reshape(128, -1)
        for mc in range(8):
            a = kc_layout(inp["ffn_w2"][l][:, 128 * mc:128 * mc + 128]).reshape(128, -1)
            w[l, 19 + mc, :, 0:a.shape[1]] = a
        wg = inp["pe_gate_w"][l]
        w[l, 27] = kc_layout(wg[:, 0:512]).reshape(128, -1)
        w[l, 29] = kc_layout(wg[:, 512:1024]).reshape(128, -1)
        a = kc_layout(inp["pe_w"][l]).reshape(128, -1)
        w[l, 28, :, 0:a.shape[1]] = a
    return w


def _prep_shared(inp, L):
    sh = {}
    sh["wch"] = _pack_weights(inp, L)
    sh["consts"] = _consts()
    vec = np.zeros((128, L, 40), np.float32)
    for l in range(L):
        vec[:, l, 0:8] = _fm(inp["norm_mix"][l])
        vec[:, l, 8:16] = _fm(inp["norm_ffn"][l])
        vec[:, l, 16] = inp["gdn_norm"][l]
        vec[:, l, 17:19] = _fm(inp["s5_d"][l])
        vec[:, l, 19:21] = _fm(inp["s5_glu_b"][l])
        vec[:, l, 21:23] = _fm(inp["cc_dw_b"][l])
        vec[:, l, 23:25] = _fm(inp["cc_ln_g"][l])
        vec[:, l, 25:27] = _fm(inp["cc_ln_b"][l])
    sh["vecs"] = vec
    sh["gfin"] = np.ascontiguousarray(_fm(inp["norm_final"]))
    cw = inp["gdn_conv_w"][:L]
    sh["cwg"] = np.ascontiguousarray(cw.reshape(L, 4, 12, 128).transpose(3, 0, 2, 1))
    cc = inp["cc_dw_w"][:L]
    sh["cwc"] = np.ascontiguousarray(cc.reshape(L, 31, 2, 128).transpose(3, 0, 2, 1))
    ba8 = np.zeros((8, L, 2), np.float32)
    for l in range(L):
        ba8[4:8, l, 0] = inp["gdn_dt_bias"][l]
        ba8[4:8, l, 1] = inp["gdn_a_log"][l]
    sh["ba8"] = ba8
    s5col = np.zeros((128, L, 8, 3), np.float32)
    s5row = np.zeros((128, L, 3, 8, 128), np.float32)
    s5bt = np.zeros((L, 2, 128, 8, 128), np.float32)
    s5ct = np.zeros((L, 2, 128, 8, 128), np.float32)
    for l in range(L):
        for g in range(16):
            q, g2 = g // 2, g % 2
            ps = slice(g2 * 64, g2 * 64 + 64)
            s5col[ps, l, q, 0] = inp["s5_lam_re"][l, g]
            s5col[ps, l, q, 1] = inp["s5_lam_im"][l, g]
            s5col[ps, l, q, 2] = inp["s5_log_dt"][l, g]
            s5row[:, l, 0, q, ps] = inp["s5_lam_re"][l, g][None, :]
            s5row[:, l, 1, q, ps] = inp["s5_lam_im"][l, g][None, :]
            s5row[:, l, 2, q, ps] = inp["s5_log_dt"][l, g]
            r0 = (g % 8) * 16
            s5bt[l, 0, r0:r0 + 16, q, ps] = inp["s5_b_re"][l, g].T
            s5bt[l, 1, r0:r0 + 16, q, ps] = inp["s5_b_im"][l, g].T
            s5ct[l, 0, ps, q, r0:r0 + 16] = inp["s5_c_re"][l, g].T
            s5ct[l, 1, ps, q, r0:r0 + 16] = inp["s5_c_im"][l, g].T
    sh["s5col"] = s5col
    sh["s5row"] = s5row
    sh["s5bt"] = s5bt
    sh["s5ct"] = s5ct
    return sh


def _prep_core(inp, b, NPT, L):
    m = {}
    xp = inp["x_prompt"][b, :NPT * 512]
    xs = inp["x_sample"][2 * b:2 * b + 2].reshape(64, D)
    x = np.concatenate([xp, xs], axis=0)
    m["xT"] = np.ascontiguousarray(x.reshape(-1, 8, 128).transpose(2, 1, 0))
    pp = inp["p_prompt"][:L, b, :NPT * 512]
    ps = inp["p_sample"][:L, 2 * b:2 * b + 2].reshape(L, 64, 256)
    p = np.concatenate([pp, ps], axis=1)
    m["pT"] = np.ascontiguousarray(p.reshape(L, -1, 2, 128).transpose(0, 3, 2, 1))
    sg = inp["state_gdn"][:L, 2 * b:2 * b + 2]
    m["st_gdn"] = np.ascontiguousarray(sg.transpose(0, 1, 3, 2, 4))
    gc = inp["state_gdn_conv"][:L, 2 * b:2 * b + 2]
    m["st_gc"] = np.ascontiguousarray(gc.reshape(L, 2, 3, 12, 128).transpose(0, 1, 4, 3, 2))
    s5 = inp["state_s5"][:L, 2 * b:2 * b + 2]
    m["st_s5"] = np.ascontiguousarray(s5.reshape(L, 2, 8, 2, 64, 2).transpose(0, 1, 3, 4, 2, 5).reshape(L, 2, 128, 8, 2))
    cc = inp["state_conv"][:L, 2 * b:2 * b + 2]
    m["st_cc"] = np.ascontiguousarray(cc.reshape(L, 2, 30, 2, 128).transpose(0, 1, 4, 3, 2))
    return m


_PROG_CACHE = {}


def run_cores(inp, NPT, L, cores):
    key = (NPT, L)
    if key not in _PROG_CACHE:
        _PROG_CACHE[key] = build_program(NPT, L)
    nc = _PROG_CACHE[key]
    sh = _prep_shared(inp, L)
    in_maps = []
    for b in cores:
        m = dict(sh)
        m.update(_prep_core(inp, b, NPT, L))
        in_maps.append(m)
    res = run_bass_kernel_spmd(nc, in_maps, core_ids=list(range(len(cores))))
    return res


def assemble(res, NPT, L, ncores):
    B = ncores
    y_p = np.zeros((B, NPT * 512, D), np.float32)
    y_s = np.zeros((2 * B, DSEQ, D), np.float32)
    gdn = np.zeros((L, 3 * B, 4, 128, 128), np.float32)
    gcv = np.zeros((L, 3 * B, 3, 1536), np.float32)
    s5 = np.zeros((L, 3 * B, 16, 64, 2), np.float32)
    ccv = np.zeros((L, 3 * B, 30, 256), np.float32)
    for b in range(B):
        r = res.results[b]
        yT = r["yT"]
        y = yT.transpose(2, 1, 0).reshape(-1, D)
        y_p[b] = y[:NPT * 512]
        y_s[2 * b:2 * b + 2] = y[NPT * 512:].reshape(2, DSEQ, D)
        og = r["o_gdn"]
        gdn[:, 3 * b:3 * b + 3] = og.transpose(0, 1, 3, 2, 4)
        oc = r["o_gc"]
        gcv[:, 3 * b:3 * b + 3] = oc.transpose(0, 1, 4, 3, 2).reshape(L, 3, 3, 1536)
        o5 = r["o_s5"]
        s5[:, 3 * b:3 * b + 3] = o5.reshape(L, 3, 2, 64, 8, 2).transpose(0, 1, 4, 2, 3, 5).reshape(L, 3, 16, 64, 2)
        o3 = r["o_cc"]
        ccv[:, 3 * b:3 * b + 3] = o3.transpose(0, 1, 4, 3, 2).reshape(L, 3, 30, 256)
    pi = [3 * b for b in range(B)]
    si = [3 * b + 1 + k for b in range(B) for k in range(2)]
    return (y_p, y_s, gdn[:, pi], gcv[:, pi], s5[:, pi], ccv[:, pi],
            gdn[:, si], gcv[:, si], s5[:, si], ccv[:, si])


def kernel(**inputs):
    inp = {k: np.asarray(v) for k, v in inputs.items()}
    res = run_cores(inp, SEQ // 512, L_FULL, list(range(8)))
    return assemble(res, SEQ // 512, L_FULL, 8)
```

```python
import math
import os
import numpy as np
import ml_dtypes
import concourse.bass as bass
import concourse.mybir as mybir
from concourse.bass_utils import run_bass_kernel_spmd

F32 = mybir.dt.float32
BF16 = mybir.dt.bfloat16
I32 = mybir.dt.int32
AF = mybir.ActivationFunctionType
ALU = mybir.AluOpType

D = 1024
S5POOL = os.environ.get("S5POOL", "pool")
PUMP_LEVEL = 6
PUMP_SEQ = 3
GDN_F32 = True
L_FULL = 4
SEQ = 4096
DSEQ = 32
HID = 2816
NCHUNK = 30
CHW = 4096
TWO_PI = 2.0 * math.pi

C_ID, C_MU, C_SU, C_IOTA, C_ONES = 0, 128, 256, 384, 896
NCONST = 1024


def _esize(dt):
    return 2 if dt == BF16 else 4


class Sched:
    ENG = ("pe", "act", "dve", "pool", "sp")

    def __init__(self, nc, esems, dsems):
        self.nc = nc
        self.sem = {}
        self.cur = {}
        for n, s in zip(self.ENG, esems):
            self.sem[n] = s
            self.cur[n] = 0
        self.dq = {"sp": [], "pool": []}
        half = len(dsems) // 2
        for i, s in enumerate(dsems):
            k = "d%d" % i
            self.sem[k] = s
            self.cur[k] = 0
            self.dq["sp"].append(k)
        self.dnext = {"sp": 0, "pool": 0}
        self.clock = {n: {} for n in self.ENG}
        self.prog = {n: [] for n in self.ENG}
        self.blocks = {}
        self.tokvc = {}
        self.nops = 0

    def _blocks(self, ap):
        sp = str(ap.space)
        if "DRAM" in sp:
            return []
        a = ap.ap
        pstep = a[0][0]
        es = _esize(ap.dtype)
        off = int(ap.offset)
        col = off % pstep if pstep > 0 else off
        ext = 1
        for st, cnt in a[1:]:
            ext += (cnt - 1) * abs(st)
        lo = col * es
        hi = lo + ext * es
        key = "P" if "PSUM" in sp else "S"
        g = 2048 if key == "P" else 256
        return [(key, b) for b in range(lo // g, (hi - 1) // g + 1)]

    def _deps(self, eng, reads, writes):
        need = {}
        clk = self.clock[eng]

        def add(tok):
            if tok is None:
                return
            s, v = tok
            if eng == "pe" and s == "pe":
                return
            if clk.get(s, 0) >= v:
                return
            if need.get(s, 0) < v:
                need[s] = v

        rb = set()
        wb = set()
        for ap in reads:
            rb.update(self._blocks(ap))
        for ap in writes:
            wb.update(self._blocks(ap))
        for b in rb:
            st = self.blocks.get(b)
            if st is not None:
                add(st[0])
                if b[0] == "P":
                    for s, v in st[1].items():
                        if s != eng:
                            add((s, v))
        for b in wb:
            st = self.blocks.get(b)
            if st is not None:
                add(st[0])
                for s, v in st[1].items():
                    add((s, v))
        return need, rb, wb

    def _emit_waits(self, eng, need):
        clk = self.clock[eng]
        for s, v in need.items():
            if clk.get(s, 0) >= v:
                continue
            sem = self.sem[s]
            self.prog[eng].append(("w", sem, v, s))
            vc = self.tokvc.get((s, v))
            if vc is not None:
                for k2, v2 in vc.items():
                    if clk.get(k2, 0) < v2:
                        clk[k2] = v2
            if clk.get(s, 0) < v:
                clk[s] = v

    def _commit(self, tok, eng, rb, wb):
        vc = dict(self.clock[eng])
        vc[tok[0]] = tok[1]
        self.tokvc[tok] = vc
        for b in rb:
            st = self.blocks.get(b)
            if st is None:
                st = [None, {}]
                self.blocks[b] = st
            st[1][tok[0]] = tok[1]
        for b in wb:
            self.blocks[b] = [tok, {}]
        self.nops += 1
        if len(self.tokvc) > 60000:
            keys = list(self.tokvc.keys())
            for k in keys[:30000]:
                del self.tokvc[k]

    def op(self, eng, fn, reads, writes):
        need, rb, wb = self._deps(eng, reads, writes)
        self._emit_waits(eng, need)
        self.cur[eng] += 1
        tok = (eng, self.cur[eng])
        self.prog[eng].append(("i", fn, self.sem[eng], 1))
        self._commit(tok, eng, rb, wb)
        return tok

    def dma(self, q, out, in_, extra_tokens=()):
        ring = self.dq[q]
        k = ring[self.dnext[q] % len(ring)]
        self.dnext[q] += 1
        need, rb, wb = self._deps(q, [in_], [out])
        clk = self.clock[q]
        prev = self.cur[k]
        if prev > 0 and clk.get(k, 0) < prev:
            need[k] = max(need.get(k, 0), prev)
        for s, v in extra_tokens:
            if clk.get(s, 0) < v:
                need[s] = max(need.get(s, 0), v)
        self._emit_waits(q, need)
        self.cur[k] += 16
        tok = (k, self.cur[k])
        self.prog[q].append(("i", lambda e, o=out, i=in_: e.dma_start(out=o, in_=i), self.sem[k], 16))
        self._commit(tok, q, rb, wb)
        return tok

    def barrier(self):
        for eng in self.ENG:
            need = {}
            for s, v in self.cur.items():
                if s == eng and eng == "pe":
                    continue
                if v > 0 and self.clock[eng].get(s, 0) < v:
                    need[s] = v
            self._emit_waits(eng, need)

    def finish(self):
        for eng in self.ENG:
            need = {}
            for s, v in self.cur.items():
                if s == eng:
                    continue
                if v > 0 and self.clock[eng].get(s, 0) < v:
                    need[s] = v
            self._emit_waits(eng, need)

    def finalize(self):
        waited = {n: set() for n in self.ENG}
        for eng in self.ENG:
            for it in self.prog[eng]:
                if it[0] == "w" and it[3] in waited:
                    waited[it[3]].add(it[2])
        self.tickmap = {}
        for n in self.ENG:
            self.tickmap[n] = {v: i + 1 for i, v in enumerate(sorted(waited[n]))}
        self.nsig = {n: len(waited[n]) for n in self.ENG}

    def replay(self, eng, e):
        seq = 0
        tm = self.tickmap
        for it in self.prog[eng]:
            if it[0] == "w":
                k = it[3]
                if k in tm:
                    e.wait_ge(it[1], tm[k][it[2]])
                else:
                    e.wait_ge(it[1], it[2])
            else:
                ins = it[1](e)
                if it[3] == 16:
                    ins.then_inc(it[2], 16)
                else:
                    seq += 1
                    if seq in tm[eng]:
                        ins.then_inc(it[2], 1)


class _Stop(Exception):
    pass


import os
KSTOP = int(os.environ.get("KSTOP", "0"))


def ck(n):
    if KSTOP == n:
        raise _Stop()


class Alloc:
    def __init__(self, big, words):
        self.big = big
        self.words = words
        self.off = 0

    def get(self, dtype, shape):
        n = 1
        for s in shape[1:]:
            n *= s
        w = n if dtype != BF16 else (n + 1) // 2
        w = (w + 63) // 64 * 64
        o = self.off
        self.off += w
        assert self.off <= self.words, "SBUF overflow %d > %d" % (self.off, self.words)
        ap = self.big[:, o:o + w]
        if dtype == BF16:
            ap = ap.bitcast(BF16)
        elif dtype == I32:
            ap = ap.bitcast(I32)
        ap = ap[:, 0:n]
        if len(shape) == 3:
            ap = ap.rearrange("p (a b) -> p a b", a=shape[1])
        elif len(shape) == 4:
            ap = ap.rearrange("p (a b c) -> p a b c", a=shape[1], b=shape[2])
        if shape[0] < 128:
            ap = ap[0:shape[0]]
        return ap


def bc(ap, shape):
    return ap.to_broadcast(list(shape))


def build_program(NPT, L, with_sample=True):
    nc = bass.Bass("TRN2", target_bir_lowering=False)
    NTOK = NPT * 512 + 64

    def din(name, shape, dt=F32):
        return nc.dram_tensor(name, list(shape), dt, kind="ExternalInput").ap()

    def dout(name, shape, dt=F32):
        return nc.dram_tensor(name, list(shape), dt, kind="ExternalOutput").ap()

    xT = din("xT", [128, 8, NTOK])
    pT = din("pT", [L, 128, 2, NTOK])
    wch = din("wch", [L, NCHUNK, 128, CHW])
    consts = din("consts", [128, NCONST])
    vecs = din("vecs", [128, L, 40])
    gfin = din("gfin", [128, 8])
    cwg = din("cwg", [128, L, 12, 4])
    cwc = din("cwc", [128, L, 2, 31])
    ba8 = din("ba8", [8, L, 2])
    s5col = din("s5col", [128, L, 8, 3])
    s5row = din("s5row", [128, L, 3, 8, 128])
    s5bt = din("s5bt", [L, 2, 128, 8, 128])
    s5ct = din("s5ct", [L, 2, 128, 8, 128])
    st_gdn = din("st_gdn", [L, 2, 128, 4, 128])
    st_gc = din("st_gc", [L, 2, 128, 12, 3])
    st_s5 = din("st_s5", [L, 2, 128, 8, 2])
    st_cc = din("st_cc", [L, 2, 128, 2, 30])

    yT = dout("yT", [128, 8, NTOK])
    o_gdn = dout("o_gdn", [L, 3, 128, 4, 128])
    o_gc = dout("o_gc", [L, 3, 128, 12, 3])
    o_s5 = dout("o_s5", [L, 3, 128, 8, 2])
    o_cc = dout("o_cc", [L, 3, 128, 2, 30])

    wscr = nc.dram_tensor("wscr", [L, NCHUNK, 128, CHW], BF16, kind="Internal").ap()
    s5tab = nc.dram_tensor("s5tab", [L, 8, 128, 2, 512], F32, kind="Internal").ap()
    s5mscr = nc.dram_tensor("s5mscr", [L, 128, 4, 8, 128], BF16, kind="Internal").ap()

    SBW = 53000
    NDS = 6
    import contextlib
    with contextlib.ExitStack() as es:
        big = es.enter_context(nc.sbuf_tensor("big", [128, SBW], F32))
        psum = es.enter_context(nc.psum_tensor("psum", [128, 8, 512], F32))
        esems = [es.enter_context(nc.semaphore("e_%s" % n)) for n in Sched.ENG]
        dsems = [es.enter_context(nc.semaphore("dm_%d" % i)) for i in range(NDS)]
        block = es.enter_context(nc.Block())
        S = Sched(nc, esems, dsems)
        A = Alloc(big, SBW)

        def mm(out, lhsT, rhs, start=True, stop=True):
            S.op("pe", lambda e: e.matmul(out, lhsT=lhsT, rhs=rhs, start=start, stop=stop),
                 [lhsT, rhs], [out])

        def tr(out, in_, ident):
            S.op("pe", lambda e: e.transpose(out=out, in_=in_, identity=ident), [in_, ident], [out])

        def act(out, in_, func, bias=None, scale=None):
            kw = {}
            r = [in_]
            if bias is not None:
                kw["bias"] = bias
                if not isinstance(bias, float):
                    r.append(bias)
            if scale is not None:
                kw["scale"] = scale
                if not isinstance(scale, float):
                    r.append(scale)
            S.op("act", lambda e: e.activation(out=out, in_=in_, func=func, **kw), r, [out])

        def tt(eng, out, in0, in1, op):
            S.op(eng, lambda e: e.tensor_tensor(out=out, in0=in0, in1=in1, op=op), [in0, in1], [out])

        def ts(eng, out, in0, s1, op0, s2=None, op1=None):
            r = [in0]
            if not isinstance(s1, float):
                r.append(s1)
            if s2 is not None and not isinstance(s2, float):
                r.append(s2)
            if op1 is None:
                S.op(eng, lambda e: e.tensor_scalar(out=out, in0=in0, scalar1=s1, scalar2=None, op0=op0), r, [out])
            else:
                S.op(eng, lambda e: e.tensor_scalar(out=out, in0=in0, scalar1=s1, scalar2=s2, op0=op0, op1=op1), r, [out])

        def stt(eng, out, in0, scalar, in1, op0, op1):
            r = [in0, in1]
            if not isinstance(scalar, float):
                r.append(scalar)
            S.op(eng, lambda e: e.scalar_tensor_tensor(out=out, in0=in0, scalar=scalar, in1=in1, op0=op0, op1=op1), r, [out])

        def cp(eng, out, in_):
            if eng == "act":
                act(out, in_, AF.Copy)
            else:
                S.op(eng, lambda e: e.tensor_copy(out=out, in_=in_), [in_], [out])

        def memset(eng, out, val):
            S.op(eng, lambda e: e.memset(out, val), [], [out])

        def recip(out, in_):
            S.op("dve", lambda e: e.reciprocal(out=out, in_=in_), [in_], [out])

        def scan(out, d0, d1, init):
            r = [d0, d1]
            if not isinstance(init, float):
                r.append(init)
            S.op("dve", lambda e: e.tensor_tensor_scan(out=out, data0=d0, data1=d1, initial=init,
                                                       op0=ALU.mult, op1=ALU.add), r, [out])

        pcnt = [0]
        pinned = set()

        def pbank(pin=False):
            while True:
                b = pcnt[0] % 8
                pcnt[0] += 1
                if b not in pinned:
                    break
            if pin:
                pinned.add(b)
            return b

        def PS(b, shape=None, dt=F32):
            ap = psum[:, b, :]
            if dt == BF16:
                ap = ap.bitcast(BF16)
            if shape is None:
                return ap
            n = 1
            for s in shape[1:]:
                n *= s
            ap = ap[:, 0:n]
            if len(shape) == 3:
                ap = ap.rearrange("p (a b) -> p a b", a=shape[1])
            if shape[0] < 128:
                ap = ap[0:shape[0]]
            return ap

        cst = A.get(F32, [128, NCONST])
        ident = cst[:, C_ID:C_ID + 128]
        masku = cst[:, C_MU:C_MU + 128]
        su = cst[:, C_SU:C_SU + 128]
        iota1 = cst[:, C_IOTA:C_IOTA + 512]
        ones = cst[:, C_ONES:C_ONES + 128]
        identb = A.get(BF16, [128, 128])
        onesb = A.get(BF16, [128, 128])
        vec = A.get(F32, [128, L, 40])
        gf = A.get(F32, [128, 8])
        cwg_s = A.get(F32, [128, L, 12, 4])
        cwc_s = A.get(F32, [128, L, 2, 31])
        ba8_s = A.get(F32, [8, L, 2])
        na8 = A.get(F32, [8, L])
        s5c = A.get(F32, [128, L, 8, 3])
        s5r = A.get(F32, [128, L, 8])
        hT = A.get(F32, [128, 8, 512])
        hn = A.get(BF16, [128, 8, 512])
        mix = A.get(BF16, [128, 8, 512])
        Sst = [A.get(F32, [128, 4, 128]) for _ in range(L)]
        ghist = [A.get(F32, [128, 12, 3]) for _ in range(L)]
        chist = [A.get(F32, [128, 2, 30]) for _ in range(L)]
        xst = [A.get(F32, [128, 8, 2]) for _ in range(L)]
        Sst_s = [A.get(F32, [128, 4, 128]) for _ in range(2)]
        ghist_s = [A.get(F32, [128, 12, 3]) for _ in range(2)]
        chist_s = [A.get(F32, [128, 2, 30]) for _ in range(2)]
        xst_s = [A.get(F32, [128, 8, 2]) for _ in range(2)]
        NSLOT = 5
        wslot = [A.get(BF16, [128, CHW]) for _ in range(NSLOT)]
        s5m = A.get(BF16, [128, 4, 8, 128])
        persist_end = A.off

        try:
            S.dma("sp", cst, consts)
            S.dma("sp", vec, vecs)
            S.dma("sp", gf, gfin)
            S.dma("sp", cwg_s, cwg)
            S.dma("sp", cwc_s, cwc)
            S.dma("sp", ba8_s, ba8)
            S.dma("sp", s5c, s5col)
            cp("dve", identb, ident)
            cp("dve", onesb, ones)
            act(na8, ba8_s[:, :, 1], AF.Exp)
            ts("dve", na8, na8, -1.0, ALU.mult)
            ck(1)
            for l in range(L):
                memset("dve", Sst[l], 0.0)
                memset("dve", ghist[l], 0.0)
                memset("dve", chist[l], 0.0)
                memset("dve", xst[l], 0.0)

            def range_reduce(dst, src, tmpi, tmpf):
                ts("dve", tmpi, src, 1.0 / TWO_PI, ALU.mult)
                cp("dve", tmpf, tmpi)
                stt("dve", dst, tmpf, -TWO_PI, src, ALU.mult, ALU.add)
                ts("dve", dst, dst, -3.1415925, ALU.max, 3.1415925, ALU.min)

            A.off = persist_end
            th = A.get(F32, [128, L, 8])
            dtc = A.get(F32, [128, L, 8])
            t_i = A.get(I32, [128, 512])
            t_f = A.get(F32, [128, 512])
            t_a = A.get(F32, [128, 512])
            t_b = A.get(F32, [128, 512])
            tabs = [A.get(F32, [128, 2, 512]) for _ in range(2)]
            act(dtc, s5c[:, :, :, 2], AF.Exp)
            tt("dve", th, s5c[:, :, :, 1], dtc, ALU.mult)
            tt("dve", s5r, s5c[:, :, :, 0], dtc, ALU.mult)
            act(s5r, s5r, AF.Exp)
            thf = th.rearrange("p a b -> p (a b)")
            range_reduce(thf, thf, t_i[:, 0:L * 8], t_f[:, 0:L * 8])
            k = 0
            for l in range(L):
                for q in range(8):
                    tb = tabs[k % 2]
                    k += 1
                    ts("dve", t_a, iota1, th[:, l, q:q + 1], ALU.mult)
                    range_reduce(t_b, t_a, t_i, t_f)
                    act(tb[:, 1, :], t_b, AF.Sin)
                    ts("dve", t_a, t_b, math.pi / 2, ALU.add)
                    range_reduce(t_b, t_a, t_i, t_f)
                    act(tb[:, 0, :], t_b, AF.Sin)
                    S.dma("sp", s5tab[l, q], tb)
            ck(2)
            s5stage = A.get(F32, [128, 3, 8, 128])
            s5t = [A.get(F32, [128, 512]) for _ in range(6)]
            for l in range(L):
                S.dma("sp", s5stage[:, 0], s5row[:, l, 0])
                S.dma("sp", s5stage[:, 1], s5row[:, l, 1])
                S.dma("sp", s5stage[:, 2], s5row[:, l, 2])
                lre, lim, ldt = s5stage[:, 0], s5stage[:, 1], s5stage[:, 2]
                act(ldt, ldt, AF.Exp)
                for hq in range(2):
                    sl = slice(hq * 4, hq * 4 + 4)
                    a_re = lre[:, sl, :].rearrange("p a b -> p (a b)")
                    a_im = lim[:, sl, :].rearrange("p a b -> p (a b)")
                    a_dt = ldt[:, sl, :].rearrange("p a b -> p (a b)")
                    w0, w1, w2, w3, w4, w5 = s5t
                    tt("dve", w0, a_im, a_dt, ALU.mult)
                    range_reduce(w1, w0, t_i, t_f)
                    act(w2, w1, AF.Sin)
                    ts("dve", w0, w1, math.pi / 2, ALU.add)
                    range_reduce(w1, w0, t_i, t_f)
                    act(w3, w1, AF.Sin)
                    tt("dve", w0, a_re, a_dt, ALU.mult)
                    act(w0, w0, AF.Exp)
                    tt("dve", w3, w3, w0, ALU.mult)
                    ts("dve", w3, w3, -1.0, ALU.add)
                    tt("dve", w2, w2, w0, ALU.mult)
                    tt("dve", w0, a_re, a_re, ALU.mult)
                    tt("dve", w1, a_im, a_im, ALU.mult)
                    tt("dve", w0, w0, w1, ALU.add)
                    recip(w0, w0)
                    tt("dve", w1, w3, a_re, ALU.mult)
                    tt("dve", w4, w2, a_im, ALU.mult)
                    tt("dve", w1, w1, w4, ALU.add)
                    tt("dve", w1, w1, w0, ALU.mult)
                    tt("dve", w4, w2, a_re, ALU.mult)
                    tt("dve", w5, w3, a_im, ALU.mult)
                    tt("dve", w4, w4, w5, ALU.subtract)
                    tt("dve", w4, w4, w0, ALU.mult)
                    S.dma("sp", w2.rearrange("p (a b) -> p a b", a=4), s5bt[l, 0][:, sl, :])
                    S.dma("sp", w3.rearrange("p (a b) -> p a b", a=4), s5bt[l, 1][:, sl, :])
                    tt("dve", w0, w1, w2, ALU.mult)
                    tt("dve", w5, w4, w3, ALU.mult)
                    tt("dve", s5m[:, 0, sl, :].rearrange("p a b -> p (a b)"), w0, w5, ALU.subtract)
                    tt("dve", w0, w1, w3, ALU.mult)
                    tt("dve", w5, w4, w2, ALU.mult)
                    tt("dve", s5m[:, 1, sl, :].rearrange("p a b -> p (a b)"), w0, w5, ALU.add)
                    S.dma("sp", w2.rearrange("p (a b) -> p a b", a=4), s5ct[l, 0][:, sl, :])
                    S.dma("sp", w3.rearrange("p (a b) -> p a b", a=4), s5ct[l, 1][:, sl, :])
                    cp("dve", s5m[:, 2, sl, :].rearrange("p a b -> p (a b)"), w2)
                    ts("dve", s5m[:, 3, sl, :].rearrange("p a b -> p (a b)"), w3, -1.0, ALU.mult)

                S.dma("sp", s5mscr[l], s5m)
            ck(3)
            A.off = persist_end
            stg = [A.get(F32, [128, CHW]) for _ in range(4)]
            stb = [A.get(BF16, [128, CHW]) for _ in range(2)]
            allc = [(l, j) for l in range(L) for j in range(NCHUNK)]
            NST = len(stg)
            for k in range(min(NST - 1, len(allc))):
                S.dma("sp", stg[k % NST], wch[allc[k][0], allc[k][1]])
            for k, (l, j) in enumerate(allc):
                kn = k + NST - 1
                if kn < len(allc):
                    S.dma("sp", stg[kn % NST], wch[allc[kn][0], allc[kn][1]])
                b = stb[k % 2]
                cp("dve" if k % 2 == 0 else "act", b, stg[k % NST])
                S.dma("sp", wscr[l, j], b)
            S.barrier()
            ck(4)
            A.off = persist_end

            wseq = []
            tiles = []
            for i in range(NPT):
                tiles.append(dict(tok0=i * 512, T=512, segs=[(0, 512)], C=128, last=(i == NPT - 1)))
            if with_sample:
                tiles.append(dict(tok0=NPT * 512, T=64, segs=[(1, 32), (2, 32)], C=32, last=True))
            for ti in range(len(tiles)):
                for l in range(L):
                    for j in range(NCHUNK):
                        wseq.append((l, j))
            wstate = dict(issued=0, used=0, released=0)

            def wpump():
                while wstate["issued"] < len(wseq) and wstate["issued"] - NSLOT < wstate["released"]:
                    m = wstate["issued"]
                    l_, j_ = wseq[m]
                    S.dma("sp", wslot[m % NSLOT], wscr[l_, j_])
                    wstate["issued"] += 1

            def wget():
                n = wstate["used"]
                wstate["used"] += 1
                wpump()
                assert wstate["issued"] > n
                return wslot[n % NSLOT]

            def wdone(k=1):
                wstate["released"] += k
                wpump()

            gluw = A.get(BF16, [128, 2, 256])
            sqb = A.get(BF16, [128, 512])
            sdt = A.get(F32, [128, 512])
            rst = A.get(F32, [128, 512])
            t512 = A.get(F32, [128, 512])
            sgp = A.get(F32, [128, 512])
            u5f = A.get(F32, [128, 2, 512])
            u5b = A.get(BF16, [128, 2, 512])
            selh = A.get(F32, [8, 2, 128])
            tabq = [A.get(F32, [128, 2, 512]) for _ in range(2)]
            s5t = [A.get(F32, [128, 512]) for _ in range(6)]
            xbre = A.get(BF16, [128, 512])
            xbim = A.get(BF16, [128, 512])
            ygf = u5f
            ygb = u5b
            ov0 = A.off
            xpb = A.get(BF16, [128, 3 * 2 * 520])
            dgc = A.get(BF16, [128, 3, 4, 128])
            qkv = A.get(F32, [128, 3, 512])
            zs = A.get(F32, [128, 512])
            kTb = A.get(BF16, [128, 512])
            kbT = A.get(BF16, [128, 512])
            qTb = A.get(BF16, [128, 512])
            qgT = A.get(BF16, [128, 512])
            betaB = A.get(F32, [128, 512])
            gcB = A.get(F32, [128, 512])
            egB = A.get(F32, [128, 512])
            oT = A.get(F32, [128, 512])
            sig8 = A.get(F32, [8, 512])
            g8 = A.get(F32, [8, 512])
            e8 = A.get(F32, [8, 512])
            cols = A.get(F32, [128, 4, 2])
            negc = A.get(F32, [128, 4])
            ecol = A.get(F32, [128, 4])
            bexp = A.get(F32, [128, 4])
            kdcol = A.get(F32, [128, 4])
            GD = F32 if GDN_F32 else BF16
            kbg = A.get(GD, [128, 4, 128])
            kd = A.get(BF16, [128, 4, 128])
            vbt = A.get(GD, [128, 4, 128])
            Du = A.get(F32, [128, 4, 128])
            EU = A.get(F32, [128, 4, 128])
            EUs = A.get(F32, [128, 4, 128])
            Ub = [A.get(GD, [128, 4, 128]) for _ in range(2)]
            Lb = [A.get(GD, [128, 4, 128]) for _ in range(2)]
            Rb = [A.get(GD, [128, 4, 128]) for _ in range(2)]
            Aqk = A.get(BF16, [128, 4, 128])
            nwT = A.get(GD, [128, 4, 128])
            vnew = A.get(BF16, [128, 128])
            Sbf = A.get(BF16, [128, 128])
            ov_end = A.off
            A.off = ov0
            xcb = A.get(BF16, [128, 2 * 2 * 544])
            dgcc = A.get(BF16, [128, 2, 31, 128])
            xg = A.get(F32, [128, 2, 512])
            xc16 = A.get(BF16, [128, 2, 512])
            ov_end = max(ov_end, A.off)
            A.off = ov0
            act_base = A.get(F32, [128, 22 * 256])
            actb = act_base.bitcast(BF16).rearrange("p (a b) -> p a b", a=22)
            yout = act_base[:, 0:8 * 512].rearrange("p (a b) -> p a b", a=8)
            pTf = A.get(F32, [128, 2, 512])
            pTb = A.get(BF16, [128, 2, 512])
            ov_end = max(ov_end, A.off)
            A.off = ov_end
            print("SBUF words used", A.off, "of", SBW)

            def rmsnorm_to(dst_bf, gcol_fn, T, final_out=None):
                pb = pbank()
                for kc in range(8):
                    act(sqb[:, 0:T], hT[:, kc, 0:T], AF.Square)
                    mm(PS(pb)[:, 0:T], onesb, sqb[:, 0:T], start=(kc == 0), stop=(kc == 7))
                act(sdt[:, 0:T], PS(pb)[:, 0:T], AF.Ln, bias=1e-6, scale=1.0 / D)
                act(rst[:, 0:T], sdt[:, 0:T], AF.Exp, scale=-0.5)
                for kc in range(8):
                    o = dst_bf[:, kc, 0:T] if final_out is None else final_out[:, kc, 0:T]
                    stt("dve", o, hT[:, kc, 0:T], gcol_fn(kc), rst[:, 0:T], ALU.mult, ALU.mult)

            VEC_GMIX, VEC_GFFN, VEC_GN, VEC_S5D, VEC_GLUB, VEC_DWB, VEC_LNG, VEC_LNB = 0, 8, 16, 17, 19, 21, 23, 25

            for tile in tiles:
                T = tile["T"]
                tok0 = tile["tok0"]
                segs = tile["segs"]
                nseg = len(segs)
                Ls = segs[0][1]
                C = tile["C"]
                nb = T // C
                m_lev = int(math.log2(C))
                is_sample = segs[0][0] != 0
                S.dma("sp", hT[:, :, 0:T], xT[:, :, tok0:tok0 + T])

                def seqstate(lst_p, lst_s, l, si):
                    seq = segs[si][0]
                    return lst_p[l] if seq == 0 else lst_s[seq - 1]

                for l in range(L):
                    if is_sample:
                        for si in range(nseg):
                            S.dma("sp", Sst_s[si], st_gdn[l, si])
                            S.dma("sp", ghist_s[si], st_gc[l, si])
                            S.dma("sp", chist_s[si], st_cc[l, si])
                            S.dma("sp", xst_s[si], st_s5[l, si])
                    S.dma("sp", s5m, s5mscr[l])

                    rmsnorm_to(hn, lambda kc: vec[:, l, VEC_GMIX + kc:VEC_GMIX + kc + 1], T)
                    ck(5)

                    w4 = wget()
                    w4a = w4[:, 0:8 * 264].rearrange("p (a b) -> p a b", a=8)
                    cp("dve", gluw.rearrange("p a b -> p (a b)"), w4[:, 8 * 264:8 * 264 + 512])
                    pb_ba = pbank()
                    for kc in range(8):
                        mm(PS(pb_ba)[0:8, 0:T], w4a[:, kc, 256:264], hn[:, kc, 0:T], start=(kc == 0), stop=(kc == 7))
                    pu = [pbank(), pbank()]
                    for mc in range(2):
                        for kc in range(8):
                            mm(PS(pu[mc])[:, 0:T], w4a[:, kc, mc * 128:(mc + 1) * 128], hn[:, kc, 0:T],
                               start=(kc == 0), stop=(kc == 7))
                    wdone(1)
                    ck(51)
                    act(sig8[:, 0:T], PS(pb_ba)[0:8, 0:T], AF.Sigmoid)
                    ck(52)
                    act(e8[:, 0:T], PS(pb_ba)[0:8, 0:T], AF.Exp, bias=ba8_s[:, l, 0:1])
                    ck(53)
                    act(e8[:, 0:T], e8[:, 0:T], AF.Ln, bias=1.0)
                    ck(54)
                    ts("dve", g8[:, 0:T], e8[:, 0:T], na8[:, l:l + 1], ALU.mult)
                    ck(55)
                    for mc in range(2):
                        cp("act", u5f[:, mc, 0:T], PS(pu[mc])[:, 0:T])
                        ck(56 + mc * 2)
                        cp("dve", u5b[:, mc, 0:T], PS(pu[mc])[:, 0:T])
                        ck(57 + mc * 2)
                    ck(6)

                    def s5_gen(l=l, T=T, Ls=Ls, nseg=nseg):
                        py = [pbank(pin=True), pbank(pin=True)]
                        for q in range(8):
                            kc = q // 4
                            tb = tabq[q % 2]
                            S.dma("sp", tb[:, :, 0:Ls], s5tab[l, q][:, :, 0:Ls])
                            p_re = pbank(pin=True)
                            p_im = pbank(pin=True)
                            mm(PS(p_re)[:, 0:T], s5m[:, 0, q, :], u5b[:, kc, 0:T])
                            mm(PS(p_im)[:, 0:T], s5m[:, 1, q, :], u5b[:, kc, 0:T])
                            yield
                            w0, w1, w2, w3, w4, w5 = s5t
                            for si in range(nseg):
                                cs = slice(si * Ls, (si + 1) * Ls)
                                nC = tb[:, 0, 0:Ls]
                                nS = tb[:, 1, 0:Ls]
                                xs_ = seqstate(xst, xst_s, l, si)
                                tt("dve", w0[:, cs], nC, PS(p_re)[:, cs], ALU.mult)
                                yield
                                tt("dve", w1[:, cs], nS, PS(p_im)[:, cs], ALU.mult)
                                yield
                                tt(S5POOL, w0[:, cs], w0[:, cs], w1[:, cs], ALU.add)
                                yield
                                tt("dve", w2[:, cs], nC, PS(p_im)[:, cs], ALU.mult)
                                yield
                                tt("dve", w3[:, cs], nS, PS(p_re)[:, cs], ALU.mult)
                                yield
                                tt(S5POOL, w2[:, cs], w2[:, cs], w3[:, cs], ALU.subtract)
                                yield
                                scan(w1[:, cs], bc(s5r[:, l, q:q + 1], [128, Ls]), w0[:, cs], xs_[:, q, 0:1])
                                yield
                                scan(w3[:, cs], bc(s5r[:, l, q:q + 1], [128, Ls]), w2[:, cs], xs_[:, q, 1:2])
                                yield
                                tt(S5POOL, w4[:, cs], nC, w1[:, cs], ALU.mult)
                                yield
                                tt(S5POOL, w5[:, cs], nS, w3[:, cs], ALU.mult)
                                yield
                                tt("dve", w0[:, cs], w4[:, cs], w5[:, cs], ALU.subtract)
                                cp("act", xbre[:, cs], w0[:, cs])
                                yield
                                tt(S5POOL, w4[:, cs], nS, w1[:, cs], ALU.mult)
                                yield
                                tt(S5POOL, w5[:, cs], nC, w3[:, cs], ALU.mult)
                                yield
                                tt("dve", w2[:, cs], w4[:, cs], w5[:, cs], ALU.add)
                                cp("act", xbim[:, cs], w2[:, cs])
                                yield
                                e_ = (si + 1) * Ls - 1
                                cp("dve", xs_[:, q, 0:1], w0[:, e_:e_ + 1])
                                cp("dve", xs_[:, q, 1:2], w2[:, e_:e_ + 1])
                                yield
                            pinned.discard(p_re)
                            pinned.discard(p_im)
                            mm(PS(py[kc])[:, 0:T], s5m[:, 2, q, :], xbre[:, 0:T], start=(q % 4 == 0), stop=False)
                            mm(PS(py[kc])[:, 0:T], s5m[:, 3, q, :], xbim[:, 0:T], start=False, stop=(q % 4 == 3))
                            yield
                        for kc in range(2):
                            stt("dve", ygf[:, kc, 0:T], u5f[:, kc, 0:T], vec[:, l, VEC_S5D + kc:VEC_S5D + kc + 1],
                                PS(py[kc])[:, 0:T], ALU.mult, ALU.add)
                            act(ygf[:, kc, 0:T], ygf[:, kc, 0:T], AF.Gelu_apprx_tanh)
                            cp("dve", ygb[:, kc, 0:T], ygf[:, kc, 0:T])
                            yield
                        pinned.discard(py[0])
                        pinned.discard(py[1])
                        for mc in range(2):
                            pb = pbank()
                            for kc in range(2):
                                mm(PS(pb)[:, 0:T], gluw[:, kc, mc * 128:(mc + 1) * 128], ygb[:, kc, 0:T],
                                   start=(kc == 0), stop=(kc == 1))
                            act(s5t[4][:, 0:T], PS(pb)[:, 0:T], AF.Sigmoid, bias=vec[:, l, VEC_GLUB + mc:VEC_GLUB + mc + 1])
                            tt("dve", mix[:, 4 + mc, 0:T], ygf[:, mc, 0:T], s5t[4][:, 0:T], ALU.mult)
                            yield

                    s5g = s5_gen()

                    def pump(n):
                        for _ in range(n):
                            try:
                                next(s5g)
                            except StopIteration:
                                return

                    for h in range(4):
                        wh = wget().rearrange("p (a b) -> p a b", a=8)
                        xv = xpb[:, 0:3 * nseg * (4 + Ls)].rearrange("p (a b c) -> p a b c", a=3, b=nseg)
                        for si in range(nseg):
                            gh = seqstate(ghist, ghist_s, l, si)
                            for c3 in range(3):
                                cp("dve", xv[:, c3, si, 1:4], gh[:, c3 * 4 + h, :])
                        for c3 in range(3):
                            for j in range(4):
                                act(dgc[:, c3, j, :], identb, AF.Copy, scale=cwg_s[:, l, c3 * 4 + h, j:j + 1])
                        ck(61)
                        pz = None
                        for c4 in range(4):
                            pb = pbank()
                            for kc in range(8):
                                mm(PS(pb)[:, 0:T], wh[:, kc, c4 * 128:(c4 + 1) * 128], hn[:, kc, 0:T],
                                   start=(kc == 0), stop=(kc == 7))
                            if c4 < 3:
                                for si in range(nseg):
                                    gh = seqstate(ghist, ghist_s, l, si)
                                    cp("act", xv[:, c4, si, 4:4 + Ls], PS(pb)[:, si * Ls:(si + 1) * Ls])
                                    cp("dve", gh[:, c4 * 4 + h, :], PS(pb)[:, (si + 1) * Ls - 3:(si + 1) * Ls])
                            else:
                                act(zs[:, 0:T], PS(pb)[:, 0:T], AF.Silu)
                        wdone(1)
                        ck(62)
                        for c3 in range(3):
                            pb = pbank()
                            for si in range(nseg):
                                for j in range(4):
                                    mm(PS(pb)[:, si * Ls:(si + 1) * Ls], dgc[:, c3, j, :], xv[:, c3, si, 1 + j:1 + j + Ls],
                                       start=(j == 0), stop=(j == 3))
                            act(qkv[:, c3, 0:T], PS(pb)[:, 0:T], AF.Silu)
                        ck(63)
                        for c3, dst, sc in ((0, qTb, 128.0 ** -0.5), (1, kTb, 1.0)):
                            act(sqb[:, 0:T], qkv[:, c3, 0:T], AF.Square)
                            pb = pbank()
                            mm(PS(pb)[:, 0:T], onesb, sqb[:, 0:T])
                            act(sdt[:, 0:T], PS(pb)[:, 0:T], AF.Ln, bias=1e-6)
                            act(rst[:, 0:T], sdt[:, 0:T], AF.Exp, scale=-0.5)
                            stt("dve", dst[:, 0:T], qkv[:, c3, 0:T], sc, rst[:, 0:T], ALU.mult, ALU.mult)
                        ck(64)
                        ts("dve", selh[:, 0, :], ones[0:8, :], ident[0:8, h:h + 1], ALU.mult)
                        ts("dve", selh[:, 1, :], ones[0:8, :], ident[0:8, 4 + h:5 + h], ALU.mult)
                        pbb = pbank()
                        mm(PS(pbb)[:, 0:T], selh[:, 0, :], sig8[:, 0:T])
                        pbg = pbank()
                        mm(PS(pbg)[:, 0:T], selh[:, 1, :], g8[:, 0:T])
                        cp("act", betaB[:, 0:T], PS(pbb)[:, 0:T])
                        for b in range(nb):
                            scan(gcB[:, b * C:(b + 1) * C], ones[:, 0:C], PS(pbg)[:, b * C:(b + 1) * C], 0.0)
                        act(egB[:, 0:T], gcB[:, 0:T], AF.Exp)
                        tt("dve", kbT[:, 0:T], kTb[:, 0:T], betaB[:, 0:T], ALU.mult)
                        tt("dve", qgT[:, 0:T], qTb[:, 0:T], egB[:, 0:T], ALU.mult)
                        ck(65)
                        pbc = pbank()
                        pcv = PS(pbc)[:, 0:nb * 2].rearrange("p (a b) -> p a b", a=nb)[0:C]
                        for b in range(nb):
                            mm(pcv[:, b, 0:1], gcB[:, b * C:(b + 1) * C], ident[:, 0:1])
                            mm(pcv[:, b, 1:2], betaB[:, b * C:(b + 1) * C], ident[:, 0:1])
                        cv = cols[0:C, 0:nb, :]
                        cp("dve", cv, pcv)
                        ts("dve", negc[0:C, 0:nb], cv[:, :, 0], -1.0, ALU.mult)
                        act(ecol[0:C, 0:nb], cv[:, :, 0], AF.Exp)
                        tt("dve", bexp[0:C, 0:nb], cv[:, :, 1], ecol[0:C, 0:nb], ALU.mult)
                        for b in range(nb):
                            act(kdcol[0:C, b:b + 1], cv[:, b, 0:1], AF.Exp, bias=gcB[0:C, (b + 1) * C - 1:(b + 1) * C], scale=-1.0)
                        ck(66)
                        pkt = pbank()
                        pk = PS(pkt, dt=BF16)[:, 0:nb * 128].rearrange("p (a b) -> p a b", a=nb)[0:C]
                        for b in range(nb):
                            tr(pk[:, b, :], kTb[:, b * C:(b + 1) * C], identb)
                        pvt = pbank()
                        pv = PS(pvt)[:, 0:nb * 128].rearrange("p (a b) -> p a b", a=nb)[0:C]
                        for b in range(nb):
                            tr(pv[:, b, :], qkv[:, 2, b * C:(b + 1) * C], ident)
                        for b in range(nb):
                            act(kbg[0:C, b, :], pk[:, b, :], AF.Copy, scale=bexp[0:C, b:b + 1])
                            act(kd[0:C, b, :], pk[:, b, :], AF.Copy, scale=kdcol[0:C, b:b + 1])
                            act(vbt[0:C, b, :], pv[:, b, :], AF.Copy, scale=cv[:, b, 1:2])
                        ck(67)
                        pg = pbank()
                        pgv = PS(pg)[:, 0:nb * C].rearrange("p (a b) -> p a b", a=nb)[0:C]
                        pa = pbank()
                        pav = PS(pa)[:, 0:nb * C].rearrange("p (a b) -> p a b", a=nb)[0:C]
                        for b in range(nb):
                            mm(pgv[:, b, :], kTb[:, b * C:(b + 1) * C], kbT[:, b * C:(b + 1) * C])
                            mm(pav[:, b, :], kTb[:, b * C:(b + 1) * C], qTb[:, b * C:(b + 1) * C])
                        Duv = Du[0:C, 0:nb, 0:C]
                        EUv = EU[0:C, 0:nb, 0:C]
                        EUsv = EUs[0:C, 0:nb, 0:C]
                        for b in range(nb):
                            stt("dve", Duv[:, b, :], gcB[0:C, b * C:(b + 1) * C], negc[0:C, b:b + 1], masku[0:C, 0:C],
                                ALU.add, ALU.add)
                        act(EUv, Duv, AF.Exp)
                        for b in range(nb):
                            tt("dve", EUsv[:, b, :], EUv[:, b, :], su[0:C, 0:C], ALU.mult)
                        U0 = Ub[0][0:C, 0:nb, 0:C]
                        tt("dve", U0, pgv, EUsv, ALU.mult)
                        Aqv = Aqk[0:C, 0:nb, 0:C]
                        tt("dve", Aqv, pav, EUv, ALU.mult)
                        ck(68)
                        plt = pbank()
                        idg = ident if GDN_F32 else identb
                        pl = PS(plt, dt=GD)[:, 0:nb * C].rearrange("p (a b) -> p a b", a=nb)[0:C]
                        for b in range(nb):
                            tr(pl[:, b, :], U0[:, b, :], idg[0:C, 0:C])
                        L0 = Lb[0][0:C, 0:nb, 0:C]
                        cp("act", L0, pl)
                        R0 = Rb[0][0:C, 0:nb, 0:C]
                        for b in range(nb):
                            stt("dve", R0[:, b, :], U0[:, b, :], -1.0, idg[0:C, 0:C], ALU.mult, ALU.add)
                        ck(69)
                        cur = 0
                        rcur = 0

                        def r_update(Lfac, rcur):
                            Rp = Rb[rcur][0:C, 0:nb, 0:C]
                            Rn = Rb[1 - rcur][0:C, 0:nb, 0:C]
                            p3 = pbank()
                            p3v = PS(p3)[:, 0:nb * C].rearrange("p (a b) -> p a b", a=nb)[0:C]
                            for b in range(nb):
                                mm(p3v[:, b, :], idg[0:C, 0:C], Rp[:, b, :], start=True, stop=False)
                                mm(p3v[:, b, :], Lfac[:, b, :], Rp[:, b, :], start=False, stop=True)
                            return p3v, Rn

                        for lev in range(1, m_lev):
                            Up = Ub[cur][0:C, 0:nb, 0:C]
                            Lp = Lb[cur][0:C, 0:nb, 0:C]
                            Un = Ub[1 - cur][0:C, 0:nb, 0:C]
                            Ln = Lb[1 - cur][0:C, 0:nb, 0:C]
                            p1 = pbank()
                            p1v = PS(p1)[:, 0:nb * C].rearrange("p (a b) -> p a b", a=nb)[0:C]
                            for b in range(nb):
                                mm(p1v[:, b, :], Up[:, b, :], Lp[:, b, :])
                            if lev < m_lev - 1:
                                p2 = pbank()
                                p2v = PS(p2)[:, 0:nb * C].rearrange("p (a b) -> p a b", a=nb)[0:C]
                                for b in range(nb):
                                    mm(p2v[:, b, :], Lp[:, b, :], Up[:, b, :])
                            if lev >= 2:
                                p3v, Rn = r_update(Lp, rcur)
                            cp("act", Ln, p1v)
                            if lev < m_lev - 1:
                                cp("dve", Un, p2v)
                            if lev >= 2:
                                cp("dve", Rn, p3v)
                                rcur = 1 - rcur
                            cur = 1 - cur
                            pump(PUMP_LEVEL)
                        p3v, Rn = r_update(Lb[cur][0:C, 0:nb, 0:C], rcur)
                        cp("dve", Rn, p3v)
                        rcur = 1 - rcur
                        R = Rb[rcur][0:C, 0:nb, 0:C]
                        ck(70)
                        pw = pbank()
                        pwv = PS(pw)[:, 0:nb * C].rearrange("p (a b) -> p a b", a=nb)
                        for b in range(nb):
                            mm(pwv[:, b, :], kbg[0:C, b, :], R[:, b, :])
                        nwv = nwT[:, 0:nb, 0:C]
                        ts("dve", nwv, pwv, -1.0, ALU.mult)
                        ck(71)
                        for b in range(nb):
                            si = (b * C) // Ls
                            Sm = seqstate(Sst, Sst_s, l, si)[:, h, :]
                            if b == 0 or (b * C) % Ls == 0:
                                cp("act", Sbf, Sm)
                            p_v = pbank()
                            pvn = PS(p_v)[0:C, 0:128]
                            mm(pvn, R[:, b, :], vbt[0:C, b, :], start=True, stop=False)
                            mm(pvn, nwv[:, b, :], Sm if GDN_F32 else Sbf, start=False, stop=True)
                            cp("act", vnew[0:C, :], pvn)
                            p_o = pbank()
                            po = PS(p_o)[:, 0:C]
                            mm(po, Sbf, qgT[:, b * C:(b + 1) * C], start=True, stop=False)
                            mm(po, vnew[0:C, :], Aqv[:, b, :], start=False, stop=True)
                            cp("dve", oT[:, b * C:(b + 1) * C], po)
                            p_s = pbank()
                            psn = PS(p_s)[:, 0:128]
                            mm(psn, kd[0:C, b, :], vnew[0:C, :])
                            stt("dve", Sm, Sm, egB[:, (b + 1) * C - 1:(b + 1) * C], psn, ALU.mult, ALU.add)
                            if b + 1 < nb and ((b + 1) * C) % Ls != 0:
                                cp("act", Sbf, Sm)
                            pump(PUMP_SEQ)
                        ck(72)
                        act(sqb[:, 0:T], oT[:, 0:T], AF.Square)
                        pb = pbank()
                        mm(PS(pb)[:, 0:T], onesb, sqb[:, 0:T])
                        act(sdt[:, 0:T], PS(pb)[:, 0:T], AF.Ln, bias=1e-6, scale=1.0 / 128)
                        act(rst[:, 0:T], sdt[:, 0:T], AF.Exp, scale=-0.5)
                        stt("dve", t512[:, 0:T], oT[:, 0:T], vec[:, l, VEC_GN:VEC_GN + 1], rst[:, 0:T], ALU.mult, ALU.mult)
                        tt("dve", mix[:, h, 0:T], t512[:, 0:T], zs[:, 0:T], ALU.mult)
                        ck(7)

                    for si in range(nseg):
                        seq = segs[si][0]
                        if tile["last"]:
                            S.dma("sp", o_gdn[l, seq], seqstate(Sst, Sst_s, l, si))
                            S.dma("sp", o_gc[l, seq], seqstate(ghist, ghist_s, l, si))

                    ck(8)
                    for _ in s5g:
                        pass
                    if tile["last"]:
                        for si in range(nseg):
                            S.dma("sp", o_s5[l, segs[si][0]], seqstate(xst, xst_s, l, si))

                    ck(9)
                    wc = wget().rearrange("p (a b) -> p a b", a=8)
                    xcv = xcb[:, 0:2 * nseg * (30 + Ls)].rearrange("p (a b c) -> p a b c", a=2, b=nseg)
                    for kc in range(2):
                        for j in range(31):
                            act(dgcc[:, kc, j, :], identb, AF.Copy, scale=cwc_s[:, l, kc, j:j + 1])
                    for si in range(nseg):
                        ch = seqstate(chist, chist_s, l, si)
                        cp("dve", xcv[:, :, si, 0:30], ch)
                    pa_ = [pbank(), pbank()]
                    pg_ = [pbank(), pbank()]
                    for c4 in range(4):
                        pb = pa_[c4] if c4 < 2 else pg_[c4 - 2]
                        for kc in range(8):
                            mm(PS(pb)[:, 0:T], wc[:, kc, c4 * 128:(c4 + 1) * 128], hn[:, kc, 0:T],
                               start=(kc == 0), stop=(kc == 7))
                    wdone(1)
                    for kc in range(2):
                        act(sgp[:, 0:T], PS(pg_[kc])[:, 0:T], AF.Sigmoid)
                        tt("dve", xg[:, kc, 0:T], PS(pa_[kc])[:, 0:T], sgp[:, 0:T], ALU.mult)
                        for si in range(nseg):
                            ch = seqstate(chist, chist_s, l, si)
                            cp("act", xcv[:, kc, si, 30:30 + Ls], xg[:, kc, si * Ls:(si + 1) * Ls])
                            cp("dve", ch[:, kc, :], xg[:, kc, (si + 1) * Ls - 30:(si + 1) * Ls])
                    pcv_ = [pbank(), pbank()]
                    for kc in range(2):
                        for si in range(nseg):
                            for j in range(31):
                                mm(PS(pcv_[kc])[:, si * Ls:(si + 1) * Ls], dgcc[:, kc, j, :], xcv[:, kc, si, j:j + Ls],
                                   start=(j == 0), stop=(j == 30))
                    xc = xg
                    for kc in range(2):
                        act(xc[:, kc, 0:T], PS(pcv_[kc])[:, 0:T], AF.Identity, bias=vec[:, l, VEC_DWB + kc:VEC_DWB + kc + 1])
                        cp("dve", xc16[:, kc, 0:T], xc[:, kc, 0:T])
                    pm = pbank()
                    for kc in range(2):
                        mm(PS(pm)[:, 0:T], onesb, xc16[:, kc, 0:T], start=(kc == 0), stop=(kc == 1))
                    for kc in range(2):
                        stt("dve", xc[:, kc, 0:T], PS(pm)[:, 0:T], -1.0 / 256, xc[:, kc, 0:T], ALU.mult, ALU.add)
                        act(xc16[:, kc, 0:T], xc[:, kc, 0:T], AF.Square)
                    pv_ = pbank()
                    for kc in range(2):
                        mm(PS(pv_)[:, 0:T], onesb, xc16[:, kc, 0:T], start=(kc == 0), stop=(kc == 1))
                    act(sdt[:, 0:T], PS(pv_)[:, 0:T], AF.Ln, bias=1e-5, scale=1.0 / 256)
                    act(rst[:, 0:T], sdt[:, 0:T], AF.Exp, scale=-0.5)
                    for kc in range(2):
                        stt("dve", t512[:, 0:T], xc[:, kc, 0:T], vec[:, l, VEC_LNG + kc:VEC_LNG + kc + 1], rst[:, 0:T],
                            ALU.mult, ALU.mult)
                        act(mix[:, 6 + kc, 0:T], t512[:, 0:T], AF.Silu, bias=vec[:, l, VEC_LNB + kc:VEC_LNB + kc + 1])
                    if tile["last"]:
                        for si in range(nseg):
                            S.dma("sp", o_cc[l, segs[si][0]], seqstate(chist, chist_s, l, si))

                    ck(10)
                    for half in range(2):
                        wo = wget().rearrange("p (a b) -> p a b", a=8)
                        for m4 in range(4):
                            mc = half * 4 + m4
                            pb = pbank()
                            for kc in range(8):
                                mm(PS(pb)[:, 0:T], wo[:, kc, m4 * 128:(m4 + 1) * 128], mix[:, kc, 0:T],
                                   start=(kc == 0), stop=(kc == 7))
                            tt("dve", hT[:, mc, 0:T], hT[:, mc, 0:T], PS(pb)[:, 0:T], ALU.add)
                        wdone(1)

                    ck(11)
                    rmsnorm_to(hn, lambda kc: vec[:, l, VEC_GFFN + kc:VEC_GFFN + kc + 1], T)
                    for j in range(11):
                        wf = wget().rearrange("p (a b) -> p a b", a=8)
                        for c2 in range(2):
                            hc = j * 2 + c2
                            p1 = pbank()
                            p3 = pbank()
                            for kc in range(8):
                                mm(PS(p1)[:, 0:T], wf[:, kc, c2 * 128:(c2 + 1) * 128], hn[:, kc, 0:T],
                                   start=(kc == 0), stop=(kc == 7))
                            for kc in range(8):
                                mm(PS(p3)[:, 0:T], wf[:, kc, 256 + c2 * 128:256 + (c2 + 1) * 128], hn[:, kc, 0:T],
                                   start=(kc == 0), stop=(kc == 7))
                            act(sgp[:, 0:T], PS(p1)[:, 0:T], AF.Silu)
                            tt("dve", actb[:, hc, 0:T], sgp[:, 0:T], PS(p3)[:, 0:T], ALU.mult)
                        wdone(1)
                    for mc in range(8):
                        w2c = wget()[:, 0:22 * 128].rearrange("p (a b) -> p a b", a=22)
                        pb = pbank()
                        for kc in range(22):
                            mm(PS(pb)[:, 0:T], w2c[:, kc, :], actb[:, kc, 0:T], start=(kc == 0), stop=(kc == 21))
                        tt("dve", hT[:, mc, 0:T], hT[:, mc, 0:T], PS(pb)[:, 0:T], ALU.add)
                        wdone(1)

                    ck(12)
                    S.dma("sp", pTf[:, :, 0:T], pT[l][:, :, tok0:tok0 + T])
                    for kc in range(8):
                        cp("act", hn[:, kc, 0:T], hT[:, kc, 0:T])
                    cp("dve", pTb[:, :, 0:T], pTf[:, :, 0:T])
                    wg0 = wget().rearrange("p (a b) -> p a b", a=8)
                    wpe = wget()[:, 0:2048].rearrange("p (a b) -> p a b", a=2)
                    wg1 = None
                    for mc in range(8):
                        if mc == 4:
                            wdone(1)
                            wg1 = wget().rearrange("p (a b) -> p a b", a=8)
                        wg = wg0 if mc < 4 else wg1
                        m4 = mc % 4
                        pb = pbank()
                        for kc in range(8):
                            mm(PS(pb)[:, 0:T], wg[:, kc, m4 * 128:(m4 + 1) * 128], hn[:, kc, 0:T],
                               start=(kc == 0), stop=(kc == 7))
                        act(sgp[:, 0:T], PS(pb)[:, 0:T], AF.Sigmoid)
                        pb2 = pbank()
                        for kc in range(2):
                            mm(PS(pb2)[:, 0:T], wpe[:, kc, mc * 128:(mc + 1) * 128], pTb[:, kc, 0:T],
                               start=(kc == 0), stop=(kc == 1))
                        tt("dve", t512[:, 0:T], PS(pb2)[:, 0:T], sgp[:, 0:T], ALU.mult)
                        tt("dve", hT[:, mc, 0:T], hT[:, mc, 0:T], t512[:, 0:T], ALU.add)
                    wdone(2)
                    ck(13)

                rmsnorm_to(None, lambda kc: gf[:, kc:kc + 1], T, final_out=yout)
                S.dma("sp", yT[:, :, tok0:tok0 + T], yout[:, :, 0:T])

        except _Stop:
            print('STOPPED at', KSTOP)
            KT = int(os.environ.get("KTAIL", "0"))
            if KT == 1:
                memset("dve", t512, 0.0)
            elif KT == 2:
                cp("act", t512, sdt)
            elif KT == 3:
                mm(PS(7), onesb, sqb)
                cp("dve", t512, PS(7))
            elif KT == 4:
                mm(PS(7), onesb, sqb)
                cp("act", t512, PS(7))
        S.finish()
        S.finalize()
        print("ops recorded:", S.nops, {k: len(v) for k, v in S.prog.items()}, "signals", S.nsig)

        @block.tensor
        def _(e):
            S.replay("pe", e)

        @block.scalar
        def _(e):
            S.replay("act", e)

        @block.vector
        def _(e):
            S.replay("dve", e)

        @block.gpsimd
        def _(e):
            S.replay("pool", e)

        @block.sync
        def _(e):
            S.replay("sp", e)
    return nc


def _consts():
    c = np.zeros((128, NCONST), np.float32)
    c[:, C_ID:C_ID + 128] = np.eye(128)
    s = np.arange(128)[:, None]
    cc = np.arange(128)[None, :]
    c[:, C_MU:C_MU + 128] = np.where(s <= cc, 0.0, -30000.0)
    c[:, C_SU:C_SU + 128] = (s < cc).astype(np.float32)
    c[:, C_IOTA:C_IOTA + 512] = np.arange(1, 513)[None, :]
    c[:, C_ONES:C_ONES + 128] = 1.0
    return c


def _fm(v):
    C = v.shape[-1]
    r = v.reshape(v.shape[:-1] + (C // 128, 128))
    return np.moveaxis(r, -1, 0)


def _pack_weights(inp, L):
    w = np.zeros((L, NCHUNK, 128, CHW), np.float32)

    def kc_layout(mat):
        K, N = mat.shape
        return mat.reshape(K // 128, 128, N).transpose(1, 0, 2)

    for l in range(L):
        win = inp["w_in"][l]
        c0 = np.concatenate([win[:, 2056:2312], win[:, 2048:2056]], axis=1)
        a = kc_layout(c0).reshape(128, -1)
        g = kc_layout(inp["s5_glu_w"][l]).reshape(128, -1)
        w[l, 0, :, 0:a.shape[1]] = a
        w[l, 0, :, a.shape[1]:a.shape[1] + g.shape[1]] = g
        for h in range(4):
            cols = np.concatenate([win[:, 128 * h:128 * h + 128], win[:, 512 + 128 * h:512 + 128 * h + 128],
                                   win[:, 1024 + 128 * h:1024 + 128 * h + 128],
                                   win[:, 1536 + 128 * h:1536 + 128 * h + 128]], axis=1)
            w[l, 1 + h] = kc_layout(cols).reshape(128, -1)
        w[l, 5] = kc_layout(win[:, 2312:2824]).reshape(128, -1)
        wo = inp["w_out"][l]
        w[l, 6] = kc_layout(wo[:, 0:512]).reshape(128, -1)
        w[l, 7] = kc_layout(wo[:, 512:1024]).reshape(128, -1)
        for j in range(11):
            cols = np.concatenate([inp["ffn_w1"][l][:, 256 * j:256 * j + 256], inp["ffn_w3"][l][:, 256 * j:256 * j + 256]], axis=1)
            w[l, 8 + j] = kc_layout(cols).reshape(128, -1)
        for mc in range(8):
            a = kc_layout(inp["ffn_w2"][l][:, 128 * mc:128 * mc + 128]).reshape(128, -1)
            w[l, 19 + mc, :, 0:a.shape[1]] = a
        wg = inp["pe_gate_w"][l]
        w[l, 27] = kc_layout(wg[:, 0:512]).reshape(128, -1)
        w[l, 29] = kc_layout(wg[:, 512:1024]).reshape(128, -1)
        a = kc_layout(inp["pe_w"][l]).reshape(128, -1)
        w[l, 28, :, 0:a.shape[1]] = a
    return w


def _prep_shared(inp, L):
    sh = {}
    sh["wch"] = _pack_weights(inp, L)
    sh["consts"] = _consts()
    vec = np.zeros((128, L, 40), np.float32)
    for l in range(L):
        vec[:, l, 0:8] = _fm(inp["norm_mix"][l])
        vec[:, l, 8:16] = _fm(inp["norm_ffn"][l])
        vec[:, l, 16] = inp["gdn_norm"][l]
        vec[:, l, 17:19] = _fm(inp["s5_d"][l])
        vec[:, l, 19:21] = _fm(inp["s5_glu_b"][l])
        vec[:, l, 21:23] = _fm(inp["cc_dw_b"][l])
        vec[:, l, 23:25] = _fm(inp["cc_ln_g"][l])
        vec[:, l, 25:27] = _fm(inp["cc_ln_b"][l])
    sh["vecs"] = vec
    sh["gfin"] = np.ascontiguousarray(_fm(inp["norm_final"]))
    cw = inp["gdn_conv_w"][:L]
    sh["cwg"] = np.ascontiguousarray(cw.reshape(L, 4, 12, 128).transpose(3, 0, 2, 1))
    cc = inp["cc_dw_w"][:L]
    sh["cwc"] = np.ascontiguousarray(cc.reshape(L, 31, 2, 128).transpose(3, 0, 2, 1))
    ba8 = np.zeros((8, L, 2), np.float32)
    for l in range(L):
        ba8[4:8, l, 0] = inp["gdn_dt_bias"][l]
        ba8[4:8, l, 1] = inp["gdn_a_log"][l]
    sh["ba8"] = ba8
    s5col = np.zeros((128, L, 8, 3), np.float32)
    s5row = np.zeros((128, L, 3, 8, 128), np.float32)
    s5bt = np.zeros((L, 2, 128, 8, 128), np.float32)
    s5ct = np.zeros((L, 2, 128, 8, 128), np.float32)
    for l in range(L):
        for g in range(16):
            q, g2 = g // 2, g % 2
            ps = slice(g2 * 64, g2 * 64 + 64)
            s5col[ps, l, q, 0] = inp["s5_lam_re"][l, g]
            s5col[ps, l, q, 1] = inp["s5_lam_im"][l, g]
            s5col[ps, l, q, 2] = inp["s5_log_dt"][l, g]
            s5row[:, l, 0, q, ps] = inp["s5_lam_re"][l, g][None, :]
            s5row[:, l, 1, q, ps] = inp["s5_lam_im"][l, g][None, :]
            s5row[:, l, 2, q, ps] = inp["s5_log_dt"][l, g]
            r0 = (g % 8) * 16
            s5bt[l, 0, r0:r0 + 16, q, ps] = inp["s5_b_re"][l, g].T
            s5bt[l, 1, r0:r0 + 16, q, ps] = inp["s5_b_im"][l, g].T
            s5ct[l, 0, ps, q, r0:r0 + 16] = inp["s5_c_re"][l, g].T
            s5ct[l, 1, ps, q, r0:r0 + 16] = inp["s5_c_im"][l, g].T
    sh["s5col"] = s5col
    sh["s5row"] = s5row
    sh["s5bt"] = s5bt
    sh["s5ct"] = s5ct
    return sh


def _prep_core(inp, b, NPT, L):
    m = {}
    xp = inp["x_prompt"][b, :NPT * 512]
    xs = inp["x_sample"][2 * b:2 * b + 2].reshape(64, D)
    x = np.concatenate([xp, xs], axis=0)
    m["xT"] = np.ascontiguousarray(x.reshape(-1, 8, 128).transpose(2, 1, 0))
    pp = inp["p_prompt"][:L, b, :NPT * 512]
    ps = inp["p_sample"][:L, 2 * b:2 * b + 2].reshape(L, 64, 256)
    p = np.concatenate([pp, ps], axis=1)
    m["pT"] = np.ascontiguousarray(p.reshape(L, -1, 2, 128).transpose(0, 3, 2, 1))
    sg = inp["state_gdn"][:L, 2 * b:2 * b + 2]
    m["st_gdn"] = np.ascontiguousarray(sg.transpose(0, 1, 3, 2, 4))
    gc = inp["state_gdn_conv"][:L, 2 * b:2 * b + 2]
    m["st_gc"] = np.ascontiguousarray(gc.reshape(L, 2, 3, 12, 128).transpose(0, 1, 4, 3, 2))
    s5 = inp["state_s5"][:L, 2 * b:2 * b + 2]
    m["st_s5"] = np.ascontiguousarray(s5.reshape(L, 2, 8, 2, 64, 2).transpose(0, 1, 3, 4, 2, 5).reshape(L, 2, 128, 8, 2))
    cc = inp["state_conv"][:L, 2 * b:2 * b + 2]
    m["st_cc"] = np.ascontiguousarray(cc.reshape(L, 2, 30, 2, 128).transpose(0, 1, 4, 3, 2))
    return m


_PROG_CACHE = {}


def run_cores(inp, NPT, L, cores):
    key = (NPT, L)
    if key not in _PROG_CACHE:
        _PROG_CACHE[key] = build_program(NPT, L)
    nc = _PROG_CACHE[key]
    sh = _prep_shared(inp, L)
    in_maps = []
    for b in cores:
        m = dict(sh)
        m.update(_prep_core(inp, b, NPT, L))
        in_maps.append(m)
    res = run_bass_kernel_spmd(nc, in_maps, core_ids=list(range(len(cores))))
    return res


def assemble(res, NPT, L, ncores):
    B = ncores
    y_p = np.zeros((B, NPT * 512, D), np.float32)
    y_s = np.zeros((2 * B, DSEQ, D), np.float32)
    gdn = np.zeros((L, 3 * B, 4, 128, 128), np.float32)
    gcv = np.zeros((L, 3 * B, 3, 1536), np.float32)
    s5 = np.zeros((L, 3 * B, 16, 64, 2), np.float32)
    ccv = np.zeros((L, 3 * B, 30, 256), np.float32)
    for b in range(B):
        r = res.results[b]
        yT = r["yT"]
        y = yT.transpose(2, 1, 0).reshape(-1, D)
        y_p[b] = y[:NPT * 512]
        y_s[2 * b:2 * b + 2] = y[NPT * 512:].reshape(2, DSEQ, D)
        og = r["o_gdn"]
        gdn[:, 3 * b:3 * b + 3] = og.transpose(0, 1, 3, 2, 4)
        oc = r["o_gc"]
        gcv[:, 3 * b:3 * b + 3] = oc.transpose(0, 1, 4, 3, 2).reshape(L, 3, 3, 1536)
        o5 = r["o_s5"]
        s5[:, 3 * b:3 * b + 3] = o5.reshape(L, 3, 2, 64, 8, 2).transpose(0, 1, 4, 2, 3, 5).reshape(L, 3, 16, 64, 2)
        o3 = r["o_cc"]
        ccv[:, 3 * b:3 * b + 3] = o3.transpose(0, 1, 4, 3, 2).reshape(L, 3, 30, 256)
    pi = [3 * b for b in range(B)]
    si = [3 * b + 1 + k for b in range(B) for k in range(2)]
    return (y_p, y_s, gdn[:, pi], gcv[:, pi], s5[:, pi], ccv[:, pi],
            gdn[:, si], gcv[:, si], s5[:, si], ccv[:, si])


def kernel(**inputs):
    inp = {k: np.asarray(v) for k, v in inputs.items()}
    res = run_cores(inp, SEQ // 512, L_FULL, list(range(8)))
    return assemble(res, SEQ // 512, L_FULL, 8)
```

```python
import math
import os
import numpy as np
import ml_dtypes
import concourse.bass as bass
import concourse.mybir as mybir
from concourse.bass_utils import run_bass_kernel_spmd

F32 = mybir.dt.float32
BF16 = mybir.dt.bfloat16
I32 = mybir.dt.int32
AF = mybir.ActivationFunctionType
ALU = mybir.AluOpType

D = 1024
S5POOL = os.environ.get("S5POOL", "pool")
PUMP_LEVEL = 6
PUMP_SEQ = 3
GDN_F32 = True
L_FULL = 4
SEQ = 4096
DSEQ = 32
HID = 2816
NCHUNK = 30
CHW = 4096
TWO_PI = 2.0 * math.pi

C_ID, C_MU, C_SU, C_IOTA, C_ONES = 0, 128, 256, 384, 896
NCONST = 1024


def _esize(dt):
    return 2 if dt == BF16 else 4


class Sched:
    ENG = ("pe", "act", "dve", "pool", "sp")

    def __init__(self, nc, esems, dsems):
        self.nc = nc
        self.sem = {}
        self.cur = {}
        for n, s in zip(self.ENG, esems):
            self.sem[n] = s
            self.cur[n] = 0
        self.dq = {"sp": [], "pool": []}
        half = len(dsems) // 2
        for i, s in enumerate(dsems):
            k = "d%d" % i
            self.sem[k] = s
            self.cur[k] = 0
            self.dq["sp"].append(k)
        self.dnext = {"sp": 0, "pool": 0}
        self.clock = {n: {} for n in self.ENG}
        self.prog = {n: [] for n in self.ENG}
        self.blocks = {}
        self.tokvc = {}
        self.nops = 0

    def _blocks(self, ap):
        sp = str(ap.space)
        if "DRAM" in sp:
            return []
        a = ap.ap
        pstep = a[0][0]
        es = _esize(ap.dtype)
        off = int(ap.offset)
        col = off % pstep if pstep > 0 else off
        ext = 1
        for st, cnt in a[1:]:
            ext += (cnt - 1) * abs(st)
        lo = col * es
        hi = lo + ext * es
        key = "P" if "PSUM" in sp else "S"
        g = 2048 if key == "P" else 256
        return [(key, b) for b in range(lo // g, (hi - 1) // g + 1)]

    def _deps(self, eng, reads, writes):
        need = {}
        clk = self.clock[eng]

        def add(tok):
            if tok is None:
                return
            s, v = tok
            if eng == "pe" and s == "pe":
                return
            if clk.get(s, 0) >= v:
                return
            if need.get(s, 0) < v:
                need[s] = v

        rb = set()
        wb = set()
        for ap in reads:
            rb.update(self._blocks(ap))
        for ap in writes:
            wb.update(self._blocks(ap))
        for b in rb:
            st = self.blocks.get(b)
            if st is not None:
                add(st[0])
                if b[0] == "P":
                    for s, v in st[1].items():
                        if s != eng:
                            add((s, v))
        for b in wb:
            st = self.blocks.get(b)
            if st is not None:
                add(st[0])
                for s, v in st[1].items():
                    add((s, v))
        return need, rb, wb

    def _emit_waits(self, eng, need):
        clk = self.clock[eng]
        for s, v in need.items():
            if clk.get(s, 0) >= v:
                continue
            sem = self.sem[s]
            self.prog[eng].append(("w", sem, v, s))
            vc = self.tokvc.get((s, v))
            if vc is not None:
                for k2, v2 in vc.items():
                    if clk.get(k2, 0) < v2:
                        clk[k2] = v2
            if clk.get(s, 0) < v:
                clk[s] = v

    def _commit(self, tok, eng, rb, wb):
        vc = dict(self.clock[eng])
        vc[tok[0]] = tok[1]
        self.tokvc[tok] = vc
        for b in rb:
            st = self.blocks.get(b)
            if st is None:
                st = [None, {}]
                self.blocks[b] = st
            st[1][tok[0]] = tok[1]
        for b in wb:
            self.blocks[b] = [tok, {}]
        self.nops += 1
        if len(self.tokvc) > 60000:
            keys = list(self.tokvc.keys())
            for k in keys[:30000]:
                del self.tokvc[k]

    def op(self, eng, fn, reads, writes):
        need, rb, wb = self._deps(eng, reads, writes)
        self._emit_waits(eng, need)
        self.cur[eng] += 1
        tok = (eng, self.cur[eng])
        self.prog[eng].append(("i", fn, self.sem[eng], 1))
        self._commit(tok, eng, rb, wb)
        return tok

    def dma(self, q, out, in_, extra_tokens=()):
        ring = self.dq[q]
        k = ring[self.dnext[q] % len(ring)]
        self.dnext[q] += 1
        need, rb, wb = self._deps(q, [in_], [out])
        clk = self.clock[q]
        prev = self.cur[k]
        if prev > 0 and clk.get(k, 0) < prev:
            need[k] = max(need.get(k, 0), prev)
        for s, v in extra_tokens:
            if clk.get(s, 0) < v:
                need[s] = max(need.get(s, 0), v)
        self._emit_waits(q, need)
        self.cur[k] += 16
        tok = (k, self.cur[k])
        self.prog[q].append(("i", lambda e, o=out, i=in_: e.dma_start(out=o, in_=i), self.sem[k], 16))
        self._commit(tok, q, rb, wb)
        return tok

    def barrier(self):
        for eng in self.ENG:
            need = {}
            for s, v in self.cur.items():
                if s == eng and eng == "pe":
                    continue
                if v > 0 and self.clock[eng].get(s, 0) < v:
                    need[s] = v
            self._emit_waits(eng, need)

    def finish(self):
        for eng in self.ENG:
            need = {}
            for s, v in self.cur.items():
                if s == eng:
                    continue
                if v > 0 and self.clock[eng].get(s, 0) < v:
                    need[s] = v
            self._emit_waits(eng, need)

    def finalize(self):
        waited = {n: set() for n in self.ENG}
        for eng in self.ENG:
            for it in self.prog[eng]:
                if it[0] == "w" and it[3] in waited:
                    waited[it[3]].add(it[2])
        self.tickmap = {}
        for n in self.ENG:
            self.tickmap[n] = {v: i + 1 for i, v in enumerate(sorted(waited[n]))}
        self.nsig = {n: len(waited[n]) for n in self.ENG}

    def replay(self, eng, e):
        seq = 0
        tm = self.tickmap
        for it in self.prog[eng]:
            if it[0] == "w":
                k = it[3]
                if k in tm:
                    e.wait_ge(it[1], tm[k][it[2]])
                else:
                    e.wait_ge(it[1], it[2])
            else:
                ins = it[1](e)
                if it[3] == 16:
                    ins.then_inc(it[2], 16)
                else:
                    seq += 1
                    if seq in tm[eng]:
                        ins.then_inc(it[2], 1)


class _Stop(Exception):
    pass


import os
KSTOP = int(os.environ.get("KSTOP", "0"))


def ck(n):
    if KSTOP == n:
        raise _Stop()


class Alloc:
    def __init__(self, big, words):
        self.big = big
        self.words = words
        self.off = 0

    def get(self, dtype, shape):
        n = 1
        for s in shape[1:]:
            n *= s
        w = n if dtype != BF16 else (n + 1) // 2
        w = (w + 63) // 64 * 64
        o = self.off
        self.off += w
        assert self.off <= self.words, "SBUF overflow %d > %d" % (self.off, self.words)
        ap = self.big[:, o:o + w]
        if dtype == BF16:
            ap = ap.bitcast(BF16)
        elif dtype == I32:
            ap = ap.bitcast(I32)
        ap = ap[:, 0:n]
        if len(shape) == 3:
            ap = ap.rearrange("p (a b) -> p a b", a=shape[1])
        elif len(shape) == 4:
            ap = ap.rearrange("p (a b c) -> p a b c", a=shape[1], b=shape[2])
        if shape[0] < 128:
            ap = ap[0:shape[0]]
        return ap


def bc(ap, shape):
    return ap.to_broadcast(list(shape))


def build_program(NPT, L, with_sample=True):
    nc = bass.Bass("TRN2", target_bir_lowering=False)
    NTOK = NPT * 512 + 64

    def din(name, shape, dt=F32):
        return nc.dram_tensor(name, list(shape), dt, kind="ExternalInput").ap()

    def dout(name, shape, dt=F32):
        return nc.dram_tensor(name, list(shape), dt, kind="ExternalOutput").ap()

    xT = din("xT", [128, 8, NTOK])
    pT = din("pT", [L, 128, 2, NTOK])
    wch = din("wch", [L, NCHUNK, 128, CHW])
    consts = din("consts", [128, NCONST])
    vecs = din("vecs", [128, L, 40])
    gfin = din("gfin", [128, 8])
    cwg = din("cwg", [128, L, 12, 4])
    cwc = din("cwc", [128, L, 2, 31])
    ba8 = din("ba8", [8, L, 2])
    s5col = din("s5col", [128, L, 8, 3])
    s5row = din("s5row", [128, L, 3, 8, 128])
    s5bt = din("s5bt", [L, 2, 128, 8, 128])
    s5ct = din("s5ct", [L, 2, 128, 8, 128])
    st_gdn = din("st_gdn", [L, 2, 128, 4, 128])
    st_gc = din("st_gc", [L, 2, 128, 12, 3])
    st_s5 = din("st_s5", [L, 2, 128, 8, 2])
    st_cc = din("st_cc", [L, 2, 128, 2, 30])

    yT = dout("yT", [128, 8, NTOK])
    o_gdn = dout("o_gdn", [L, 3, 128, 4, 128])
    o_gc = dout("o_gc", [L, 3, 128, 12, 3])
    o_s5 = dout("o_s5", [L, 3, 128, 8, 2])
    o_cc = dout("o_cc", [L, 3, 128, 2, 30])

    wscr = nc.dram_tensor("wscr", [L, NCHUNK, 128, CHW], BF16, kind="Internal").ap()
    s5tab = nc.dram_tensor("s5tab", [L, 8, 128, 2, 512], F32, kind="Internal").ap()
    s5mscr = nc.dram_tensor("s5mscr", [L, 128, 4, 8, 128], BF16, kind="Internal").ap()
    dgscr = nc.dram_tensor("dgscr", [L, 128, 12, 4, 128], BF16, kind="Internal").ap()
    dcscr = nc.dram_tensor("dcscr", [L, 128, 2, 31, 128], BF16, kind="Internal").ap()

    SBW = 53000
    NDS = 6
    import contextlib
    with contextlib.ExitStack() as es:
        big = es.enter_context(nc.sbuf_tensor("big", [128, SBW], F32))
        psum = es.enter_context(nc.psum_tensor("psum", [128, 8, 512], F32))
        esems = [es.enter_context(nc.semaphore("e_%s" % n)) for n in Sched.ENG]
        dsems = [es.enter_context(nc.semaphore("dm_%d" % i)) for i in range(NDS)]
        block = es.enter_context(nc.Block())
        S = Sched(nc, esems, dsems)
        A = Alloc(big, SBW)

        def mm(out, lhsT, rhs, start=True, stop=True):
            S.op("pe", lambda e: e.matmul(out, lhsT=lhsT, rhs=rhs, start=start, stop=stop),
                 [lhsT, rhs], [out])

        def tr(out, in_, ident):
            S.op("pe", lambda e: e.transpose(out=out, in_=in_, identity=ident), [in_, ident], [out])

        def act(out, in_, func, bias=None, scale=None):
            kw = {}
            r = [in_]
            if bias is not None:
                kw["bias"] = bias
                if not isinstance(bias, float):
                    r.append(bias)
            if scale is not None:
                kw["scale"] = scale
                if not isinstance(scale, float):
                    r.append(scale)
            S.op("act", lambda e: e.activation(out=out, in_=in_, func=func, **kw), r, [out])

        def tt(eng, out, in0, in1, op):
            S.op(eng, lambda e: e.tensor_tensor(out=out, in0=in0, in1=in1, op=op), [in0, in1], [out])

        def ts(eng, out, in0, s1, op0, s2=None, op1=None):
            r = [in0]
            if not isinstance(s1, float):
                r.append(s1)
            if s2 is not None and not isinstance(s2, float):
                r.append(s2)
            if op1 is None:
                S.op(eng, lambda e: e.tensor_scalar(out=out, in0=in0, scalar1=s1, scalar2=None, op0=op0), r, [out])
            else:
                S.op(eng, lambda e: e.tensor_scalar(out=out, in0=in0, scalar1=s1, scalar2=s2, op0=op0, op1=op1), r, [out])

        def stt(eng, out, in0, scalar, in1, op0, op1):
            r = [in0, in1]
            if not isinstance(scalar, float):
                r.append(scalar)
            S.op(eng, lambda e: e.scalar_tensor_tensor(out=out, in0=in0, scalar=scalar, in1=in1, op0=op0, op1=op1), r, [out])

        def cp(eng, out, in_):
            if eng == "act":
                act(out, in_, AF.Copy)
            else:
                S.op(eng, lambda e: e.tensor_copy(out=out, in_=in_), [in_], [out])

        def memset(eng, out, val):
            S.op(eng, lambda e: e.memset(out, val), [], [out])

        def recip(out, in_):
            S.op("dve", lambda e: e.reciprocal(out=out, in_=in_), [in_], [out])

        def scan(out, d0, d1, init):
            r = [d0, d1]
            if not isinstance(init, float):
                r.append(init)
            S.op("dve", lambda e: e.tensor_tensor_scan(out=out, data0=d0, data1=d1, initial=init,
                                                       op0=ALU.mult, op1=ALU.add), r, [out])

        pcnt = [0]
        pinned = set()

        def pbank(pin=False):
            while True:
                b = pcnt[0] % 8
                pcnt[0] += 1
                if b not in pinned:
                    break
            if pin:
                pinned.add(b)
            return b

        def PS(b, shape=None, dt=F32):
            ap = psum[:, b, :]
            if dt == BF16:
                ap = ap.bitcast(BF16)
            if shape is None:
                return ap
            n = 1
            for s in shape[1:]:
                n *= s
            ap = ap[:, 0:n]
            if len(shape) == 3:
                ap = ap.rearrange("p (a b) -> p a b", a=shape[1])
            if shape[0] < 128:
                ap = ap[0:shape[0]]
            return ap

        cst = A.get(F32, [128, NCONST])
        ident = cst[:, C_ID:C_ID + 128]
        masku = cst[:, C_MU:C_MU + 128]
        su = cst[:, C_SU:C_SU + 128]
        iota1 = cst[:, C_IOTA:C_IOTA + 512]
        ones = cst[:, C_ONES:C_ONES + 128]
        identb = A.get(BF16, [128, 128])
        onesb = A.get(BF16, [128, 128])
        vec = A.get(F32, [128, L, 40])
        gf = A.get(F32, [128, 8])
        cwg_s = A.get(F32, [128, L, 12, 4])
        cwc_s = A.get(F32, [128, L, 2, 31])
        ba8_s = A.get(F32, [8, L, 2])
        na8 = A.get(F32, [8, L])
        s5c = A.get(F32, [128, L, 8, 3])
        s5r = A.get(F32, [128, L, 8])
        hT = A.get(F32, [128, 8, 512])
        hn = A.get(BF16, [128, 8, 512])
        mix = A.get(BF16, [128, 8, 512])
        Sst = [A.get(F32, [128, 4, 128]) for _ in range(L)]
        ghist = [A.get(F32, [128, 12, 3]) for _ in range(L)]
        chist = [A.get(F32, [128, 2, 30]) for _ in range(L)]
        xst = [A.get(F32, [128, 8, 2]) for _ in range(L)]
        Sst_s = [A.get(F32, [128, 4, 128]) for _ in range(2)]
        ghist_s = [A.get(F32, [128, 12, 3]) for _ in range(2)]
        chist_s = [A.get(F32, [128, 2, 30]) for _ in range(2)]
        xst_s = [A.get(F32, [128, 8, 2]) for _ in range(2)]
        NSLOT = 5
        wslot = [A.get(BF16, [128, CHW]) for _ in range(NSLOT)]
        s5m = A.get(BF16, [128, 4, 8, 128])
        persist_end = A.off

        try:
            S.dma("sp", cst, consts)
            S.dma("sp", vec, vecs)
            S.dma("sp", gf, gfin)
            S.dma("sp", cwg_s, cwg)
            S.dma("sp", cwc_s, cwc)
            S.dma("sp", ba8_s, ba8)
            S.dma("sp", s5c, s5col)
            cp("dve", identb, ident)
            cp("dve", onesb, ones)
            act(na8, ba8_s[:, :, 1], AF.Exp)
            ts("dve", na8, na8, -1.0, ALU.mult)
            ck(1)
            for l in range(L):
                memset("dve", Sst[l], 0.0)
                memset("dve", ghist[l], 0.0)
                memset("dve", chist[l], 0.0)
                memset("dve", xst[l], 0.0)

            def range_reduce(dst, src, tmpi, tmpf):
                ts("dve", tmpi, src, 1.0 / TWO_PI, ALU.mult)
                cp("dve", tmpf, tmpi)
                stt("dve", dst, tmpf, -TWO_PI, src, ALU.mult, ALU.add)
                ts("dve", dst, dst, -3.1415925, ALU.max, 3.1415925, ALU.min)

            A.off = persist_end
            th = A.get(F32, [128, L, 8])
            dtc = A.get(F32, [128, L, 8])
            t_i = A.get(I32, [128, 512])
            t_f = A.get(F32, [128, 512])
            t_a = A.get(F32, [128, 512])
            t_b = A.get(F32, [128, 512])
            tabs = [A.get(F32, [128, 2, 512]) for _ in range(2)]
            act(dtc, s5c[:, :, :, 2], AF.Exp)
            tt("dve", th, s5c[:, :, :, 1], dtc, ALU.mult)
            tt("dve", s5r, s5c[:, :, :, 0], dtc, ALU.mult)
            act(s5r, s5r, AF.Exp)
            thf = th.rearrange("p a b -> p (a b)")
            range_reduce(thf, thf, t_i[:, 0:L * 8], t_f[:, 0:L * 8])
            k = 0
            for l in range(L):
                for q in range(8):
                    tb = tabs[k % 2]
                    k += 1
                    ts("dve", t_a, iota1, th[:, l, q:q + 1], ALU.mult)
                    range_reduce(t_b, t_a, t_i, t_f)
                    act(tb[:, 1, :], t_b, AF.Sin)
                    ts("dve", t_a, t_b, math.pi / 2, ALU.add)
                    range_reduce(t_b, t_a, t_i, t_f)
                    act(tb[:, 0, :], t_b, AF.Sin)
                    S.dma("sp", s5tab[l, q], tb)
            ck(2)
            s5stage = A.get(F32, [128, 3, 8, 128])
            s5t = [A.get(F32, [128, 512]) for _ in range(6)]
            for l in range(L):
                S.dma("sp", s5stage[:, 0], s5row[:, l, 0])
                S.dma("sp", s5stage[:, 1], s5row[:, l, 1])
                S.dma("sp", s5stage[:, 2], s5row[:, l, 2])
                lre, lim, ldt = s5stage[:, 0], s5stage[:, 1], s5stage[:, 2]
                act(ldt, ldt, AF.Exp)
                for hq in range(2):
                    sl = slice(hq * 4, hq * 4 + 4)
                    a_re = lre[:, sl, :].rearrange("p a b -> p (a b)")
                    a_im = lim[:, sl, :].rearrange("p a b -> p (a b)")
                    a_dt = ldt[:, sl, :].rearrange("p a b -> p (a b)")
                    w0, w1, w2, w3, w4, w5 = s5t
                    tt("dve", w0, a_im, a_dt, ALU.mult)
                    range_reduce(w1, w0, t_i, t_f)
                    act(w2, w1, AF.Sin)
                    ts("dve", w0, w1, math.pi / 2, ALU.add)
                    range_reduce(w1, w0, t_i, t_f)
                    act(w3, w1, AF.Sin)
                    tt("dve", w0, a_re, a_dt, ALU.mult)
                    act(w0, w0, AF.Exp)
                    tt("dve", w3, w3, w0, ALU.mult)
                    ts("dve", w3, w3, -1.0, ALU.add)
                    tt("dve", w2, w2, w0, ALU.mult)
                    tt("dve", w0, a_re, a_re, ALU.mult)
                    tt("dve", w1, a_im, a_im, ALU.mult)
                    tt("dve", w0, w0, w1, ALU.add)
                    recip(w0, w0)
                    tt("dve", w1, w3, a_re, ALU.mult)
                    tt("dve", w4, w2, a_im, ALU.mult)
                    tt("dve", w1, w1, w4, ALU.add)
                    tt("dve", w1, w1, w0, ALU.mult)
                    tt("dve", w4, w2, a_re, ALU.mult)
                    tt("dve", w5, w3, a_im, ALU.mult)
                    tt("dve", w4, w4, w5, ALU.subtract)
                    tt("dve", w4, w4, w0, ALU.mult)
                    S.dma("sp", w2.rearrange("p (a b) -> p a b", a=4), s5bt[l, 0][:, sl, :])
                    S.dma("sp", w3.rearrange("p (a b) -> p a b", a=4), s5bt[l, 1][:, sl, :])
                    tt("dve", w0, w1, w2, ALU.mult)
                    tt("dve", w5, w4, w3, ALU.mult)
                    tt("dve", s5m[:, 0, sl, :].rearrange("p a b -> p (a b)"), w0, w5, ALU.subtract)
                    tt("dve", w0, w1, w3, ALU.mult)
                    tt("dve", w5, w4, w2, ALU.mult)
                    tt("dve", s5m[:, 1, sl, :].rearrange("p a b -> p (a b)"), w0, w5, ALU.add)
                    S.dma("sp", w2.rearrange("p (a b) -> p a b", a=4), s5ct[l, 0][:, sl, :])
                    S.dma("sp", w3.rearrange("p (a b) -> p a b", a=4), s5ct[l, 1][:, sl, :])
                    cp("dve", s5m[:, 2, sl, :].rearrange("p a b -> p (a b)"), w2)
                    ts("dve", s5m[:, 3, sl, :].rearrange("p a b -> p (a b)"), w3, -1.0, ALU.mult)

                S.dma("sp", s5mscr[l], s5m)
            ck(3)
            dgst = A.get(BF16, [128, 12, 4, 128])
            dcst = A.get(BF16, [128, 2, 31, 128])
            for l in range(L):
                for c in range(12):
                    for j in range(4):
                        act(dgst[:, c, j, :], identb, AF.Copy, scale=cwg_s[:, l, c, j:j + 1])
                S.dma("sp", dgscr[l], dgst)
                for kc in range(2):
                    for j in range(31):
                        ts("dve", dcst[:, kc, j, :], identb, cwc_s[:, l, kc, j:j + 1], ALU.mult)
                S.dma("sp", dcscr[l], dcst)
            A.off = persist_end
            stg = [A.get(F32, [128, CHW]) for _ in range(4)]
            stb = [A.get(BF16, [128, CHW]) for _ in range(2)]
            allc = [(l, j) for l in range(L) for j in range(NCHUNK)]
            NST = len(stg)
            for k in range(min(NST - 1, len(allc))):
                S.dma("sp", stg[k % NST], wch[allc[k][0], allc[k][1]])
            for k, (l, j) in enumerate(allc):
                kn = k + NST - 1
                if kn < len(allc):
                    S.dma("sp", stg[kn % NST], wch[allc[kn][0], allc[kn][1]])
                b = stb[k % 2]
                cp("dve" if k % 2 == 0 else "act", b, stg[k % NST])
                S.dma("sp", wscr[l, j], b)
            S.barrier()
            ck(4)
            A.off = persist_end

            wseq = []
            tiles = []
            for i in range(NPT):
                tiles.append(dict(tok0=i * 512, T=512, segs=[(0, 512)], C=128, last=(i == NPT - 1)))
            if with_sample:
                tiles.append(dict(tok0=NPT * 512, T=64, segs=[(1, 32), (2, 32)], C=32, last=True))
            for ti in range(len(tiles)):
                for l in range(L):
                    for j in range(NCHUNK):
                        wseq.append((l, j))
            wstate = dict(issued=0, used=0, released=0)

            def wpump():
                while wstate["issued"] < len(wseq) and wstate["issued"] - NSLOT < wstate["released"]:
                    m = wstate["issued"]
                    l_, j_ = wseq[m]
                    S.dma("sp", wslot[m % NSLOT], wscr[l_, j_])
                    wstate["issued"] += 1

            def wget():
                n = wstate["used"]
                wstate["used"] += 1
                wpump()
                assert wstate["issued"] > n
                return wslot[n % NSLOT]

            def wdone(k=1):
                wstate["released"] += k
                wpump()

            gluw = A.get(BF16, [128, 2, 256])
            sqb = A.get(BF16, [128, 512])
            sdt = A.get(F32, [128, 512])
            rst = A.get(F32, [128, 512])
            t512 = A.get(F32, [128, 512])
            sgp = A.get(F32, [128, 512])
            u5f = A.get(F32, [128, 2, 512])
            u5b = A.get(BF16, [128, 2, 512])
            selh = A.get(F32, [8, 2, 128])
            tabq = [A.get(F32, [128, 2, 512]) for _ in range(2)]
            s5t = [A.get(F32, [128, 512]) for _ in range(6)]
            xbre = A.get(BF16, [128, 512])
            xbim = A.get(BF16, [128, 512])
            ygf = u5f
            ygb = u5b
            ov0 = A.off
            xpb = A.get(BF16, [128, 3 * 2 * 520])
            dgc = A.get(BF16, [128, 3, 4, 128])
            qkv = A.get(F32, [128, 3, 512])
            zs = A.get(F32, [128, 512])
            kTb = A.get(BF16, [128, 512])
            kbT = A.get(BF16, [128, 512])
            qTb = A.get(BF16, [128, 512])
            qgT = A.get(BF16, [128, 512])
            betaB = A.get(F32, [128, 512])
            gcB = A.get(F32, [128, 512])
            egB = A.get(F32, [128, 512])
            oT = A.get(F32, [128, 512])
            sig8 = A.get(F32, [8, 512])
            g8 = A.get(F32, [8, 512])
            e8 = A.get(F32, [8, 512])
            cols = A.get(F32, [128, 4, 2])
            negc = A.get(F32, [128, 4])
            ecol = A.get(F32, [128, 4])
            bexp = A.get(F32, [128, 4])
            kdcol = A.get(F32, [128, 4])
            GD = F32 if GDN_F32 else BF16
            kbg = A.get(GD, [128, 4, 128])
            kd = A.get(BF16, [128, 4, 128])
            vbt = A.get(GD, [128, 4, 128])
            Du = A.get(F32, [128, 4, 128])
            EU = A.get(F32, [128, 4, 128])
            EUs = A.get(F32, [128, 4, 128])
            Ub = [A.get(GD, [128, 4, 128]) for _ in range(2)]
            Lb = [A.get(GD, [128, 4, 128]) for _ in range(2)]
            Rb = [A.get(GD, [128, 4, 128]) for _ in range(2)]
            Aqk = A.get(BF16, [128, 4, 128])
            nwT = A.get(GD, [128, 4, 128])
            vnew = A.get(BF16, [128, 128])
            Sbf = A.get(BF16, [128, 128])
            ov_end = A.off
            A.off = ov0
            xcb = A.get(BF16, [128, 2 * 2 * 544])
            dgcc = A.get(BF16, [128, 2, 31, 128])
            xg = A.get(F32, [128, 2, 512])
            xc16 = A.get(BF16, [128, 2, 512])
            ov_end = max(ov_end, A.off)
            A.off = ov0
            act_base = A.get(F32, [128, 22 * 256])
            actb = act_base.bitcast(BF16).rearrange("p (a b) -> p a b", a=22)
            yout = act_base[:, 0:8 * 512].rearrange("p (a b) -> p a b", a=8)
            pTf = A.get(F32, [128, 2, 512])
            pTb = A.get(BF16, [128, 2, 512])
            ov_end = max(ov_end, A.off)
            A.off = ov_end
            print("SBUF words used", A.off, "of", SBW)

            def rmsnorm_to(dst_bf, gcol_fn, T, final_out=None):
                pb = pbank()
                for kc in range(8):
                    act(sqb[:, 0:T], hT[:, kc, 0:T], AF.Square)
                    mm(PS(pb)[:, 0:T], onesb, sqb[:, 0:T], start=(kc == 0), stop=(kc == 7))
                act(sdt[:, 0:T], PS(pb)[:, 0:T], AF.Ln, bias=1e-6, scale=1.0 / D)
                act(rst[:, 0:T], sdt[:, 0:T], AF.Exp, scale=-0.5)
                for kc in range(8):
                    o = dst_bf[:, kc, 0:T] if final_out is None else final_out[:, kc, 0:T]
                    stt("dve", o, hT[:, kc, 0:T], gcol_fn(kc), rst[:, 0:T], ALU.mult, ALU.mult)

            VEC_GMIX, VEC_GFFN, VEC_GN, VEC_S5D, VEC_GLUB, VEC_DWB, VEC_LNG, VEC_LNB = 0, 8, 16, 17, 19, 21, 23, 25

            for tile in tiles:
                T = tile["T"]
                tok0 = tile["tok0"]
                segs = tile["segs"]
                nseg = len(segs)
                Ls = segs[0][1]
                C = tile["C"]
                nb = T // C
                m_lev = int(math.log2(C))
                is_sample = segs[0][0] != 0
                S.dma("sp", hT[:, :, 0:T], xT[:, :, tok0:tok0 + T])

                def seqstate(lst_p, lst_s, l, si):
                    seq = segs[si][0]
                    return lst_p[l] if seq == 0 else lst_s[seq - 1]

                for l in range(L):
                    if is_sample:
                        for si in range(nseg):
                            S.dma("sp", Sst_s[si], st_gdn[l, si])
                            S.dma("sp", ghist_s[si], st_gc[l, si])
                            S.dma("sp", chist_s[si], st_cc[l, si])
                            S.dma("sp", xst_s[si], st_s5[l, si])
                    S.dma("sp", s5m, s5mscr[l])

                    rmsnorm_to(hn, lambda kc: vec[:, l, VEC_GMIX + kc:VEC_GMIX + kc + 1], T)
                    ck(5)

                    w4 = wget()
                    w4a = w4[:, 0:8 * 264].rearrange("p (a b) -> p a b", a=8)
                    cp("dve", gluw.rearrange("p a b -> p (a b)"), w4[:, 8 * 264:8 * 264 + 512])
                    pb_ba = pbank()
                    for kc in range(8):
                        mm(PS(pb_ba)[0:8, 0:T], w4a[:, kc, 256:264], hn[:, kc, 0:T], start=(kc == 0), stop=(kc == 7))
                    pu = [pbank(), pbank()]
                    for mc in range(2):
                        for kc in range(8):
                            mm(PS(pu[mc])[:, 0:T], w4a[:, kc, mc * 128:(mc + 1) * 128], hn[:, kc, 0:T],
                               start=(kc == 0), stop=(kc == 7))
                    wdone(1)
                    ck(51)
                    act(sig8[:, 0:T], PS(pb_ba)[0:8, 0:T], AF.Sigmoid)
                    ck(52)
                    act(e8[:, 0:T], PS(pb_ba)[0:8, 0:T], AF.Exp, bias=ba8_s[:, l, 0:1])
                    ck(53)
                    act(e8[:, 0:T], e8[:, 0:T], AF.Ln, bias=1.0)
                    ck(54)
                    ts("dve", g8[:, 0:T], e8[:, 0:T], na8[:, l:l + 1], ALU.mult)
                    ck(55)
                    for mc in range(2):
                        cp("act", u5f[:, mc, 0:T], PS(pu[mc])[:, 0:T])
                        ck(56 + mc * 2)
                        cp("dve", u5b[:, mc, 0:T], PS(pu[mc])[:, 0:T])
                        ck(57 + mc * 2)
                    ck(6)

                    def s5_gen(l=l, T=T, Ls=Ls, nseg=nseg):
                        py = [pbank(pin=True), pbank(pin=True)]
                        for q in range(8):
                            kc = q // 4
                            tb = tabq[q % 2]
                            S.dma("sp", tb[:, :, 0:Ls], s5tab[l, q][:, :, 0:Ls])
                            p_re = pbank(pin=True)
                            p_im = pbank(pin=True)
                            mm(PS(p_re)[:, 0:T], s5m[:, 0, q, :], u5b[:, kc, 0:T])
                            mm(PS(p_im)[:, 0:T], s5m[:, 1, q, :], u5b[:, kc, 0:T])
                            yield
                            w0, w1, w2, w3, w4, w5 = s5t
                            for si in range(nseg):
                                cs = slice(si * Ls, (si + 1) * Ls)
                                nC = tb[:, 0, 0:Ls]
                                nS = tb[:, 1, 0:Ls]
                                xs_ = seqstate(xst, xst_s, l, si)
                                tt("dve", w0[:, cs], nC, PS(p_re)[:, cs], ALU.mult)
                                yield
                                tt("dve", w1[:, cs], nS, PS(p_im)[:, cs], ALU.mult)
                                yield
                                tt(S5POOL, w0[:, cs], w0[:, cs], w1[:, cs], ALU.add)
                                yield
                                tt("dve", w2[:, cs], nC, PS(p_im)[:, cs], ALU.mult)
                                yield
                                tt("dve", w3[:, cs], nS, PS(p_re)[:, cs], ALU.mult)
                                yield
                                tt(S5POOL, w2[:, cs], w2[:, cs], w3[:, cs], ALU.subtract)
                                yield
                                scan(w1[:, cs], bc(s5r[:, l, q:q + 1], [128, Ls]), w0[:, cs], xs_[:, q, 0:1])
                                yield
                                scan(w3[:, cs], bc(s5r[:, l, q:q + 1], [128, Ls]), w2[:, cs], xs_[:, q, 1:2])
                                yield
                                tt(S5POOL, w4[:, cs], nC, w1[:, cs], ALU.mult)
                                yield
                                tt(S5POOL, w5[:, cs], nS, w3[:, cs], ALU.mult)
                                yield
                                tt("dve", w0[:, cs], w4[:, cs], w5[:, cs], ALU.subtract)
                                cp("act", xbre[:, cs], w0[:, cs])
                                yield
                                tt(S5POOL, w4[:, cs], nS, w1[:, cs], ALU.mult)
                                yield
                                tt(S5POOL, w5[:, cs], nC, w3[:, cs], ALU.mult)
                                yield
                                tt("dve", w2[:, cs], w4[:, cs], w5[:, cs], ALU.add)
                                cp("act", xbim[:, cs], w2[:, cs])
                                yield
                                e_ = (si + 1) * Ls - 1
                                cp("dve", xs_[:, q, 0:1], w0[:, e_:e_ + 1])
                                cp("dve", xs_[:, q, 1:2], w2[:, e_:e_ + 1])
                                yield
                            pinned.discard(p_re)
                            pinned.discard(p_im)
                            mm(PS(py[kc])[:, 0:T], s5m[:, 2, q, :], xbre[:, 0:T], start=(q % 4 == 0), stop=False)
                            mm(PS(py[kc])[:, 0:T], s5m[:, 3, q, :], xbim[:, 0:T], start=False, stop=(q % 4 == 3))
                            yield
                        for kc in range(2):
                            stt("dve", ygf[:, kc, 0:T], u5f[:, kc, 0:T], vec[:, l, VEC_S5D + kc:VEC_S5D + kc + 1],
                                PS(py[kc])[:, 0:T], ALU.mult, ALU.add)
                            act(ygf[:, kc, 0:T], ygf[:, kc, 0:T], AF.Gelu_apprx_tanh)
                            cp("dve", ygb[:, kc, 0:T], ygf[:, kc, 0:T])
                            yield
                        pinned.discard(py[0])
                        pinned.discard(py[1])
                        for mc in range(2):
                            pb = pbank()
                            for kc in range(2):
                                mm(PS(pb)[:, 0:T], gluw[:, kc, mc * 128:(mc + 1) * 128], ygb[:, kc, 0:T],
                                   start=(kc == 0), stop=(kc == 1))
                            act(s5t[4][:, 0:T], PS(pb)[:, 0:T], AF.Sigmoid, bias=vec[:, l, VEC_GLUB + mc:VEC_GLUB + mc + 1])
                            tt("dve", mix[:, 4 + mc, 0:T], ygf[:, mc, 0:T], s5t[4][:, 0:T], ALU.mult)
                            yield

                    s5g = s5_gen()

                    def pump(n):
                        for _ in range(n):
                            try:
                                next(s5g)
                            except StopIteration:
                                return

                    for h in range(4):
                        wh = wget().rearrange("p (a b) -> p a b", a=8)
                        xv = xpb[:, 0:3 * nseg * (4 + Ls)].rearrange("p (a b c) -> p a b c", a=3, b=nseg)
                        for si in range(nseg):
                            gh = seqstate(ghist, ghist_s, l, si)
                            for c3 in range(3):
                                cp("dve", xv[:, c3, si, 1:4], gh[:, c3 * 4 + h, :])
                        S.dma("sp", dgc, dgscr[l].rearrange("p (a b) j m -> p a b j m", b=4)[:, :, h, :, :])
                        ck(61)
                        pz = None
                        for c4 in range(4):
                            pb = pbank()
                            for kc in range(8):
                                mm(PS(pb)[:, 0:T], wh[:, kc, c4 * 128:(c4 + 1) * 128], hn[:, kc, 0:T],
                                   start=(kc == 0), stop=(kc == 7))
                            if c4 < 3:
                                for si in range(nseg):
                                    gh = seqstate(ghist, ghist_s, l, si)
                                    cp("act", xv[:, c4, si, 4:4 + Ls], PS(pb)[:, si * Ls:(si + 1) * Ls])
                                    cp("dve", gh[:, c4 * 4 + h, :], PS(pb)[:, (si + 1) * Ls - 3:(si + 1) * Ls])
                            else:
                                act(zs[:, 0:T], PS(pb)[:, 0:T], AF.Silu)
                        wdone(1)
                        ck(62)
                        for c3 in range(3):
                            pb = pbank()
                            for si in range(nseg):
                                for j in range(4):
                                    mm(PS(pb)[:, si * Ls:(si + 1) * Ls], dgc[:, c3, j, :], xv[:, c3, si, 1 + j:1 + j + Ls],
                                       start=(j == 0), stop=(j == 3))
                            act(qkv[:, c3, 0:T], PS(pb)[:, 0:T], AF.Silu)
                        ck(63)
                        for c3, dst, sc in ((0, qTb, 128.0 ** -0.5), (1, kTb, 1.0)):
                            act(sqb[:, 0:T], qkv[:, c3, 0:T], AF.Square)
                            pb = pbank()
                            mm(PS(pb)[:, 0:T], onesb, sqb[:, 0:T])
                            act(sdt[:, 0:T], PS(pb)[:, 0:T], AF.Ln, bias=1e-6)
                            act(rst[:, 0:T], sdt[:, 0:T], AF.Exp, scale=-0.5)
                            stt("dve", dst[:, 0:T], qkv[:, c3, 0:T], sc, rst[:, 0:T], ALU.mult, ALU.mult)
                        ck(64)
                        ts("dve", selh[:, 0, :], ones[0:8, :], ident[0:8, h:h + 1], ALU.mult)
                        ts("dve", selh[:, 1, :], ones[0:8, :], ident[0:8, 4 + h:5 + h], ALU.mult)
                        pbb = pbank()
                        mm(PS(pbb)[:, 0:T], selh[:, 0, :], sig8[:, 0:T])
                        pbg = pbank()
                        mm(PS(pbg)[:, 0:T], selh[:, 1, :], g8[:, 0:T])
                        cp("act", betaB[:, 0:T], PS(pbb)[:, 0:T])
                        for b in range(nb):
                            scan(gcB[:, b * C:(b + 1) * C], ones[:, 0:C], PS(pbg)[:, b * C:(b + 1) * C], 0.0)
                        act(egB[:, 0:T], gcB[:, 0:T], AF.Exp)
                        tt("dve", kbT[:, 0:T], kTb[:, 0:T], betaB[:, 0:T], ALU.mult)
                        tt("dve", qgT[:, 0:T], qTb[:, 0:T], egB[:, 0:T], ALU.mult)
                        ck(65)
                        pbc = pbank()
                        pcv = PS(pbc)[:, 0:nb * 2].rearrange("p (a b) -> p a b", a=nb)[0:C]
                        for b in range(nb):
                            mm(pcv[:, b, 0:1], gcB[:, b * C:(b + 1) * C], ident[:, 0:1])
                            mm(pcv[:, b, 1:2], betaB[:, b * C:(b + 1) * C], ident[:, 0:1])
                        cv = cols[0:C, 0:nb, :]
                        cp("dve", cv, pcv)
                        ts("dve", negc[0:C, 0:nb], cv[:, :, 0], -1.0, ALU.mult)
                        act(ecol[0:C, 0:nb], cv[:, :, 0], AF.Exp)
                        tt("dve", bexp[0:C, 0:nb], cv[:, :, 1], ecol[0:C, 0:nb], ALU.mult)
                        for b in range(nb):
                            act(kdcol[0:C, b:b + 1], cv[:, b, 0:1], AF.Exp, bias=gcB[0:C, (b + 1) * C - 1:(b + 1) * C], scale=-1.0)
                        ck(66)
                        pkt = pbank()
                        pk = PS(pkt, dt=BF16)[:, 0:nb * 128].rearrange("p (a b) -> p a b", a=nb)[0:C]
                        for b in range(nb):
                            tr(pk[:, b, :], kTb[:, b * C:(b + 1) * C], identb)
                        pvt = pbank()
                        pv = PS(pvt)[:, 0:nb * 128].rearrange("p (a b) -> p a b", a=nb)[0:C]
                        for b in range(nb):
                            tr(pv[:, b, :], qkv[:, 2, b * C:(b + 1) * C], ident)
                        for b in range(nb):
                            act(kbg[0:C, b, :], pk[:, b, :], AF.Copy, scale=bexp[0:C, b:b + 1])
                            act(kd[0:C, b, :], pk[:, b, :], AF.Copy, scale=kdcol[0:C, b:b + 1])
                            act(vbt[0:C, b, :], pv[:, b, :], AF.Copy, scale=cv[:, b, 1:2])
                        ck(67)
                        pg = pbank()
                        pgv = PS(pg)[:, 0:nb * C].rearrange("p (a b) -> p a b", a=nb)[0:C]
                        pa = pbank()
                        pav = PS(pa)[:, 0:nb * C].rearrange("p (a b) -> p a b", a=nb)[0:C]
                        for b in range(nb):
                            mm(pgv[:, b, :], kTb[:, b * C:(b + 1) * C], kbT[:, b * C:(b + 1) * C])
                            mm(pav[:, b, :], kTb[:, b * C:(b + 1) * C], qTb[:, b * C:(b + 1) * C])
                        Duv = Du[0:C, 0:nb, 0:C]
                        EUv = EU[0:C, 0:nb, 0:C]
                        EUsv = EUs[0:C, 0:nb, 0:C]
                        for b in range(nb):
                            stt("dve", Duv[:, b, :], gcB[0:C, b * C:(b + 1) * C], negc[0:C, b:b + 1], masku[0:C, 0:C],
                                ALU.add, ALU.add)
                        act(EUv, Duv, AF.Exp)
                        for b in range(nb):
                            tt("dve", EUsv[:, b, :], EUv[:, b, :], su[0:C, 0:C], ALU.mult)
                        U0 = Ub[0][0:C, 0:nb, 0:C]
                        tt("dve", U0, pgv, EUsv, ALU.mult)
                        Aqv = Aqk[0:C, 0:nb, 0:C]
                        tt("dve", Aqv, pav, EUv, ALU.mult)
                        ck(68)
                        plt = pbank()
                        idg = ident if GDN_F32 else identb
                        pl = PS(plt, dt=GD)[:, 0:nb * C].rearrange("p (a b) -> p a b", a=nb)[0:C]
                        for b in range(nb):
                            tr(pl[:, b, :], U0[:, b, :], idg[0:C, 0:C])
                        L0 = Lb[0][0:C, 0:nb, 0:C]
                        cp("act", L0, pl)
                        R0 = Rb[0][0:C, 0:nb, 0:C]
                        for b in range(nb):
                            stt("dve", R0[:, b, :], U0[:, b, :], -1.0, idg[0:C, 0:C], ALU.mult, ALU.add)
                        ck(69)
                        cur = 0
                        rcur = 0

                        def r_update(Lfac, rcur):
                            Rp = Rb[rcur][0:C, 0:nb, 0:C]
                            Rn = Rb[1 - rcur][0:C, 0:nb, 0:C]
                            p3 = pbank()
                            p3v = PS(p3)[:, 0:nb * C].rearrange("p (a b) -> p a b", a=nb)[0:C]
                            for b in range(nb):
                                mm(p3v[:, b, :], idg[0:C, 0:C], Rp[:, b, :], start=True, stop=False)
                                mm(p3v[:, b, :], Lfac[:, b, :], Rp[:, b, :], start=False, stop=True)
                            return p3v, Rn

                        for lev in range(1, m_lev):
                            Up = Ub[cur][0:C, 0:nb, 0:C]
                            Lp = Lb[cur][0:C, 0:nb, 0:C]
                            Un = Ub[1 - cur][0:C, 0:nb, 0:C]
                            Ln = Lb[1 - cur][0:C, 0:nb, 0:C]
                            p1 = pbank()
                            p1v = PS(p1)[:, 0:nb * C].rearrange("p (a b) -> p a b", a=nb)[0:C]
                            for b in range(nb):
                                mm(p1v[:, b, :], Up[:, b, :], Lp[:, b, :])
                            if lev < m_lev - 1:
                                p2 = pbank()
                                p2v = PS(p2)[:, 0:nb * C].rearrange("p (a b) -> p a b", a=nb)[0:C]
                                for b in range(nb):
                                    mm(p2v[:, b, :], Lp[:, b, :], Up[:, b, :])
                            if lev >= 2:
                                p3v, Rn = r_update(Lp, rcur)
                            cp("act", Ln, p1v)
                            if lev < m_lev - 1:
                                cp("dve", Un, p2v)
                            if lev >= 2:
                                cp("dve", Rn, p3v)
                                rcur = 1 - rcur
                            cur = 1 - cur
                            pump(PUMP_LEVEL)
                        p3v, Rn = r_update(Lb[cur][0:C, 0:nb, 0:C], rcur)
                        cp("dve", Rn, p3v)
                        rcur = 1 - rcur
                        R = Rb[rcur][0:C, 0:nb, 0:C]
                        ck(70)
                        pw = pbank()
                        pwv = PS(pw)[:, 0:nb * C].rearrange("p (a b) -> p a b", a=nb)
                        for b in range(nb):
                            mm(pwv[:, b, :], kbg[0:C, b, :], R[:, b, :])
                        nwv = nwT[:, 0:nb, 0:C]
                        ts("dve", nwv, pwv, -1.0, ALU.mult)
                        ck(71)
                        for b in range(nb):
                            si = (b * C) // Ls
                            Sm = seqstate(Sst, Sst_s, l, si)[:, h, :]
                            if b == 0 or (b * C) % Ls == 0:
                                cp("act", Sbf, Sm)
                            p_v = pbank()
                            pvn = PS(p_v)[0:C, 0:128]
                            mm(pvn, R[:, b, :], vbt[0:C, b, :], start=True, stop=False)
                            mm(pvn, nwv[:, b, :], Sm if GDN_F32 else Sbf, start=False, stop=True)
                            cp("act", vnew[0:C, :], pvn)
                            p_o = pbank()
                            po = PS(p_o)[:, 0:C]
                            mm(po, Sbf, qgT[:, b * C:(b + 1) * C], start=True, stop=False)
                            mm(po, vnew[0:C, :], Aqv[:, b, :], start=False, stop=True)
                            cp("dve", oT[:, b * C:(b + 1) * C], po)
                            p_s = pbank()
                            psn = PS(p_s)[:, 0:128]
                            mm(psn, kd[0:C, b, :], vnew[0:C, :])
                            stt("dve", Sm, Sm, egB[:, (b + 1) * C - 1:(b + 1) * C], psn, ALU.mult, ALU.add)
                            if b + 1 < nb and ((b + 1) * C) % Ls != 0:
                                cp("act", Sbf, Sm)
                            pump(PUMP_SEQ)
                        ck(72)
                        act(sqb[:, 0:T], oT[:, 0:T], AF.Square)
                        pb = pbank()
                        mm(PS(pb)[:, 0:T], onesb, sqb[:, 0:T])
                        act(sdt[:, 0:T], PS(pb)[:, 0:T], AF.Ln, bias=1e-6, scale=1.0 / 128)
                        act(rst[:, 0:T], sdt[:, 0:T], AF.Exp, scale=-0.5)
                        stt("dve", t512[:, 0:T], oT[:, 0:T], vec[:, l, VEC_GN:VEC_GN + 1], rst[:, 0:T], ALU.mult, ALU.mult)
                        tt("dve", mix[:, h, 0:T], t512[:, 0:T], zs[:, 0:T], ALU.mult)
                        ck(7)

                    for si in range(nseg):
                        seq = segs[si][0]
                        if tile["last"]:
                            S.dma("sp", o_gdn[l, seq], seqstate(Sst, Sst_s, l, si))
                            S.dma("sp", o_gc[l, seq], seqstate(ghist, ghist_s, l, si))

                    ck(8)
                    for _ in s5g:
                        pass
                    if tile["last"]:
                        for si in range(nseg):
                            S.dma("sp", o_s5[l, segs[si][0]], seqstate(xst, xst_s, l, si))

                    ck(9)
                    wc = wget().rearrange("p (a b) -> p a b", a=8)
                    xcv = xcb[:, 0:2 * nseg * (30 + Ls)].rearrange("p (a b c) -> p a b c", a=2, b=nseg)
                    S.dma("sp", dgcc, dcscr[l])
                    for si in range(nseg):
                        ch = seqstate(chist, chist_s, l, si)
                        cp("dve", xcv[:, :, si, 0:30], ch)
                    pa_ = [pbank(), pbank()]
                    pg_ = [pbank(), pbank()]
                    for c4 in range(4):
                        pb = pa_[c4] if c4 < 2 else pg_[c4 - 2]
                        for kc in range(8):
                            mm(PS(pb)[:, 0:T], wc[:, kc, c4 * 128:(c4 + 1) * 128], hn[:, kc, 0:T],
                               start=(kc == 0), stop=(kc == 7))
                    wdone(1)
                    for kc in range(2):
                        act(sgp[:, 0:T], PS(pg_[kc])[:, 0:T], AF.Sigmoid)
                        tt("dve", xg[:, kc, 0:T], PS(pa_[kc])[:, 0:T], sgp[:, 0:T], ALU.mult)
                        for si in range(nseg):
                            ch = seqstate(chist, chist_s, l, si)
                            cp("act", xcv[:, kc, si, 30:30 + Ls], xg[:, kc, si * Ls:(si + 1) * Ls])
                            cp("dve", ch[:, kc, :], xg[:, kc, (si + 1) * Ls - 30:(si + 1) * Ls])
                    pcv_ = [pbank(), pbank()]
                    for kc in range(2):
                        for si in range(nseg):
                            for j in range(31):
                                mm(PS(pcv_[kc])[:, si * Ls:(si + 1) * Ls], dgcc[:, kc, j, :], xcv[:, kc, si, j:j + Ls],
                                   start=(j == 0), stop=(j == 30))
                    xc = xg
                    for kc in range(2):
                        act(xc[:, kc, 0:T], PS(pcv_[kc])[:, 0:T], AF.Identity, bias=vec[:, l, VEC_DWB + kc:VEC_DWB + kc + 1])
                        cp("dve", xc16[:, kc, 0:T], xc[:, kc, 0:T])
                    pm = pbank()
                    for kc in range(2):
                        mm(PS(pm)[:, 0:T], onesb, xc16[:, kc, 0:T], start=(kc == 0), stop=(kc == 1))
                    for kc in range(2):
                        stt("dve", xc[:, kc, 0:T], PS(pm)[:, 0:T], -1.0 / 256, xc[:, kc, 0:T], ALU.mult, ALU.add)
                        act(xc16[:, kc, 0:T], xc[:, kc, 0:T], AF.Square)
                    pv_ = pbank()
                    for kc in range(2):
                        mm(PS(pv_)[:, 0:T], onesb, xc16[:, kc, 0:T], start=(kc == 0), stop=(kc == 1))
                    act(sdt[:, 0:T], PS(pv_)[:, 0:T], AF.Ln, bias=1e-5, scale=1.0 / 256)
                    act(rst[:, 0:T], sdt[:, 0:T], AF.Exp, scale=-0.5)
                    for kc in range(2):
                        stt("dve", t512[:, 0:T], xc[:, kc, 0:T], vec[:, l, VEC_LNG + kc:VEC_LNG + kc + 1], rst[:, 0:T],
                            ALU.mult, ALU.mult)
                        act(mix[:, 6 + kc, 0:T], t512[:, 0:T], AF.Silu, bias=vec[:, l, VEC_LNB + kc:VEC_LNB + kc + 1])
                    if tile["last"]:
                        for si in range(nseg):
                            S.dma("sp", o_cc[l, segs[si][0]], seqstate(chist, chist_s, l, si))

                    ck(10)
                    for half in range(2):
                        wo = wget().rearrange("p (a b) -> p a b", a=8)
                        for m4 in range(4):
                            mc = half * 4 + m4
                            pb = pbank()
                            for kc in range(8):
                                mm(PS(pb)[:, 0:T], wo[:, kc, m4 * 128:(m4 + 1) * 128], mix[:, kc, 0:T],
                                   start=(kc == 0), stop=(kc == 7))
                            tt("dve", hT[:, mc, 0:T], hT[:, mc, 0:T], PS(pb)[:, 0:T], ALU.add)
                        wdone(1)

                    ck(11)
                    rmsnorm_to(hn, lambda kc: vec[:, l, VEC_GFFN + kc:VEC_GFFN + kc + 1], T)
                    for j in range(11):
                        wf = wget().rearrange("p (a b) -> p a b", a=8)
                        for c2 in range(2):
                            hc = j * 2 + c2
                            p1 = pbank()
                            p3 = pbank()
                            for kc in range(8):
                                mm(PS(p1)[:, 0:T], wf[:, kc, c2 * 128:(c2 + 1) * 128], hn[:, kc, 0:T],
                                   start=(kc == 0), stop=(kc == 7))
                            for kc in range(8):
                                mm(PS(p3)[:, 0:T], wf[:, kc, 256 + c2 * 128:256 + (c2 + 1) * 128], hn[:, kc, 0:T],
                                   start=(kc == 0), stop=(kc == 7))
                            act(sgp[:, 0:T], PS(p1)[:, 0:T], AF.Silu)
                            tt("dve", actb[:, hc, 0:T], sgp[:, 0:T], PS(p3)[:, 0:T], ALU.mult)
                        wdone(1)
                    for mc in range(8):
                        w2c = wget()[:, 0:22 * 128].rearrange("p (a b) -> p a b", a=22)
                        pb = pbank()
                        for kc in range(22):
                            mm(PS(pb)[:, 0:T], w2c[:, kc, :], actb[:, kc, 0:T], start=(kc == 0), stop=(kc == 21))
                        tt("dve", hT[:, mc, 0:T], hT[:, mc, 0:T], PS(pb)[:, 0:T], ALU.add)
                        wdone(1)

                    ck(12)
                    S.dma("sp", pTf[:, :, 0:T], pT[l][:, :, tok0:tok0 + T])
                    for kc in range(8):
                        cp("act", hn[:, kc, 0:T], hT[:, kc, 0:T])
                    cp("dve", pTb[:, :, 0:T], pTf[:, :, 0:T])
                    wg0 = wget().rearrange("p (a b) -> p a b", a=8)
                    wpe = wget()[:, 0:2048].rearrange("p (a b) -> p a b", a=2)
                    wg1 = None
                    for mc in range(8):
                        if mc == 4:
                            wdone(1)
                            wg1 = wget().rearrange("p (a b) -> p a b", a=8)
                        wg = wg0 if mc < 4 else wg1
                        m4 = mc % 4
                        pb = pbank()
                        for kc in range(8):
                            mm(PS(pb)[:, 0:T], wg[:, kc, m4 * 128:(m4 + 1) * 128], hn[:, kc, 0:T],
                               start=(kc == 0), stop=(kc == 7))
                        act(sgp[:, 0:T], PS(pb)[:, 0:T], AF.Sigmoid)
                        pb2 = pbank()
                        for kc in range(2):
                            mm(PS(pb2)[:, 0:T], wpe[:, kc, mc * 128:(mc + 1) * 128], pTb[:, kc, 0:T],
                               start=(kc == 0), stop=(kc == 1))
                        tt("dve", t512[:, 0:T], PS(pb2)[:, 0:T], sgp[:, 0:T], ALU.mult)
                        tt("dve", hT[:, mc, 0:T], hT[:, mc, 0:T], t512[:, 0:T], ALU.add)
                    wdone(2)
                    ck(13)

                rmsnorm_to(None, lambda kc: gf[:, kc:kc + 1], T, final_out=yout)
                S.dma("sp", yT[:, :, tok0:tok0 + T], yout[:, :, 0:T])

        except _Stop:
            print('STOPPED at', KSTOP)
            KT = int(os.environ.get("KTAIL", "0"))
            if KT == 1:
                memset("dve", t512, 0.0)
            elif KT == 2:
                cp("act", t512, sdt)
            elif KT == 3:
                mm(PS(7), onesb, sqb)
                cp("dve", t512, PS(7))
            elif KT == 4:
                mm(PS(7), onesb, sqb)
                cp("act", t512, PS(7))
        S.finish()
        S.finalize()
        print("ops recorded:", S.nops, {k: len(v) for k, v in S.prog.items()}, "signals", S.nsig)

        @block.tensor
        def _(e):
            S.replay("pe", e)

        @block.scalar
        def _(e):
            S.replay("act", e)

        @block.vector
        def _(e):
            S.replay("dve", e)

        @block.gpsimd
        def _(e):
            S.replay("pool", e)

        @block.sync
        def _(e):
            S.replay("sp", e)
    return nc


def _consts():
    c = np.zeros((128, NCONST), np.float32)
    c[:, C_ID:C_ID + 128] = np.eye(128)
    s = np.arange(128)[:, None]
    cc = np.arange(128)[None, :]
    c[:, C_MU:C_MU + 128] = np.where(s <= cc, 0.0, -30000.0)
    c[:, C_SU:C_SU + 128] = (s < cc).astype(np.float32)
    c[:, C_IOTA:C_IOTA + 512] = np.arange(1, 513)[None, :]
    c[:, C_ONES:C_ONES + 128] = 1.0
    return c


def _fm(v):
    C = v.shape[-1]
    r = v.reshape(v.shape[:-1] + (C // 128, 128))
    return np.moveaxis(r, -1, 0)


def _pack_weights(inp, L):
    w = np.zeros((L, NCHUNK, 128, CHW), np.float32)

    def kc_layout(mat):
        K, N = mat.shape
        return mat.reshape(K // 128, 128, N).transpose(1, 0, 2)

    for l in range(L):
        win = inp["w_in"][l]
        c0 = np.concatenate([win[:, 2056:2312], win[:, 2048:2056]], axis=1)
        a = kc_layout(c0).reshape(128, -1)
        g = kc_layout(inp["s5_glu_w"][l]).reshape(128, -1)
        w[l, 0, :, 0:a.shape[1]] = a
        w[l, 0, :, a.shape[1]:a.shape[1] + g.shape[1]] = g
        for h in range(4):
            cols = np.concatenate([win[:, 128 * h:128 * h + 128], win[:, 512 + 128 * h:512 + 128 * h + 128],
                                   win[:, 1024 + 128 * h:1024 + 128 * h + 128],
                                   win[:, 1536 + 128 * h:1536 + 128 * h + 128]], axis=1)
            w[l, 1 + h] = kc_layout(cols).reshape(128, -1)
        w[l, 5] = kc_layout(win[:, 2312:2824]).reshape(128, -1)
        wo = inp["w_out"][l]
        w[l, 6] = kc_layout(wo[:, 0:512]).reshape(128, -1)
        w[l, 7] = kc_layout(wo[:, 512:1024]).reshape(128, -1)
        for j in range(11):
            cols = np.concatenate([inp["ffn_w1"][l][:, 256 * j:256 * j + 256], inp["ffn_w3"][l][:, 256 * j:256 * j + 256]], axis=1)
            w[l, 8 + j] = kc_layout(cols).reshape(128, -1)
        for mc in range(8):
            a = kc_layout(inp["ffn_w2"][l][:, 128 * mc:128 * mc + 128]).reshape(128, -1)
            w[l, 19 + mc, :, 0:a.shape[1]] = a
        wg = inp["pe_gate_w"][l]
        w[l, 27] = kc_layout(wg[:, 0:512]).reshape(128, -1)
        w[l, 29] = kc_layout(wg[:, 512:1024]).reshape(128, -1)
        a = kc_layout(inp["pe_w"][l]).reshape(128, -1)
        w[l, 28, :, 0:a.shape[1]] = a
    return w


def _prep_shared(inp, L):
    sh = {}
    sh["wch"] = _pack_weights(inp, L)
    sh["consts"] = _consts()
    vec = np.zeros((128, L, 40), np.float32)
    for l in range(L):
        vec[:, l, 0:8] = _fm(inp["norm_mix"][l])
        vec[:, l, 8:16] = _fm(inp["norm_ffn"][l])
        vec[:, l, 16] = inp["gdn_norm"][l]
        vec[:, l, 17:19] = _fm(inp["s5_d"][l])
        vec[:, l, 19:21] = _fm(inp["s5_glu_b"][l])
        vec[:, l, 21:23] = _fm(inp["cc_dw_b"][l])
        vec[:, l, 23:25] = _fm(inp["cc_ln_g"][l])
        vec[:, l, 25:27] = _fm(inp["cc_ln_b"][l])
    sh["vecs"] = vec
    sh["gfin"] = np.ascontiguousarray(_fm(inp["norm_final"]))
    cw = inp["gdn_conv_w"][:L]
    sh["cwg"] = np.ascontiguousarray(cw.reshape(L, 4, 12, 128).transpose(3, 0, 2, 1))
    cc = inp["cc_dw_w"][:L]
    sh["cwc"] = np.ascontiguousarray(cc.reshape(L, 31, 2, 128).transpose(3, 0, 2, 1))
    ba8 = np.zeros((8, L, 2), np.float32)
    for l in range(L):
        ba8[4:8, l, 0] = inp["gdn_dt_bias"][l]
        ba8[4:8, l, 1] = inp["gdn_a_log"][l]
    sh["ba8"] = ba8
    s5col = np.zeros((128, L, 8, 3), np.float32)
    s5row = np.zeros((128, L, 3, 8, 128), np.float32)
    s5bt = np.zeros((L, 2, 128, 8, 128), np.float32)
    s5ct = np.zeros((L, 2, 128, 8, 128), np.float32)
    for l in range(L):
        for g in range(16):
            q, g2 = g // 2, g % 2
            ps = slice(g2 * 64, g2 * 64 + 64)
            s5col[ps, l, q, 0] = inp["s5_lam_re"][l, g]
            s5col[ps, l, q, 1] = inp["s5_lam_im"][l, g]
            s5col[ps, l, q, 2] = inp["s5_log_dt"][l, g]
            s5row[:, l, 0, q, ps] = inp["s5_lam_re"][l, g][None, :]
            s5row[:, l, 1, q, ps] = inp["s5_lam_im"][l, g][None, :]
            s5row[:, l, 2, q, ps] = inp["s5_log_dt"][l, g]
            r0 = (g % 8) * 16
            s5bt[l, 0, r0:r0 + 16, q, ps] = inp["s5_b_re"][l, g].T
            s5bt[l, 1, r0:r0 + 16, q, ps] = inp["s5_b_im"][l, g].T
            s5ct[l, 0, ps, q, r0:r0 + 16] = inp["s5_c_re"][l, g].T
            s5ct[l, 1, ps, q, r0:r0 + 16] = inp["s5_c_im"][l, g].T
    sh["s5col"] = s5col
    sh["s5row"] = s5row
    sh["s5bt"] = s5bt
    sh["s5ct"] = s5ct
    return sh


def _prep_core(inp, b, NPT, L):
    m = {}
    xp = inp["x_prompt"][b, :NPT * 512]
    xs = inp["x_sample"][2 * b:2 * b + 2].reshape(64, D)
    x = np.concatenate([xp, xs], axis=0)
    m["xT"] = np.ascontiguousarray(x.reshape(-1, 8, 128).transpose(2, 1, 0))
    pp = inp["p_prompt"][:L, b, :NPT * 512]
    ps = inp["p_sample"][:L, 2 * b:2 * b + 2].reshape(L, 64, 256)
    p = np.concatenate([pp, ps], axis=1)
    m["pT"] = np.ascontiguousarray(p.reshape(L, -1, 2, 128).transpose(0, 3, 2, 1))
    sg = inp["state_gdn"][:L, 2 * b:2 * b + 2]
    m["st_gdn"] = np.ascontiguousarray(sg.transpose(0, 1, 3, 2, 4))
    gc = inp["state_gdn_conv"][:L, 2 * b:2 * b + 2]
    m["st_gc"] = np.ascontiguousarray(gc.reshape(L, 2, 3, 12, 128).transpose(0, 1, 4, 3, 2))
    s5 = inp["state_s5"][:L, 2 * b:2 * b + 2]
    m["st_s5"] = np.ascontiguousarray(s5.reshape(L, 2, 8, 2, 64, 2).transpose(0, 1, 3, 4, 2, 5).reshape(L, 2, 128, 8, 2))
    cc = inp["state_conv"][:L, 2 * b:2 * b + 2]
    m["st_cc"] = np.ascontiguousarray(cc.reshape(L, 2, 30, 2, 128).transpose(0, 1, 4, 3, 2))
    return m


_PROG_CACHE = {}


def run_cores(inp, NPT, L, cores):
    key = (NPT, L)
    if key not in _PROG_CACHE:
        _PROG_CACHE[key] = build_program(NPT, L)
    nc = _PROG_CACHE[key]
    sh = _prep_shared(inp, L)
    in_maps = []
    for b in cores:
        m = dict(sh)
        m.update(_prep_core(inp, b, NPT, L))
        in_maps.append(m)
    res = run_bass_kernel_spmd(nc, in_maps, core_ids=list(range(len(cores))))
    return res


def assemble(res, NPT, L, ncores):
    B = ncores
    y_p = np.zeros((B, NPT * 512, D), np.float32)
    y_s = np.zeros((2 * B, DSEQ, D), np.float32)
    gdn = np.zeros((L, 3 * B, 4, 128, 128), np.float32)
    gcv = np.zeros((L, 3 * B, 3, 1536), np.float32)
    s5 = np.zeros((L, 3 * B, 16, 64, 2), np.float32)
    ccv = np.zeros((L, 3 * B, 30, 256), np.float32)
    for b in range(B):
        r = res.results[b]
        yT = r["yT"]
        y = yT.transpose(2, 1, 0).reshape(-1, D)
        y_p[b] = y[:NPT * 512]
        y_s[2 * b:2 * b + 2] = y[NPT * 512:].reshape(2, DSEQ, D)
        og = r["o_gdn"]
        gdn[:, 3 * b:3 * b + 3] = og.transpose(0, 1, 3, 2, 4)
        oc = r["o_gc"]
        gcv[:, 3 * b:3 * b + 3] = oc.transpose(0, 1, 4, 3, 2).reshape(L, 3, 3, 1536)
        o5 = r["o_s5"]
        s5[:, 3 * b:3 * b + 3] = o5.reshape(L, 3, 2, 64, 8, 2).transpose(0, 1, 4, 2, 3, 5).reshape(L, 3, 16, 64, 2)
        o3 = r["o_cc"]
        ccv[:, 3 * b:3 * b + 3] = o3.transpose(0, 1, 4, 3, 2).reshape(L, 3, 30, 256)
    pi = [3 * b for b in range(B)]
    si = [3 * b + 1 + k for b in range(B) for k in range(2)]
    return (y_p, y_s, gdn[:, pi], gcv[:, pi], s5[:, pi], ccv[:, pi],
            gdn[:, si], gcv[:, si], s5[:, si], ccv[:, si])


def kernel(**inputs):
    inp = {k: np.asarray(v) for k, v in inputs.items()}
    res = run_cores(inp, SEQ // 512, L_FULL, list(range(8)))
    return assemble(res, SEQ // 512, L_FULL, 8)
```

```python
import math
import os
import numpy as np
import ml_dtypes
import concourse.bass as bass
import concourse.mybir as mybir
from concourse.bass_utils import run_bass_kernel_spmd

F32 = mybir.dt.float32
BF16 = mybir.dt.bfloat16
I32 = mybir.dt.int32
AF = mybir.ActivationFunctionType
ALU = mybir.AluOpType

D = 1024
S5POOL = os.environ.get("S5POOL", "pool")
PUMP_LEVEL = 3
PUMP_SEQ = 4
GDN_F32 = True
L_FULL = 4
SEQ = 4096
DSEQ = 32
HID = 2816
NCHUNK = 30
CHW = 4096
TWO_PI = 2.0 * math.pi

C_ID, C_MU, C_SU, C_IOTA, C_ONES = 0, 128, 256, 384, 896
NCONST = 1024


def _esize(dt):
    return 2 if dt == BF16 else 4


class Sched:
    ENG = ("pe", "act", "dve", "pool", "sp")

    def __init__(self, nc, esems, dsems):
        self.nc = nc
        self.sem = {}
        self.cur = {}
        for n, s in zip(self.ENG, esems):
            self.sem[n] = s
            self.cur[n] = 0
        self.dq = {"sp": [], "pool": []}
        half = len(dsems) // 2
        for i, s in enumerate(dsems):
            k = "d%d" % i
            self.sem[k] = s
            self.cur[k] = 0
            self.dq["sp"].append(k)
        self.dnext = {"sp": 0, "pool": 0}
        self.clock = {n: {} for n in self.ENG}
        self.prog = {n: [] for n in self.ENG}
        self.blocks = {}
        self.tokvc = {}
        self.nops = 0

    def _blocks(self, ap):
        sp = str(ap.space)
        if "DRAM" in sp:
            return []
        a = ap.ap
        pstep = a[0][0]
        es = _esize(ap.dtype)
        off = int(ap.offset)
        col = off % pstep if pstep > 0 else off
        ext = 1
        for st, cnt in a[1:]:
            ext += (cnt - 1) * abs(st)
        lo = col * es
        hi = lo + ext * es
        key = "P" if "PSUM" in sp else "S"
        g = 2048 if key == "P" else 256
        return [(key, b) for b in range(lo // g, (hi - 1) // g + 1)]

    def _deps(self, eng, reads, writes):
        need = {}
        clk = self.clock[eng]

        def add(tok):
            if tok is None:
                return
            s, v = tok
            if eng == "pe" and s == "pe":
                return
            if clk.get(s, 0) >= v:
                return
            if need.get(s, 0) < v:
                need[s] = v

        rb = set()
        wb = set()
        for ap in reads:
            rb.update(self._blocks(ap))
        for ap in writes:
            wb.update(self._blocks(ap))
        for b in rb:
            st = self.blocks.get(b)
            if st is not None:
                add(st[0])
                if b[0] == "P":
                    for s, v in st[1].items():
                        if s != eng:
                            add((s, v))
        for b in wb:
            st = self.blocks.get(b)
            if st is not None:
                add(st[0])
                for s, v in st[1].items():
                    add((s, v))
        return need, rb, wb

    def _emit_waits(self, eng, need):
        clk = self.clock[eng]
        for s, v in need.items():
            if clk.get(s, 0) >= v:
                continue
            sem = self.sem[s]
            self.prog[eng].append(("w", sem, v, s))
            vc = self.tokvc.get((s, v))
            if vc is not None:
                for k2, v2 in vc.items():
                    if clk.get(k2, 0) < v2:
                        clk[k2] = v2
            if clk.get(s, 0) < v:
                clk[s] = v

    def _commit(self, tok, eng, rb, wb):
        vc = dict(self.clock[eng])
        vc[tok[0]] = tok[1]
        self.tokvc[tok] = vc
        for b in rb:
            st = self.blocks.get(b)
            if st is None:
                st = [None, {}]
                self.blocks[b] = st
            st[1][tok[0]] = tok[1]
        for b in wb:
            self.blocks[b] = [tok, {}]
        self.nops += 1
        if len(self.tokvc) > 60000:
            keys = list(self.tokvc.keys())
            for k in keys[:30000]:
                del self.tokvc[k]

    def op(self, eng, fn, reads, writes):
        need, rb, wb = self._deps(eng, reads, writes)
        self._emit_waits(eng, need)
        self.cur[eng] += 1
        tok = (eng, self.cur[eng])
        self.prog[eng].append(("i", fn, self.sem[eng], 1))
        self._commit(tok, eng, rb, wb)
        return tok

    def dma(self, q, out, in_, extra_tokens=()):
        ring = self.dq[q]
        k = ring[self.dnext[q] % len(ring)]
        self.dnext[q] += 1
        need, rb, wb = self._deps(q, [in_], [out])
        clk = self.clock[q]
        prev = self.cur[k]
        if prev > 0 and clk.get(k, 0) < prev:
            need[k] = max(need.get(k, 0), prev)
        for s, v in extra_tokens:
            if clk.get(s, 0) < v:
                need[s] = max(need.get(s, 0), v)
        self._emit_waits(q, need)
        self.cur[k] += 16
        tok = (k, self.cur[k])
        self.prog[q].append(("i", lambda e, o=out, i=in_: e.dma_start(out=o, in_=i), self.sem[k], 16))
        self._commit(tok, q, rb, wb)
        return tok

    def barrier(self):
        for eng in self.ENG:
            need = {}
            for s, v in self.cur.items():
                if s == eng and eng == "pe":
                    continue
                if v > 0 and self.clock[eng].get(s, 0) < v:
                    need[s] = v
            self._emit_waits(eng, need)

    def finish(self):
        for eng in self.ENG:
            need = {}
            for s, v in self.cur.items():
                if s == eng:
                    continue
                if v > 0 and self.clock[eng].get(s, 0) < v:
                    need[s] = v
            self._emit_waits(eng, need)

    def finalize(self):
        waited = {n: set() for n in self.ENG}
        for eng in self.ENG:
            for it in self.prog[eng]:
                if it[0] == "w" and it[3] in waited:
                    waited[it[3]].add(it[2])
        self.tickmap = {}
        for n in self.ENG:
            self.tickmap[n] = {v: i + 1 for i, v in enumerate(sorted(waited[n]))}
        self.nsig = {n: len(waited[n]) for n in self.ENG}

    def replay(self, eng, e):
        seq = 0
        tm = self.tickmap
        for it in self.prog[eng]:
            if it[0] == "w":
                k = it[3]
                if k in tm:
                    e.wait_ge(it[1], tm[k][it[2]])
                else:
                    e.wait_ge(it[1], it[2])
            else:
                ins = it[1](e)
                if it[3] == 16:
                    ins.then_inc(it[2], 16)
                else:
                    seq += 1
                    if seq in tm[eng]:
                        ins.then_inc(it[2], 1)


class _Stop(Exception):
    pass


import os
KSTOP = int(os.environ.get("KSTOP", "0"))


def ck(n):
    if KSTOP == n:
        raise _Stop()


class Alloc:
    def __init__(self, big, words):
        self.big = big
        self.words = words
        self.off = 0

    def get(self, dtype, shape):
        n = 1
        for s in shape[1:]:
            n *= s
        w = n if dtype != BF16 else (n + 1) // 2
        w = (w + 63) // 64 * 64
        o = self.off
        self.off += w
        assert self.off <= self.words, "SBUF overflow %d > %d" % (self.off, self.words)
        ap = self.big[:, o:o + w]
        if dtype == BF16:
            ap = ap.bitcast(BF16)
        elif dtype == I32:
            ap = ap.bitcast(I32)
        ap = ap[:, 0:n]
        if len(shape) == 3:
            ap = ap.rearrange("p (a b) -> p a b", a=shape[1])
        elif len(shape) == 4:
            ap = ap.rearrange("p (a b c) -> p a b c", a=shape[1], b=shape[2])
        if shape[0] < 128:
            ap = ap[0:shape[0]]
        return ap


def bc(ap, shape):
    return ap.to_broadcast(list(shape))


def build_program(NPT, L, with_sample=True):
    nc = bass.Bass("TRN2", target_bir_lowering=False)
    NTOK = NPT * 512 + 64

    def din(name, shape, dt=F32):
        return nc.dram_tensor(name, list(shape), dt, kind="ExternalInput").ap()

    def dout(name, shape, dt=F32):
        return nc.dram_tensor(name, list(shape), dt, kind="ExternalOutput").ap()

    xT = din("xT", [128, 8, NTOK])
    pT = din("pT", [L, 128, 2, NTOK])
    wch = din("wch", [L, NCHUNK, 128, CHW])
    consts = din("consts", [128, NCONST])
    vecs = din("vecs", [128, L, 40])
    gfin = din("gfin", [128, 8])
    cwg = din("cwg", [128, L, 12, 4])
    cwc = din("cwc", [128, L, 2, 31])
    ba8 = din("ba8", [8, L, 2])
    s5col = din("s5col", [128, L, 8, 3])
    s5row = din("s5row", [128, L, 3, 8, 128])
    s5bt = din("s5bt", [L, 2, 128, 8, 128])
    s5ct = din("s5ct", [L, 2, 128, 8, 128])
    st_gdn = din("st_gdn", [L, 2, 128, 4, 128])
    st_gc = din("st_gc", [L, 2, 128, 12, 3])
    st_s5 = din("st_s5", [L, 2, 128, 8, 2])
    st_cc = din("st_cc", [L, 2, 128, 2, 30])

    yT = dout("yT", [128, 8, NTOK])
    o_gdn = dout("o_gdn", [L, 3, 128, 4, 128])
    o_gc = dout("o_gc", [L, 3, 128, 12, 3])
    o_s5 = dout("o_s5", [L, 3, 128, 8, 2])
    o_cc = dout("o_cc", [L, 3, 128, 2, 30])

    wscr = nc.dram_tensor("wscr", [L, NCHUNK, 128, CHW], BF16, kind="Internal").ap()
    s5tab = nc.dram_tensor("s5tab", [L, 8, 128, 2, 512], F32, kind="Internal").ap()
    s5mscr = nc.dram_tensor("s5mscr", [L, 128, 4, 8, 128], BF16, kind="Internal").ap()
    dgscr = nc.dram_tensor("dgscr", [L, 128, 12, 4, 128], BF16, kind="Internal").ap()
    dcscr = nc.dram_tensor("dcscr", [L, 128, 2, 31, 128], BF16, kind="Internal").ap()

    SBW = 53000
    NDS = 6
    import contextlib
    with contextlib.ExitStack() as es:
        big = es.enter_context(nc.sbuf_tensor("big", [128, SBW], F32))
        psum = es.enter_context(nc.psum_tensor("psum", [128, 8, 512], F32))
        esems = [es.enter_context(nc.semaphore("e_%s" % n)) for n in Sched.ENG]
        dsems = [es.enter_context(nc.semaphore("dm_%d" % i)) for i in range(NDS)]
        block = es.enter_context(nc.Block())
        S = Sched(nc, esems, dsems)
        A = Alloc(big, SBW)

        def mm(out, lhsT, rhs, start=True, stop=True):
            S.op("pe", lambda e: e.matmul(out, lhsT=lhsT, rhs=rhs, start=start, stop=stop),
                 [lhsT, rhs], [out])

        def tr(out, in_, ident):
            S.op("pe", lambda e: e.transpose(out=out, in_=in_, identity=ident), [in_, ident], [out])

        def act(out, in_, func, bias=None, scale=None):
            kw = {}
            r = [in_]
            if bias is not None:
                kw["bias"] = bias
                if not isinstance(bias, float):
                    r.append(bias)
            if scale is not None:
                kw["scale"] = scale
                if not isinstance(scale, float):
                    r.append(scale)
            S.op("act", lambda e: e.activation(out=out, in_=in_, func=func, **kw), r, [out])

        def tt(eng, out, in0, in1, op):
            S.op(eng, lambda e: e.tensor_tensor(out=out, in0=in0, in1=in1, op=op), [in0, in1], [out])

        def ts(eng, out, in0, s1, op0, s2=None, op1=None):
            r = [in0]
            if not isinstance(s1, float):
                r.append(s1)
            if s2 is not None and not isinstance(s2, float):
                r.append(s2)
            if op1 is None:
                S.op(eng, lambda e: e.tensor_scalar(out=out, in0=in0, scalar1=s1, scalar2=None, op0=op0), r, [out])
            else:
                S.op(eng, lambda e: e.tensor_scalar(out=out, in0=in0, scalar1=s1, scalar2=s2, op0=op0, op1=op1), r, [out])

        def stt(eng, out, in0, scalar, in1, op0, op1):
            r = [in0, in1]
            if not isinstance(scalar, float):
                r.append(scalar)
            S.op(eng, lambda e: e.scalar_tensor_tensor(out=out, in0=in0, scalar=scalar, in1=in1, op0=op0, op1=op1), r, [out])

        def cp(eng, out, in_):
            if eng == "act":
                act(out, in_, AF.Copy)
            else:
                S.op(eng, lambda e: e.tensor_copy(out=out, in_=in_), [in_], [out])

        def memset(eng, out, val):
            S.op(eng, lambda e: e.memset(out, val), [], [out])

        def recip(out, in_):
            S.op("dve", lambda e: e.reciprocal(out=out, in_=in_), [in_], [out])

        def scan(out, d0, d1, init):
            r = [d0, d1]
            if not isinstance(init, float):
                r.append(init)
            S.op("dve", lambda e: e.tensor_tensor_scan(out=out, data0=d0, data1=d1, initial=init,
                                                       op0=ALU.mult, op1=ALU.add), r, [out])

        pcnt = [0]
        pinned = set()

        def pbank(pin=False):
            while True:
                b = pcnt[0] % 8
                pcnt[0] += 1
                if b not in pinned:
                    break
            if pin:
                pinned.add(b)
            return b

        def PS(b, shape=None, dt=F32):
            ap = psum[:, b, :]
            if dt == BF16:
                ap = ap.bitcast(BF16)
            if shape is None:
                return ap
            n = 1
            for s in shape[1:]:
                n *= s
            ap = ap[:, 0:n]
            if len(shape) == 3:
                ap = ap.rearrange("p (a b) -> p a b", a=shape[1])
            if shape[0] < 128:
                ap = ap[0:shape[0]]
            return ap

        cst = A.get(F32, [128, NCONST])
        ident = cst[:, C_ID:C_ID + 128]
        masku = cst[:, C_MU:C_MU + 128]
        su = cst[:, C_SU:C_SU + 128]
        iota1 = cst[:, C_IOTA:C_IOTA + 512]
        ones = cst[:, C_ONES:C_ONES + 128]
        identb = A.get(BF16, [128, 128])
        onesb = A.get(BF16, [128, 128])
        vec = A.get(F32, [128, L, 40])
        gf = A.get(F32, [128, 8])
        cwg_s = A.get(F32, [128, L, 12, 4])
        cwc_s = A.get(F32, [128, L, 2, 31])
        ba8_s = A.get(F32, [8, L, 2])
        na8 = A.get(F32, [8, L])
        s5c = A.get(F32, [128, L, 8, 3])
        s5r = A.get(F32, [128, L, 8])
        hT = A.get(F32, [128, 8, 512])
        hn = A.get(BF16, [128, 8, 512])
        mix = A.get(BF16, [128, 8, 512])
        Sst = [A.get(F32, [128, 4, 128]) for _ in range(L)]
        ghist = [A.get(F32, [128, 12, 3]) for _ in range(L)]
        chist = [A.get(F32, [128, 2, 30]) for _ in range(L)]
        xst = [A.get(F32, [128, 8, 2]) for _ in range(L)]
        Sst_s = [A.get(F32, [128, 4, 128]) for _ in range(2)]
        ghist_s = [A.get(F32, [128, 12, 3]) for _ in range(2)]
        chist_s = [A.get(F32, [128, 2, 30]) for _ in range(2)]
        xst_s = [A.get(F32, [128, 8, 2]) for _ in range(2)]
        NSLOT = 5
        wslot = [A.get(BF16, [128, CHW]) for _ in range(NSLOT)]
        s5m = A.get(BF16, [128, 4, 8, 128])
        persist_end = A.off

        try:
            S.dma("sp", cst, consts)
            S.dma("sp", vec, vecs)
            S.dma("sp", gf, gfin)
            S.dma("sp", cwg_s, cwg)
            S.dma("sp", cwc_s, cwc)
            S.dma("sp", ba8_s, ba8)
            S.dma("sp", s5c, s5col)
            cp("dve", identb, ident)
            cp("dve", onesb, ones)
            act(na8, ba8_s[:, :, 1], AF.Exp)
            ts("dve", na8, na8, -1.0, ALU.mult)
            ck(1)
            for l in range(L):
                memset("dve", Sst[l], 0.0)
                memset("dve", ghist[l], 0.0)
                memset("dve", chist[l], 0.0)
                memset("dve", xst[l], 0.0)

            def range_reduce(dst, src, tmpi, tmpf):
                ts("dve", tmpi, src, 1.0 / TWO_PI, ALU.mult)
                cp("dve", tmpf, tmpi)
                stt("dve", dst, tmpf, -TWO_PI, src, ALU.mult, ALU.add)
                ts("dve", dst, dst, -3.1415925, ALU.max, 3.1415925, ALU.min)

            A.off = persist_end
            th = A.get(F32, [128, L, 8])
            dtc = A.get(F32, [128, L, 8])
            t_i = A.get(I32, [128, 512])
            t_f = A.get(F32, [128, 512])
            t_a = A.get(F32, [128, 512])
            t_b = A.get(F32, [128, 512])
            tabs = [A.get(F32, [128, 2, 512]) for _ in range(2)]
            act(dtc, s5c[:, :, :, 2], AF.Exp)
            tt("dve", th, s5c[:, :, :, 1], dtc, ALU.mult)
            tt("dve", s5r, s5c[:, :, :, 0], dtc, ALU.mult)
            act(s5r, s5r, AF.Exp)
            thf = th.rearrange("p a b -> p (a b)")
            range_reduce(thf, thf, t_i[:, 0:L * 8], t_f[:, 0:L * 8])
            k = 0
            for l in range(L):
                for q in range(8):
                    tb = tabs[k % 2]
                    k += 1
                    ts("dve", t_a, iota1, th[:, l, q:q + 1], ALU.mult)
                    range_reduce(t_b, t_a, t_i, t_f)
                    act(tb[:, 1, :], t_b, AF.Sin)
                    ts("dve", t_a, t_b, math.pi / 2, ALU.add)
                    range_reduce(t_b, t_a, t_i, t_f)
                    act(tb[:, 0, :], t_b, AF.Sin)
                    S.dma("sp", s5tab[l, q], tb)
            ck(2)
            s5stage = A.get(F32, [128, 3, 8, 128])
            s5t = [A.get(F32, [128, 512]) for _ in range(6)]
            for l in range(L):
                S.dma("sp", s5stage[:, 0], s5row[:, l, 0])
                S.dma("sp", s5stage[:, 1], s5row[:, l, 1])
                S.dma("sp", s5stage[:, 2], s5row[:, l, 2])
                lre, lim, ldt = s5stage[:, 0], s5stage[:, 1], s5stage[:, 2]
                act(ldt, ldt, AF.Exp)
                for hq in range(2):
                    sl = slice(hq * 4, hq * 4 + 4)
                    a_re = lre[:, sl, :].rearrange("p a b -> p (a b)")
                    a_im = lim[:, sl, :].rearrange("p a b -> p (a b)")
                    a_dt = ldt[:, sl, :].rearrange("p a b -> p (a b)")
                    w0, w1, w2, w3, w4, w5 = s5t
                    tt("dve", w0, a_im, a_dt, ALU.mult)
                    range_reduce(w1, w0, t_i, t_f)
                    act(w2, w1, AF.Sin)
                    ts("dve", w0, w1, math.pi / 2, ALU.add)
                    range_reduce(w1, w0, t_i, t_f)
                    act(w3, w1, AF.Sin)
                    tt("dve", w0, a_re, a_dt, ALU.mult)
                    act(w0, w0, AF.Exp)
                    tt("dve", w3, w3, w0, ALU.mult)
                    ts("dve", w3, w3, -1.0, ALU.add)
                    tt("dve", w2, w2, w0, ALU.mult)
                    tt("dve", w0, a_re, a_re, ALU.mult)
                    tt("dve", w1, a_im, a_im, ALU.mult)
                    tt("dve", w0, w0, w1, ALU.add)
                    recip(w0, w0)
                    tt("dve", w1, w3, a_re, ALU.mult)
                    tt("dve", w4, w2, a_im, ALU.mult)
                    tt("dve", w1, w1, w4, ALU.add)
                    tt("dve", w1, w1, w0, ALU.mult)
                    tt("dve", w4, w2, a_re, ALU.mult)
                    tt("dve", w5, w3, a_im, ALU.mult)
                    tt("dve", w4, w4, w5, ALU.subtract)
                    tt("dve", w4, w4, w0, ALU.mult)
                    S.dma("sp", w2.rearrange("p (a b) -> p a b", a=4), s5bt[l, 0][:, sl, :])
                    S.dma("sp", w3.rearrange("p (a b) -> p a b", a=4), s5bt[l, 1][:, sl, :])
                    tt("dve", w0, w1, w2, ALU.mult)
                    tt("dve", w5, w4, w3, ALU.mult)
                    tt("dve", s5m[:, 0, sl, :].rearrange("p a b -> p (a b)"), w0, w5, ALU.subtract)
                    tt("dve", w0, w1, w3, ALU.mult)
                    tt("dve", w5, w4, w2, ALU.mult)
                    tt("dve", s5m[:, 1, sl, :].rearrange("p a b -> p (a b)"), w0, w5, ALU.add)
                    S.dma("sp", w2.rearrange("p (a b) -> p a b", a=4), s5ct[l, 0][:, sl, :])
                    S.dma("sp", w3.rearrange("p (a b) -> p a b", a=4), s5ct[l, 1][:, sl, :])
                    cp("dve", s5m[:, 2, sl, :].rearrange("p a b -> p (a b)"), w2)
                    ts("dve", s5m[:, 3, sl, :].rearrange("p a b -> p (a b)"), w3, -1.0, ALU.mult)

                S.dma("sp", s5mscr[l], s5m)
            ck(3)
            dgst = A.get(BF16, [128, 12, 4, 128])
            dcst = A.get(BF16, [128, 2, 31, 128])
            for l in range(L):
                for c in range(12):
                    for j in range(4):
                        act(dgst[:, c, j, :], identb, AF.Copy, scale=cwg_s[:, l, c, j:j + 1])
                S.dma("sp", dgscr[l], dgst)
                for kc in range(2):
                    for j in range(31):
                        ts("dve", dcst[:, kc, j, :], identb, cwc_s[:, l, kc, j:j + 1], ALU.mult)
                S.dma("sp", dcscr[l], dcst)
            A.off = persist_end
            stg = [A.get(F32, [128, CHW]) for _ in range(4)]
            stb = [A.get(BF16, [128, CHW]) for _ in range(2)]
            allc = [(l, j) for l in range(L) for j in range(NCHUNK)]
            NST = len(stg)
            for k in range(min(NST - 1, len(allc))):
                S.dma("sp", stg[k % NST], wch[allc[k][0], allc[k][1]])
            for k, (l, j) in enumerate(allc):
                kn = k + NST - 1
                if kn < len(allc):
                    S.dma("sp", stg[kn % NST], wch[allc[kn][0], allc[kn][1]])
                b = stb[k % 2]
                cp("dve" if k % 2 == 0 else "act", b, stg[k % NST])
                S.dma("sp", wscr[l, j], b)
            S.barrier()
            ck(4)
            A.off = persist_end

            wseq = []
            tiles = []
            for i in range(NPT):
                tiles.append(dict(tok0=i * 512, T=512, segs=[(0, 512)], C=128, last=(i == NPT - 1)))
            if with_sample:
                tiles.append(dict(tok0=NPT * 512, T=64, segs=[(1, 32), (2, 32)], C=32, last=True))
            for ti in range(len(tiles)):
                for l in range(L):
                    for j in range(NCHUNK):
                        wseq.append((l, j))
            wstate = dict(issued=0, used=0, released=0)

            def wpump():
                while wstate["issued"] < len(wseq) and wstate["issued"] - NSLOT < wstate["released"]:
                    m = wstate["issued"]
                    l_, j_ = wseq[m]
                    S.dma("sp", wslot[m % NSLOT], wscr[l_, j_])
                    wstate["issued"] += 1

            def wget():
                n = wstate["used"]
                wstate["used"] += 1
                wpump()
                assert wstate["issued"] > n
                return wslot[n % NSLOT]

            def wdone(k=1):
                wstate["released"] += k
                wpump()

            gluw = A.get(BF16, [128, 2, 256])
            sqb = A.get(BF16, [128, 512])
            sdt = A.get(F32, [128, 512])
            rst = A.get(F32, [128, 512])
            t512 = A.get(F32, [128, 512])
            sgp = A.get(F32, [128, 512])
            u5f = A.get(F32, [128, 2, 512])
            u5b = A.get(BF16, [128, 2, 512])
            selh = A.get(F32, [8, 2, 128])
            tabq = [A.get(F32, [128, 2, 512]) for _ in range(2)]
            s5t = [A.get(F32, [128, 512]) for _ in range(6)]
            xbre = A.get(BF16, [128, 512])
            xbim = A.get(BF16, [128, 512])
            ygf = u5f
            ygb = u5b
            ov0 = A.off
            xpb = A.get(BF16, [128, 3 * 2 * 520])
            dgc = A.get(BF16, [128, 3, 4, 128])
            qkv = A.get(F32, [128, 3, 512])
            zs = A.get(F32, [128, 512])
            kTb = A.get(BF16, [128, 512])
            kbT = A.get(BF16, [128, 512])
            qTb = A.get(BF16, [128, 512])
            qgT = A.get(BF16, [128, 512])
            betaB = A.get(F32, [128, 512])
            gcB = A.get(F32, [128, 512])
            egB = A.get(F32, [128, 512])
            oT = A.get(F32, [128, 512])
            sig8 = A.get(F32, [8, 512])
            g8 = A.get(F32, [8, 512])
            e8 = A.get(F32, [8, 512])
            cols = A.get(F32, [128, 4, 2])
            negc = A.get(F32, [128, 4])
            ecol = A.get(F32, [128, 4])
            bexp = A.get(F32, [128, 4])
            kdcol = A.get(F32, [128, 4])
            GD = F32 if GDN_F32 else BF16
            kbg = A.get(GD, [128, 4, 128])
            kd = A.get(BF16, [128, 4, 128])
            vbt = A.get(GD, [128, 4, 128])
            Du = A.get(F32, [128, 4, 128])
            EU = A.get(F32, [128, 4, 128])
            EUs = A.get(F32, [128, 4, 128])
            Ub = [A.get(GD, [128, 4, 128]) for _ in range(2)]
            Lb = [A.get(GD, [128, 4, 128]) for _ in range(2)]
            Rb = [A.get(GD, [128, 4, 128]) for _ in range(2)]
            Aqk = A.get(BF16, [128, 4, 128])
            nwT = A.get(GD, [128, 4, 128])
            vnew = A.get(BF16, [128, 128])
            Sbf = A.get(BF16, [128, 128])
            ov_end = A.off
            A.off = ov0
            xcb = A.get(BF16, [128, 2 * 2 * 544])
            dgcc = A.get(BF16, [128, 2, 31, 128])
            xg = A.get(F32, [128, 2, 512])
            xc16 = A.get(BF16, [128, 2, 512])
            ov_end = max(ov_end, A.off)
            A.off = ov0
            act_base = A.get(F32, [128, 22 * 256])
            actb = act_base.bitcast(BF16).rearrange("p (a b) -> p a b", a=22)
            yout = act_base[:, 0:8 * 512].rearrange("p (a b) -> p a b", a=8)
            pTf = A.get(F32, [128, 2, 512])
            pTb = A.get(BF16, [128, 2, 512])
            ov_end = max(ov_end, A.off)
            A.off = ov_end
            print("SBUF words used", A.off, "of", SBW)

            def rmsnorm_to(dst_bf, gcol_fn, T, final_out=None):
                pb = pbank()
                for kc in range(8):
                    act(sqb[:, 0:T], hT[:, kc, 0:T], AF.Square)
                    mm(PS(pb)[:, 0:T], onesb, sqb[:, 0:T], start=(kc == 0), stop=(kc == 7))
                act(sdt[:, 0:T], PS(pb)[:, 0:T], AF.Ln, bias=1e-6, scale=1.0 / D)
                act(rst[:, 0:T], sdt[:, 0:T], AF.Exp, scale=-0.5)
                for kc in range(8):
                    o = dst_bf[:, kc, 0:T] if final_out is None else final_out[:, kc, 0:T]
                    stt("dve", o, hT[:, kc, 0:T], gcol_fn(kc), rst[:, 0:T], ALU.mult, ALU.mult)

            VEC_GMIX, VEC_GFFN, VEC_GN, VEC_S5D, VEC_GLUB, VEC_DWB, VEC_LNG, VEC_LNB = 0, 8, 16, 17, 19, 21, 23, 25

            for tile in tiles:
                T = tile["T"]
                tok0 = tile["tok0"]
                segs = tile["segs"]
                nseg = len(segs)
                Ls = segs[0][1]
                C = tile["C"]
                nb = T // C
                m_lev = int(math.log2(C))
                is_sample = segs[0][0] != 0
                S.dma("sp", hT[:, :, 0:T], xT[:, :, tok0:tok0 + T])

                def seqstate(lst_p, lst_s, l, si):
                    seq = segs[si][0]
                    return lst_p[l] if seq == 0 else lst_s[seq - 1]

                for l in range(L):
                    if is_sample:
                        for si in range(nseg):
                            S.dma("sp", Sst_s[si], st_gdn[l, si])
                            S.dma("sp", ghist_s[si], st_gc[l, si])
                            S.dma("sp", chist_s[si], st_cc[l, si])
                            S.dma("sp", xst_s[si], st_s5[l, si])
                    S.dma("sp", s5m, s5mscr[l])

                    rmsnorm_to(hn, lambda kc: vec[:, l, VEC_GMIX + kc:VEC_GMIX + kc + 1], T)
                    ck(5)

                    w4 = wget()
                    w4a = w4[:, 0:8 * 264].rearrange("p (a b) -> p a b", a=8)
                    cp("dve", gluw.rearrange("p a b -> p (a b)"), w4[:, 8 * 264:8 * 264 + 512])
                    pb_ba = pbank()
                    for kc in range(8):
                        mm(PS(pb_ba)[0:8, 0:T], w4a[:, kc, 256:264], hn[:, kc, 0:T], start=(kc == 0), stop=(kc == 7))
                    pu = [pbank(), pbank()]
                    for mc in range(2):
                        for kc in range(8):
                            mm(PS(pu[mc])[:, 0:T], w4a[:, kc, mc * 128:(mc + 1) * 128], hn[:, kc, 0:T],
                               start=(kc == 0), stop=(kc == 7))
                    wdone(1)
                    ck(51)
                    act(sig8[:, 0:T], PS(pb_ba)[0:8, 0:T], AF.Sigmoid)
                    ck(52)
                    act(e8[:, 0:T], PS(pb_ba)[0:8, 0:T], AF.Exp, bias=ba8_s[:, l, 0:1])
                    ck(53)
                    act(e8[:, 0:T], e8[:, 0:T], AF.Ln, bias=1.0)
                    ck(54)
                    ts("dve", g8[:, 0:T], e8[:, 0:T], na8[:, l:l + 1], ALU.mult)
                    ck(55)
                    for mc in range(2):
                        cp("act", u5f[:, mc, 0:T], PS(pu[mc])[:, 0:T])
                        ck(56 + mc * 2)
                        cp("dve", u5b[:, mc, 0:T], PS(pu[mc])[:, 0:T])
                        ck(57 + mc * 2)
                    ck(6)

                    def s5_gen(l=l, T=T, Ls=Ls, nseg=nseg):
                        py = [pbank(pin=True), pbank(pin=True)]
                        for q in range(8):
                            kc = q // 4
                            tb = tabq[q % 2]
                            S.dma("sp", tb[:, :, 0:Ls], s5tab[l, q][:, :, 0:Ls])
                            p_re = pbank(pin=True)
                            p_im = pbank(pin=True)
                            mm(PS(p_re)[:, 0:T], s5m[:, 0, q, :], u5b[:, kc, 0:T])
                            mm(PS(p_im)[:, 0:T], s5m[:, 1, q, :], u5b[:, kc, 0:T])
                            yield
                            w0, w1, w2, w3, w4, w5 = s5t
                            for si in range(nseg):
                                cs = slice(si * Ls, (si + 1) * Ls)
                                nC = tb[:, 0, 0:Ls]
                                nS = tb[:, 1, 0:Ls]
                                xs_ = seqstate(xst, xst_s, l, si)
                                tt("dve", w0[:, cs], nC, PS(p_re)[:, cs], ALU.mult)
                                yield
                                tt("dve", w1[:, cs], nS, PS(p_im)[:, cs], ALU.mult)
                                yield
                                tt(S5POOL, w0[:, cs], w0[:, cs], w1[:, cs], ALU.add)
                                yield
                                tt("dve", w2[:, cs], nC, PS(p_im)[:, cs], ALU.mult)
                                yield
                                tt("dve", w3[:, cs], nS, PS(p_re)[:, cs], ALU.mult)
                                yield
                                tt(S5POOL, w2[:, cs], w2[:, cs], w3[:, cs], ALU.subtract)
                                yield
                                scan(w1[:, cs], bc(s5r[:, l, q:q + 1], [128, Ls]), w0[:, cs], xs_[:, q, 0:1])
                                yield
                                scan(w3[:, cs], bc(s5r[:, l, q:q + 1], [128, Ls]), w2[:, cs], xs_[:, q, 1:2])
                                yield
                                tt(S5POOL, w4[:, cs], nC, w1[:, cs], ALU.mult)
                                yield
                                tt(S5POOL, w5[:, cs], nS, w3[:, cs], ALU.mult)
                                yield
                                tt("dve", w0[:, cs], w4[:, cs], w5[:, cs], ALU.subtract)
                                cp("act", xbre[:, cs], w0[:, cs])
                                yield
                                tt(S5POOL, w4[:, cs], nS, w1[:, cs], ALU.mult)
                                yield
                                tt(S5POOL, w5[:, cs], nC, w3[:, cs], ALU.mult)
                                yield
                                tt("dve", w2[:, cs], w4[:, cs], w5[:, cs], ALU.add)
                                cp("act", xbim[:, cs], w2[:, cs])
                                yield
                                e_ = (si + 1) * Ls - 1
                                cp("dve", xs_[:, q, 0:1], w0[:, e_:e_ + 1])
                                cp("dve", xs_[:, q, 1:2], w2[:, e_:e_ + 1])
                                yield
                            pinned.discard(p_re)
                            pinned.discard(p_im)
                            mm(PS(py[kc])[:, 0:T], s5m[:, 2, q, :], xbre[:, 0:T], start=(q % 4 == 0), stop=False)
                            mm(PS(py[kc])[:, 0:T], s5m[:, 3, q, :], xbim[:, 0:T], start=False, stop=(q % 4 == 3))
                            yield
                        for kc in range(2):
                            stt("dve", ygf[:, kc, 0:T], u5f[:, kc, 0:T], vec[:, l, VEC_S5D + kc:VEC_S5D + kc + 1],
                                PS(py[kc])[:, 0:T], ALU.mult, ALU.add)
                            act(ygf[:, kc, 0:T], ygf[:, kc, 0:T], AF.Gelu_apprx_tanh)
                            cp("dve", ygb[:, kc, 0:T], ygf[:, kc, 0:T])
                            yield
                        pinned.discard(py[0])
                        pinned.discard(py[1])
                        for mc in range(2):
                            pb = pbank()
                            for kc in range(2):
                                mm(PS(pb)[:, 0:T], gluw[:, kc, mc * 128:(mc + 1) * 128], ygb[:, kc, 0:T],
                                   start=(kc == 0), stop=(kc == 1))
                            act(s5t[4][:, 0:T], PS(pb)[:, 0:T], AF.Sigmoid, bias=vec[:, l, VEC_GLUB + mc:VEC_GLUB + mc + 1])
                            tt("dve", mix[:, 4 + mc, 0:T], ygf[:, mc, 0:T], s5t[4][:, 0:T], ALU.mult)
                            yield

                    s5g = s5_gen()

                    def pump(n):
                        for _ in range(n):
                            try:
                                next(s5g)
                            except StopIteration:
                                return

                    for h in range(4):
                        wh = wget().rearrange("p (a b) -> p a b", a=8)
                        xv = xpb[:, 0:3 * nseg * (4 + Ls)].rearrange("p (a b c) -> p a b c", a=3, b=nseg)
                        for si in range(nseg):
                            gh = seqstate(ghist, ghist_s, l, si)
                            for c3 in range(3):
                                cp("dve", xv[:, c3, si, 1:4], gh[:, c3 * 4 + h, :])
                        S.dma("sp", dgc, dgscr[l].rearrange("p (a b) j m -> p a b j m", b=4)[:, :, h, :, :])
                        ck(61)
                        pz = None
                        for c4 in range(4):
                            pb = pbank()
                            for kc in range(8):
                                mm(PS(pb)[:, 0:T], wh[:, kc, c4 * 128:(c4 + 1) * 128], hn[:, kc, 0:T],
                                   start=(kc == 0), stop=(kc == 7))
                            if c4 < 3:
                                for si in range(nseg):
                                    gh = seqstate(ghist, ghist_s, l, si)
                                    cp("act", xv[:, c4, si, 4:4 + Ls], PS(pb)[:, si * Ls:(si + 1) * Ls])
                                    cp("dve", gh[:, c4 * 4 + h, :], PS(pb)[:, (si + 1) * Ls - 3:(si + 1) * Ls])
                            else:
                                act(zs[:, 0:T], PS(pb)[:, 0:T], AF.Silu)
                        wdone(1)
                        ck(62)
                        for c3 in range(3):
                            pb = pbank()
                            for si in range(nseg):
                                for j in range(4):
                                    mm(PS(pb)[:, si * Ls:(si + 1) * Ls], dgc[:, c3, j, :], xv[:, c3, si, 1 + j:1 + j + Ls],
                                       start=(j == 0), stop=(j == 3))
                            act(qkv[:, c3, 0:T], PS(pb)[:, 0:T], AF.Silu)
                        ck(63)
                        for c3, dst, sc in ((0, qTb, 128.0 ** -0.5), (1, kTb, 1.0)):
                            act(sqb[:, 0:T], qkv[:, c3, 0:T], AF.Square)
                            pb = pbank()
                            mm(PS(pb)[:, 0:T], onesb, sqb[:, 0:T])
                            act(sdt[:, 0:T], PS(pb)[:, 0:T], AF.Ln, bias=1e-6)
                            act(rst[:, 0:T], sdt[:, 0:T], AF.Exp, scale=-0.5)
                            stt("dve", dst[:, 0:T], qkv[:, c3, 0:T], sc, rst[:, 0:T], ALU.mult, ALU.mult)
                        ck(64)
                        ts("dve", selh[:, 0, :], ones[0:8, :], ident[0:8, h:h + 1], ALU.mult)
                        ts("dve", selh[:, 1, :], ones[0:8, :], ident[0:8, 4 + h:5 + h], ALU.mult)
                        pbb = pbank()
                        mm(PS(pbb)[:, 0:T], selh[:, 0, :], sig8[:, 0:T])
                        pbg = pbank()
                        mm(PS(pbg)[:, 0:T], selh[:, 1, :], g8[:, 0:T])
                        cp("act", betaB[:, 0:T], PS(pbb)[:, 0:T])
                        for b in range(nb):
                            scan(gcB[:, b * C:(b + 1) * C], ones[:, 0:C], PS(pbg)[:, b * C:(b + 1) * C], 0.0)
                        act(egB[:, 0:T], gcB[:, 0:T], AF.Exp)
                        tt("dve", kbT[:, 0:T], kTb[:, 0:T], betaB[:, 0:T], ALU.mult)
                        tt("dve", qgT[:, 0:T], qTb[:, 0:T], egB[:, 0:T], ALU.mult)
                        ck(65)
                        pbc = pbank()
                        pcv = PS(pbc)[:, 0:nb * 2].rearrange("p (a b) -> p a b", a=nb)[0:C]
                        for b in range(nb):
                            mm(pcv[:, b, 0:1], gcB[:, b * C:(b + 1) * C], ident[:, 0:1])
                            mm(pcv[:, b, 1:2], betaB[:, b * C:(b + 1) * C], ident[:, 0:1])
                        cv = cols[0:C, 0:nb, :]
                        cp("dve", cv, pcv)
                        ts("dve", negc[0:C, 0:nb], cv[:, :, 0], -1.0, ALU.mult)
                        act(ecol[0:C, 0:nb], cv[:, :, 0], AF.Exp)
                        tt("dve", bexp[0:C, 0:nb], cv[:, :, 1], ecol[0:C, 0:nb], ALU.mult)
                        for b in range(nb):
                            act(kdcol[0:C, b:b + 1], cv[:, b, 0:1], AF.Exp, bias=gcB[0:C, (b + 1) * C - 1:(b + 1) * C], scale=-1.0)
                        ck(66)
                        pkt = pbank()
                        pk = PS(pkt, dt=BF16)[:, 0:nb * 128].rearrange("p (a b) -> p a b", a=nb)[0:C]
                        for b in range(nb):
                            tr(pk[:, b, :], kTb[:, b * C:(b + 1) * C], identb)
                        pvt = pbank()
                        pv = PS(pvt)[:, 0:nb * 128].rearrange("p (a b) -> p a b", a=nb)[0:C]
                        for b in range(nb):
                            tr(pv[:, b, :], qkv[:, 2, b * C:(b + 1) * C], ident)
                        for b in range(nb):
                            act(kbg[0:C, b, :], pk[:, b, :], AF.Copy, scale=bexp[0:C, b:b + 1])
                            act(kd[0:C, b, :], pk[:, b, :], AF.Copy, scale=kdcol[0:C, b:b + 1])
                            act(vbt[0:C, b, :], pv[:, b, :], AF.Copy, scale=cv[:, b, 1:2])
                        ck(67)
                        pg = pbank()
                        pgv = PS(pg)[:, 0:nb * C].rearrange("p (a b) -> p a b", a=nb)[0:C]
                        pa = pbank()
                        pav = PS(pa)[:, 0:nb * C].rearrange("p (a b) -> p a b", a=nb)[0:C]
                        for b in range(nb):
                            mm(pgv[:, b, :], kTb[:, b * C:(b + 1) * C], kbT[:, b * C:(b + 1) * C])
                            mm(pav[:, b, :], kTb[:, b * C:(b + 1) * C], qTb[:, b * C:(b + 1) * C])
                        Duv = Du[0:C, 0:nb, 0:C]
                        EUv = EU[0:C, 0:nb, 0:C]
                        EUsv = EUs[0:C, 0:nb, 0:C]
                        for b in range(nb):
                            stt("dve", Duv[:, b, :], gcB[0:C, b * C:(b + 1) * C], negc[0:C, b:b + 1], masku[0:C, 0:C],
                                ALU.add, ALU.add)
                        act(EUv, Duv, AF.Exp)
                        for b in range(nb):
                            tt("dve", EUsv[:, b, :], EUv[:, b, :], su[0:C, 0:C], ALU.mult)
                        U0 = Ub[0][0:C, 0:nb, 0:C]
                        tt("dve", U0, pgv, EUsv, ALU.mult)
                        Aqv = Aqk[0:C, 0:nb, 0:C]
                        tt("dve", Aqv, pav, EUv, ALU.mult)
                        ck(68)
                        plt = pbank()
                        idg = ident if GDN_F32 else identb
                        pl = PS(plt, dt=GD)[:, 0:nb * C].rearrange("p (a b) -> p a b", a=nb)[0:C]
                        for b in range(nb):
                            tr(pl[:, b, :], U0[:, b, :], idg[0:C, 0:C])
                        L0 = Lb[0][0:C, 0:nb, 0:C]
                        cp("act", L0, pl)
                        R0 = Rb[0][0:C, 0:nb, 0:C]
                        for b in range(nb):
                            stt("dve", R0[:, b, :], U0[:, b, :], -1.0, idg[0:C, 0:C], ALU.mult, ALU.add)
                        ck(69)
                        cur = 0
                        rcur = 0

                        def r_update(Lfac, rcur):
                            Rp = Rb[rcur][0:C, 0:nb, 0:C]
                            Rn = Rb[1 - rcur][0:C, 0:nb, 0:C]
                            p3 = pbank()
                            p3v = PS(p3)[:, 0:nb * C].rearrange("p (a b) -> p a b", a=nb)[0:C]
                            for b in range(nb):
                                mm(p3v[:, b, :], idg[0:C, 0:C], Rp[:, b, :], start=True, stop=False)
                                mm(p3v[:, b, :], Lfac[:, b, :], Rp[:, b, :], start=False, stop=True)
                            return p3v, Rn

                        for lev in range(1, m_lev):
                            Up = Ub[cur][0:C, 0:nb, 0:C]
                            Lp = Lb[cur][0:C, 0:nb, 0:C]
                            Un = Ub[1 - cur][0:C, 0:nb, 0:C]
                            Ln = Lb[1 - cur][0:C, 0:nb, 0:C]
                            p1 = pbank()
                            p1v = PS(p1)[:, 0:nb * C].rearrange("p (a b) -> p a b", a=nb)[0:C]
                            for b in range(nb):
                                mm(p1v[:, b, :], Up[:, b, :], Lp[:, b, :])
                            if lev < m_lev - 1:
                                p2 = pbank()
                                p2v = PS(p2)[:, 0:nb * C].rearrange("p (a b) -> p a b", a=nb)[0:C]
                                for b in range(nb):
                                    mm(p2v[:, b, :], Lp[:, b, :], Up[:, b, :])
                            if lev >= 2:
                                p3v, Rn = r_update(Lp, rcur)
                            cp("act", Ln, p1v)
                            if lev < m_lev - 1:
                                cp("dve", Un, p2v)
                            if lev >= 2:
                                cp("dve", Rn, p3v)
                                rcur = 1 - rcur
                            cur = 1 - cur
                            pump(PUMP_LEVEL)
                        p3v, Rn = r_update(Lb[cur][0:C, 0:nb, 0:C], rcur)
                        cp("dve", Rn, p3v)
                        rcur = 1 - rcur
                        R = Rb[rcur][0:C, 0:nb, 0:C]
                        ck(70)
                        pw = pbank()
                        pwv = PS(pw)[:, 0:nb * C].rearrange("p (a b) -> p a b", a=nb)
                        for b in range(nb):
                            mm(pwv[:, b, :], kbg[0:C, b, :], R[:, b, :])
                        nwv = nwT[:, 0:nb, 0:C]
                        ts("dve", nwv, pwv, -1.0, ALU.mult)
                        ck(71)
                        for b in range(nb):
                            si = (b * C) // Ls
                            Sm = seqstate(Sst, Sst_s, l, si)[:, h, :]
                            if b == 0 or (b * C) % Ls == 0:
                                cp("act", Sbf, Sm)
                            p_v = pbank()
                            pvn = PS(p_v)[0:C, 0:128]
                            mm(pvn, R[:, b, :], vbt[0:C, b, :], start=True, stop=False)
                            mm(pvn, nwv[:, b, :], Sm if GDN_F32 else Sbf, start=False, stop=True)
                            cp("act", vnew[0:C, :], pvn)
                            p_o = pbank()
                            po = PS(p_o)[:, 0:C]
                            mm(po, Sbf, qgT[:, b * C:(b + 1) * C], start=True, stop=False)
                            mm(po, vnew[0:C, :], Aqv[:, b, :], start=False, stop=True)
                            cp("dve", oT[:, b * C:(b + 1) * C], po)
                            p_s = pbank()
                            psn = PS(p_s)[:, 0:128]
                            mm(psn, kd[0:C, b, :], vnew[0:C, :])
                            stt("dve", Sm, Sm, egB[:, (b + 1) * C - 1:(b + 1) * C], psn, ALU.mult, ALU.add)
                            if b + 1 < nb and ((b + 1) * C) % Ls != 0:
                                cp("act", Sbf, Sm)
                            pump(PUMP_SEQ)
                        ck(72)
                        act(sqb[:, 0:T], oT[:, 0:T], AF.Square)
                        pb = pbank()
                        mm(PS(pb)[:, 0:T], onesb, sqb[:, 0:T])
                        act(sdt[:, 0:T], PS(pb)[:, 0:T], AF.Ln, bias=1e-6, scale=1.0 / 128)
                        act(rst[:, 0:T], sdt[:, 0:T], AF.Exp, scale=-0.5)
                        stt("dve", t512[:, 0:T], oT[:, 0:T], vec[:, l, VEC_GN:VEC_GN + 1], rst[:, 0:T], ALU.mult, ALU.mult)
                        tt("dve", mix[:, h, 0:T], t512[:, 0:T], zs[:, 0:T], ALU.mult)
                        ck(7)

                    for si in range(nseg):
                        seq = segs[si][0]
                        if tile["last"]:
                            S.dma("sp", o_gdn[l, seq], seqstate(Sst, Sst_s, l, si))
                            S.dma("sp", o_gc[l, seq], seqstate(ghist, ghist_s, l, si))

                    ck(8)
                    for _ in s5g:
                        pass
                    if tile["last"]:
                        for si in range(nseg):
                            S.dma("sp", o_s5[l, segs[si][0]], seqstate(xst, xst_s, l, si))

                    ck(9)
                    wc = wget().rearrange("p (a b) -> p a b", a=8)
                    xcv = xcb[:, 0:2 * nseg * (30 + Ls)].rearrange("p (a b c) -> p a b c", a=2, b=nseg)
                    S.dma("sp", dgcc, dcscr[l])
                    for si in range(nseg):
                        ch = seqstate(chist, chist_s, l, si)
                        cp("dve", xcv[:, :, si, 0:30], ch)
                    pa_ = [pbank(), pbank()]
                    pg_ = [pbank(), pbank()]
                    for c4 in range(4):
                        pb = pa_[c4] if c4 < 2 else pg_[c4 - 2]
                        for kc in range(8):
                            mm(PS(pb)[:, 0:T], wc[:, kc, c4 * 128:(c4 + 1) * 128], hn[:, kc, 0:T],
                               start=(kc == 0), stop=(kc == 7))
                    wdone(1)
                    for kc in range(2):
                        act(sgp[:, 0:T], PS(pg_[kc])[:, 0:T], AF.Sigmoid)
                        tt("dve", xg[:, kc, 0:T], PS(pa_[kc])[:, 0:T], sgp[:, 0:T], ALU.mult)
                        for si in range(nseg):
                            ch = seqstate(chist, chist_s, l, si)
                            cp("act", xcv[:, kc, si, 30:30 + Ls], xg[:, kc, si * Ls:(si + 1) * Ls])
                            cp("dve", ch[:, kc, :], xg[:, kc, (si + 1) * Ls - 30:(si + 1) * Ls])
                    pcv_ = [pbank(), pbank()]
                    for kc in range(2):
                        for si in range(nseg):
                            for j in range(31):
                                mm(PS(pcv_[kc])[:, si * Ls:(si + 1) * Ls], dgcc[:, kc, j, :], xcv[:, kc, si, j:j + Ls],
                                   start=(j == 0), stop=(j == 30))
                    xc = xg
                    for kc in range(2):
                        act(xc[:, kc, 0:T], PS(pcv_[kc])[:, 0:T], AF.Identity, bias=vec[:, l, VEC_DWB + kc:VEC_DWB + kc + 1])
                        cp("dve", xc16[:, kc, 0:T], xc[:, kc, 0:T])
                    pm = pbank()
                    for kc in range(2):
                        mm(PS(pm)[:, 0:T], onesb, xc16[:, kc, 0:T], start=(kc == 0), stop=(kc == 1))
                    for kc in range(2):
                        stt("dve", xc[:, kc, 0:T], PS(pm)[:, 0:T], -1.0 / 256, xc[:, kc, 0:T], ALU.mult, ALU.add)
                        act(xc16[:, kc, 0:T], xc[:, kc, 0:T], AF.Square)
                    pv_ = pbank()
                    for kc in range(2):
                        mm(PS(pv_)[:, 0:T], onesb, xc16[:, kc, 0:T], start=(kc == 0), stop=(kc == 1))
                    act(sdt[:, 0:T], PS(pv_)[:, 0:T], AF.Ln, bias=1e-5, scale=1.0 / 256)
                    act(rst[:, 0:T], sdt[:, 0:T], AF.Exp, scale=-0.5)
                    for kc in range(2):
                        stt("dve", t512[:, 0:T], xc[:, kc, 0:T], vec[:, l, VEC_LNG + kc:VEC_LNG + kc + 1], rst[:, 0:T],
                            ALU.mult, ALU.mult)
                        act(mix[:, 6 + kc, 0:T], t512[:, 0:T], AF.Silu, bias=vec[:, l, VEC_LNB + kc:VEC_LNB + kc + 1])
                    if tile["last"]:
                        for si in range(nseg):
                            S.dma("sp", o_cc[l, segs[si][0]], seqstate(chist, chist_s, l, si))

                    ck(10)
                    for half in range(2):
                        wo = wget().rearrange("p (a b) -> p a b", a=8)
                        for m4 in range(4):
                            mc = half * 4 + m4
                            pb = pbank()
                            for kc in range(8):
                                mm(PS(pb)[:, 0:T], wo[:, kc, m4 * 128:(m4 + 1) * 128], mix[:, kc, 0:T],
                                   start=(kc == 0), stop=(kc == 7))
                            tt("dve", hT[:, mc, 0:T], hT[:, mc, 0:T], PS(pb)[:, 0:T], ALU.add)
                        wdone(1)

                    ck(11)
                    rmsnorm_to(hn, lambda kc: vec[:, l, VEC_GFFN + kc:VEC_GFFN + kc + 1], T)
                    for j in range(11):
                        wf = wget().rearrange("p (a b) -> p a b", a=8)
                        for c2 in range(2):
                            hc = j * 2 + c2
                            p1 = pbank()
                            p3 = pbank()
                            for kc in range(8):
                                mm(PS(p1)[:, 0:T], wf[:, kc, c2 * 128:(c2 + 1) * 128], hn[:, kc, 0:T],
                                   start=(kc == 0), stop=(kc == 7))
                            for kc in range(8):
                                mm(PS(p3)[:, 0:T], wf[:, kc, 256 + c2 * 128:256 + (c2 + 1) * 128], hn[:, kc, 0:T],
                                   start=(kc == 0), stop=(kc == 7))
                            act(sgp[:, 0:T], PS(p1)[:, 0:T], AF.Silu)
                            tt("dve", actb[:, hc, 0:T], sgp[:, 0:T], PS(p3)[:, 0:T], ALU.mult)
                        wdone(1)
                    for mc in range(8):
                        w2c = wget()[:, 0:22 * 128].rearrange("p (a b) -> p a b", a=22)
                        pb = pbank()
                        for kc in range(22):
                            mm(PS(pb)[:, 0:T], w2c[:, kc, :], actb[:, kc, 0:T], start=(kc == 0), stop=(kc == 21))
                        tt("dve", hT[:, mc, 0:T], hT[:, mc, 0:T], PS(pb)[:, 0:T], ALU.add)
                        wdone(1)

                    ck(12)
                    S.dma("sp", pTf[:, :, 0:T], pT[l][:, :, tok0:tok0 + T])
                    for kc in range(8):
                        cp("act", hn[:, kc, 0:T], hT[:, kc, 0:T])
                    cp("dve", pTb[:, :, 0:T], pTf[:, :, 0:T])
                    wg0 = wget().rearrange("p (a b) -> p a b", a=8)
                    wpe = wget()[:, 0:2048].rearrange("p (a b) -> p a b", a=2)
                    wg1 = None
                    for mc in range(8):
                        if mc == 4:
                            wdone(1)
                            wg1 = wget().rearrange("p (a b) -> p a b", a=8)
                        wg = wg0 if mc < 4 else wg1
                        m4 = mc % 4
                        pb = pbank()
                        for kc in range(8):
                            mm(PS(pb)[:, 0:T], wg[:, kc, m4 * 128:(m4 + 1) * 128], hn[:, kc, 0:T],
                               start=(kc == 0), stop=(kc == 7))
                        act(sgp[:, 0:T], PS(pb)[:, 0:T], AF.Sigmoid)
                        pb2 = pbank()
                        for kc in range(2):
                            mm(PS(pb2)[:, 0:T], wpe[:, kc, mc * 128:(mc + 1) * 128], pTb[:, kc, 0:T],
                               start=(kc == 0), stop=(kc == 1))
                        tt("dve", t512[:, 0:T], PS(pb2)[:, 0:T], sgp[:, 0:T], ALU.mult)
                        tt("dve", hT[:, mc, 0:T], hT[:, mc, 0:T], t512[:, 0:T], ALU.add)
                    wdone(2)
                    ck(13)

                rmsnorm_to(None, lambda kc: gf[:, kc:kc + 1], T, final_out=yout)
                S.dma("sp", yT[:, :, tok0:tok0 + T], yout[:, :, 0:T])

        except _Stop:
            print('STOPPED at', KSTOP)
            KT = int(os.environ.get("KTAIL", "0"))
            if KT == 1:
                memset("dve", t512, 0.0)
            elif KT == 2:
                cp("act", t512, sdt)
            elif KT == 3:
                mm(PS(7), onesb, sqb)
                cp("dve", t512, PS(7))
            elif KT == 4:
                mm(PS(7), onesb, sqb)
                cp("act", t512, PS(7))
        S.finish()
        S.finalize()
        print("ops recorded:", S.nops, {k: len(v) for k, v in S.prog.items()}, "signals", S.nsig)

        @block.tensor
        def _(e):
            S.replay("pe", e)

        @block.scalar
        def _(e):
            S.replay("act", e)

        @block.vector
        def _(e):
            S.replay("dve", e)

        @block.gpsimd
        def _(e):
            S.replay("pool", e)

        @block.sync
        def _(e):
            S.replay("sp", e)
    return nc


def _consts():
    c = np.zeros((128, NCONST), np.float32)
    c[:, C_ID:C_ID + 128] = np.eye(128)
    s = np.arange(128)[:, None]
    cc = np.arange(128)[None, :]
    c[:, C_MU:C_MU + 128] = np.where(s <= cc, 0.0, -30000.0)
    c[:, C_SU:C_SU + 128] = (s < cc).astype(np.float32)
    c[:, C_IOTA:C_IOTA + 512] = np.arange(1, 513)[None, :]
    c[:, C_ONES:C_ONES + 128] = 1.0
    return c


def _fm(v):
    C = v.shape[-1]
    r = v.reshape(v.shape[:-1] + (C // 128, 128))
    return np.moveaxis(r, -1, 0)


def _pack_weights(inp, L):
    w = np.zeros((L, NCHUNK, 128, CHW), np.float32)

    def kc_layout(mat):
        K, N = mat.shape
        return mat.reshape(K // 128, 128, N).transpose(1, 0, 2)

    for l in range(L):
        win = inp["w_in"][l]
        c0 = np.concatenate([win[:, 2056:2312], win[:, 2048:2056]], axis=1)
        a = kc_layout(c0).reshape(128, -1)
        g = kc_layout(inp["s5_glu_w"][l]).reshape(128, -1)
        w[l, 0, :, 0:a.shape[1]] = a
        w[l, 0, :, a.shape[1]:a.shape[1] + g.shape[1]] = g
        for h in range(4):
            cols = np.concatenate([win[:, 128 * h:128 * h + 128], win[:, 512 + 128 * h:512 + 128 * h + 128],
                                   win[:, 1024 + 128 * h:1024 + 128 * h + 128],
                                   win[:, 1536 + 128 * h:1536 + 128 * h + 128]], axis=1)
            w[l, 1 + h] = kc_layout(cols).reshape(128, -1)
        w[l, 5] = kc_layout(win[:, 2312:2824]).reshape(128, -1)
        wo = inp["w_out"][l]
        w[l, 6] = kc_layout(wo[:, 0:512]).reshape(128, -1)
        w[l, 7] = kc_layout(wo[:, 512:1024]).reshape(128, -1)
        for j in range(11):
            cols = np.concatenate([inp["ffn_w1"][l][:, 256 * j:256 * j + 256], inp["ffn_w3"][l][:, 256 * j:256 * j + 256]], axis=1)
            w[l, 8 + j] = kc_layout(cols).reshape(128, -1)
        for mc in range(8):
            a = kc_layout(inp["ffn_w2"][l][:, 128 * mc:128 * mc + 128]).reshape(128, -1)
            w[l, 19 + mc, :, 0:a.shape[1]] = a
        wg = inp["pe_gate_w"][l]
        w[l, 27] = kc_layout(wg[:, 0:512]).reshape(128, -1)
        w[l, 29] = kc_layout(wg[:, 512:1024]).reshape(128, -1)
        a = kc_layout(inp["pe_w"][l]).reshape(128, -1)
        w[l, 28, :, 0:a.shape[1]] = a
    return w


def _prep_shared(inp, L):
    sh = {}
    sh["wch"] = _pack_weights(inp, L)
    sh["consts"] = _consts()
    vec = np.zeros((128, L, 40), np.float32)
    for l in range(L):
        vec[:, l, 0:8] = _fm(inp["norm_mix"][l])
        vec[:, l, 8:16] = _fm(inp["norm_ffn"][l])
        vec[:, l, 16] = inp["gdn_norm"][l]
        vec[:, l, 17:19] = _fm(inp["s5_d"][l])
        vec[:, l, 19:21] = _fm(inp["s5_glu_b"][l])
        vec[:, l, 21:23] = _fm(inp["cc_dw_b"][l])
        vec[:, l, 23:25] = _fm(inp["cc_ln_g"][l])
        vec[:, l, 25:27] = _fm(inp["cc_ln_b"][l])
    sh["vecs"] = vec
    sh["gfin"] = np.ascontiguousarray(_fm(inp["norm_final"]))
    cw = inp["gdn_conv_w"][:L]
    sh["cwg"] = np.ascontiguousarray(cw.reshape(L, 4, 12, 128).transpose(3, 0, 2, 1))
    cc = inp["cc_dw_w"][:L]
    sh["cwc"] = np.ascontiguousarray(cc.reshape(L, 31, 2, 128).transpose(3, 0, 2, 1))
    ba8 = np.zeros((8, L, 2), np.float32)
    for l in range(L):
        ba8[4:8, l, 0] = inp["gdn_dt_bias"][l]
        ba8[4:8, l, 1] = inp["gdn_a_log"][l]
    sh["ba8"] = ba8
    s5col = np.zeros((128, L, 8, 3), np.float32)
    s5row = np.zeros((128, L, 3, 8, 128), np.float32)
    s5bt = np.zeros((L, 2, 128, 8, 128), np.float32)
    s5ct = np.zeros((L, 2, 128, 8, 128), np.float32)
    for l in range(L):
        for g in range(16):
            q, g2 = g // 2, g % 2
            ps = slice(g2 * 64, g2 * 64 + 64)
            s5col[ps, l, q, 0] = inp["s5_lam_re"][l, g]
            s5col[ps, l, q, 1] = inp["s5_lam_im"][l, g]
            s5col[ps, l, q, 2] = inp["s5_log_dt"][l, g]
            s5row[:, l, 0, q, ps] = inp["s5_lam_re"][l, g][None, :]
            s5row[:, l, 1, q, ps] = inp["s5_lam_im"][l, g][None, :]
            s5row[:, l, 2, q, ps] = inp["s5_log_dt"][l, g]
            r0 = (g % 8) * 16
            s5bt[l, 0, r0:r0 + 16, q, ps] = inp["s5_b_re"][l, g].T
            s5bt[l, 1, r0:r0 + 16, q, ps] = inp["s5_b_im"][l, g].T
            s5ct[l, 0, ps, q, r0:r0 + 16] = inp["s5_c_re"][l, g].T
            s5ct[l, 1, ps, q, r0:r0 + 16] = inp["s5_c_im"][l, g].T
    sh["s5col"] = s5col
    sh["s5row"] = s5row
    sh["s5bt"] = s5bt
    sh["s5ct"] = s5ct
    return sh


def _prep_core(inp, b, NPT, L):
    m = {}
    xp = inp["x_prompt"][b, :NPT * 512]
    xs = inp["x_sample"][2 * b:2 * b + 2].reshape(64, D)
    x = np.concatenate([xp, xs], axis=0)
    m["xT"] = np.ascontiguousarray(x.reshape(-1, 8, 128).transpose(2, 1, 0))
    pp = inp["p_prompt"][:L, b, :NPT * 512]
    ps = inp["p_sample"][:L, 2 * b:2 * b + 2].reshape(L, 64, 256)
    p = np.concatenate([pp, ps], axis=1)
    m["pT"] = np.ascontiguousarray(p.reshape(L, -1, 2, 128).transpose(0, 3, 2, 1))
    sg = inp["state_gdn"][:L, 2 * b:2 * b + 2]
    m["st_gdn"] = np.ascontiguousarray(sg.transpose(0, 1, 3, 2, 4))
    gc = inp["state_gdn_conv"][:L, 2 * b:2 * b + 2]
    m["st_gc"] = np.ascontiguousarray(gc.reshape(L, 2, 3, 12, 128).transpose(0, 1, 4, 3, 2))
    s5 = inp["state_s5"][:L, 2 * b:2 * b + 2]
    m["st_s5"] = np.ascontiguousarray(s5.reshape(L, 2, 8, 2, 64, 2).transpose(0, 1, 3, 4, 2, 5).reshape(L, 2, 128, 8, 2))
    cc = inp["state_conv"][:L, 2 * b:2 * b + 2]
    m["st_cc"] = np.ascontiguousarray(cc.reshape(L, 2, 30, 2, 128).transpose(0, 1, 4, 3, 2))
    return m


_PROG_CACHE = {}


def run_cores(inp, NPT, L, cores):
    key = (NPT, L)
    if key not in _PROG_CACHE:
        _PROG_CACHE[key] = build_program(NPT, L)
    nc = _PROG_CACHE[key]
    sh = _prep_shared(inp, L)
    in_maps = []
    for b in cores:
        m = dict(sh)
        m.update(_prep_core(inp, b, NPT, L))
        in_maps.append(m)
    res = run_bass_kernel_spmd(nc, in_maps, core_ids=list(range(len(cores))))
    return res


def assemble(res, NPT, L, ncores):
    B = ncores
    y_p = np.zeros((B, NPT * 512, D), np.float32)
    y_s = np.zeros((2 * B, DSEQ, D), np.float32)
    gdn = np.zeros((L, 3 * B, 4, 128, 128), np.float32)
    gcv = np.zeros((L, 3 * B, 3, 1536), np.float32)
    s5 = np.zeros((L, 3 * B, 16, 64, 2), np.float32)
    ccv = np.zeros((L, 3 * B, 30, 256), np.float32)
    for b in range(B):
        r = res.results[b]
        yT = r["yT"]
        y = yT.transpose(2, 1, 0).reshape(-1, D)
        y_p[b] = y[:NPT * 512]
        y_s[2 * b:2 * b + 2] = y[NPT * 512:].reshape(2, DSEQ, D)
        og = r["o_gdn"]
        gdn[:, 3 * b:3 * b + 3] = og.transpose(0, 1, 3, 2, 4)
        oc = r["o_gc"]
        gcv[:, 3 * b:3 * b + 3] = oc.transpose(0, 1, 4, 3, 2).reshape(L, 3, 3, 1536)
        o5 = r["o_s5"]
        s5[:, 3 * b:3 * b + 3] = o5.reshape(L, 3, 2, 64, 8, 2).transpose(0, 1, 4, 2, 3, 5).reshape(L, 3, 16, 64, 2)
        o3 = r["o_cc"]
        ccv[:, 3 * b:3 * b + 3] = o3.transpose(0, 1, 4, 3, 2).reshape(L, 3, 30, 256)
    pi = [3 * b for b in range(B)]
    si = [3 * b + 1 + k for b in range(B) for k in range(2)]
    return (y_p, y_s, gdn[:, pi], gcv[:, pi], s5[:, pi], ccv[:, pi],
            gdn[:, si], gcv[:, si], s5[:, si], ccv[:, si])


def kernel(**inputs):
    inp = {k: np.asarray(v) for k, v in inputs.items()}
    res = run_cores(inp, SEQ // 512, L_FULL, list(range(8)))
    return assemble(res, SEQ // 512, L_FULL, 8)
```
